# Optimizing a Trainium2 kernel written in Bass

```python
import jax
import jax.numpy as jnp
from jax import lax
import numpy as np

D_MODEL = 1024
BATCH = 2
SEQ = 8192
DEPTH = 2
DEC_BATCH = 32
DEC_SEQ = 4
PAST_LEN = 8192
PAGE_SIZE = 128

CHUNK = 128
A_GROUPS = 4
A_GROUP_DIM = 128
A_WIDTH = A_GROUPS * A_GROUP_DIM
HEAD_DIM = 64
KV_HEADS = 4
DIL_PAIRS = ((128, 1), (512, 4), (2048, 16))
N_DIL = len(DIL_PAIRS)
Q_HEADS = N_DIL * KV_HEADS
Q_W = Q_HEADS * HEAD_DIM
KV_W = KV_HEADS * HEAD_DIM
BAND = 128
MAX_WINDOW = 2048
C_WIDTH = 512
CONV_W = 3
D_FF = -(-(8 * D_MODEL) // (3 * 256)) * 256
EPS = 1e-6

SPLIT_SIZES = (A_WIDTH, A_WIDTH, Q_W, KV_W, KV_W, C_WIDTH, C_WIDTH, C_WIDTH, D_MODEL, D_MODEL, D_MODEL)
SPLIT_IDX = tuple(sum(SPLIT_SIZES[:i + 1]) for i in range(len(SPLIT_SIZES) - 1))
IN_W = sum(SPLIT_SIZES)

kernel_name = 'hybrid_gated_gmlp_dilated_conv_decoder_step'


def rms_norm(x, g):
    xf = x.astype(jnp.float32)
    y = xf * lax.rsqrt(jnp.mean(xf * xf, axis=-1, keepdims=True) + EPS)
    return (y * g).astype(x.dtype)


def layer_norm(x, g, b):
    xf = x.astype(jnp.float32)
    mu = jnp.mean(xf, axis=-1, keepdims=True)
    xc = xf - mu
    y = xc * lax.rsqrt(jnp.mean(xc * xc, axis=-1, keepdims=True) + EPS)
    return (y * g + b).astype(x.dtype)


def alibi_slopes():
    s = [2.0 ** (-8.0 * (i + 1) / Q_HEADS) for i in range(Q_HEADS)]
    return jnp.asarray(s, jnp.float32).reshape(N_DIL, KV_HEADS)


def split_proj(h, w_in):
    z = jnp.einsum('bsd,de->bse', h, w_in)
    return jnp.split(z, SPLIT_IDX, axis=-1)


def gmlp_inputs(u_raw, v_raw, ln_g, ln_b):
    return jax.nn.gelu(u_raw), layer_norm(jax.nn.gelu(v_raw), ln_g, ln_b)


def spatial_gate(v, w_s, b_s):
    L = v.shape[-3]
    w = jnp.tril(w_s[:, :L, :L])
    return jnp.einsum('gts,...sgc->...tgc', w, v) + jnp.transpose(b_s[:, :L])[:, :, None]


def banded_group(q, k, v, slopes, dil):
    Bn, S, H, D = q.shape
    span = dil * BAND
    Sp = -(-S // span) * span
    J = Sp // dil
    nb = J // BAND

    def to_blocks(a):
        a = jnp.pad(a, ((0, 0), (0, Sp - S), (0, 0), (0, 0)))
        a = a.reshape(Bn, J, dil, H, D).transpose(0, 2, 1, 3, 4)
        return a.reshape(Bn, dil, nb, BAND, H, D)

    def with_prev(a):
        prev = jnp.pad(a, ((0, 0), (0, 0), (1, 0), (0, 0), (0, 0), (0, 0)))[:, :, :-1]
        return jnp.concatenate([prev, a], axis=3)

    qb = to_blocks(q)
    kb = with_prev(to_blocks(k))
    vb = with_prev(to_blocks(v))
    s = jnp.einsum('brnqhd,brnkhd->brnhqk', qb, kb, preferred_element_type=jnp.float32) * (HEAD_DIM ** -0.5)
    dist = (jnp.arange(BAND)[:, None] + BAND) - jnp.arange(2 * BAND)[None, :]
    valid = ((dist >= 0) & (dist <= BAND))[None] & (
        (jnp.arange(nb)[:, None, None] > 0) | (jnp.arange(2 * BAND)[None, None, :] >= BAND))
    s = s - (slopes * dil)[:, None, None] * dist.astype(jnp.float32)
    s = jnp.where(valid[None, None, :, None], s, -jnp.inf)
    m = jnp.max(s, axis=-1, keepdims=True)
    p = jnp.exp(s - m)
    den = jnp.sum(p, axis=-1)
    o = jnp.einsum('brnhqk,brnkhd->brnqhd', p, vb) / jnp.transpose(den, (0, 1, 2, 4, 3))[..., None]
    lse = m[..., 0] + jnp.log(den)
    o = o.reshape(Bn, dil, J, H, D).transpose(0, 2, 1, 3, 4).reshape(Bn, Sp, H, D)[:, :S]
    lse = jnp.transpose(lse, (0, 1, 2, 4, 3)).reshape(Bn, dil, J, H).transpose(0, 2, 1, 3).reshape(Bn, Sp, H)[:, :S]
    return o, lse


def combine_groups(outs, lses):
    w = jax.nn.softmax(jnp.stack(lses), axis=0)
    return jnp.sum(w[..., None] * jnp.stack(outs), axis=0)


def dilated_attn_prompt(q, k, v):
    slopes = alibi_slopes()
    outs, lses = [], []
    for g, (_, dil) in enumerate(DIL_PAIRS):
        o, l = banded_group(q[:, :, g], k, v, slopes[g], dil)
        outs.append(o)
        lses.append(l)
    return combine_groups(outs, lses).astype(v.dtype)


def dilated_attn_sample(q, k_all, v_all, L):
    T = q.shape[1]
    slopes = alibi_slopes()
    taps = jnp.arange(BAND + 1)
    outs, lses = [], []
    for g, (_, dil) in enumerate(DIL_PAIRS):
        idx = L + jnp.arange(T)[:, None] - dil * taps[None, :]
        valid = idx >= 0
        idx = jnp.maximum(idx, 0)
        kg = k_all[:, idx]
        vg = v_all[:, idx]
        s = jnp.einsum('bthd,btkhd->bthk', q[:, :, g], kg, preferred_element_type=jnp.float32) * (HEAD_DIM ** -0.5)
        s = s - (slopes[g] * dil)[:, None] * taps.astype(jnp.float32)[None, :]
        s = jnp.where(valid[None, :, None, :], s, -jnp.inf)
        m = jnp.max(s, axis=-1, keepdims=True)
        p = jnp.exp(s - m)
        den = jnp.sum(p, axis=-1)
        outs.append(jnp.einsum('bthk,btkhd->bthd', p, vg) / den[..., None])
        lses.append(m[..., 0] + jnp.log(den))
    return combine_groups(outs, lses).astype(v_all.dtype)


def short_conv(z, prev, w, b):
    T = z.shape[1]
    zp = jnp.concatenate([prev.astype(z.dtype), z], axis=1)
    y = b + sum(w[i] * zp[:, i:i + T] for i in range(CONV_W))
    return y, zp[:, T:]


def merge_branches(out_a, out_b, out_c, ga, gb, gc, b_gate, w_br_a, w_br_b, w_br_c, w_o):
    bga, bgb, bgc = jnp.split(b_gate, 3)
    merged = (jax.nn.sigmoid(ga + bga) * (out_a @ w_br_a)
              + jax.nn.sigmoid(gb + bgb) * (out_b @ w_br_b)
              + jax.nn.sigmoid(gc + bgc) * (out_c @ w_br_c))
    return merged @ w_o


def mixer_prompt(h, w_in, a_ln_g, a_ln_b, a_ws, a_bs, c_conv_w, c_conv_b, w_br_a, w_br_b, w_br_c, b_gate, w_o):
    Bn, S, _ = h.shape
    ua, va, q, k, v, cx, cb, cc, ga, gb, gc = split_proj(h, w_in)
    u, vn = gmlp_inputs(ua, va, a_ln_g, a_ln_b)
    sa = spatial_gate(vn.reshape(Bn, S // CHUNK, CHUNK, A_GROUPS, A_GROUP_DIM), a_ws, a_bs).reshape(Bn, S, A_WIDTH)
    out_a = u * sa
    k = k.reshape(Bn, S, KV_HEADS, HEAD_DIM)
    v = v.reshape(Bn, S, KV_HEADS, HEAD_DIM)
    out_b = dilated_attn_prompt(q.reshape(Bn, S, N_DIL, KV_HEADS, HEAD_DIM), k, v).reshape(Bn, S, KV_W)
    conv, conv_state = short_conv(cc * cx, jnp.zeros((Bn, CONV_W - 1, C_WIDTH), h.dtype), c_conv_w, c_conv_b)
    out_c = cb * conv
    y = merge_branches(out_a, out_b, out_c, ga, gb, gc, b_gate, w_br_a, w_br_b, w_br_c, w_o)
    win = min(MAX_WINDOW, S)
    return y, k[:, S - win:], v[:, S - win:], conv_state


def mixer_sample(h, cache_k, cache_v, conv_prev, w_in, a_ln_g, a_ln_b, a_ws, a_bs, c_conv_w, c_conv_b,
                 w_br_a, w_br_b, w_br_c, b_gate, w_o):
    Bn, T, _ = h.shape
    ua, va, q, k, v, cx, cb, cc, ga, gb, gc = split_proj(h, w_in)
    u, vn = gmlp_inputs(ua, va, a_ln_g, a_ln_b)
    sa = spatial_gate(vn.reshape(Bn, T, A_GROUPS, A_GROUP_DIM), a_ws, a_bs).reshape(Bn, T, A_WIDTH)
    out_a = u * sa
    k = k.reshape(Bn, T, KV_HEADS, HEAD_DIM)
    v = v.reshape(Bn, T, KV_HEADS, HEAD_DIM)
    L = cache_k.shape[1]
    k_all = jnp.concatenate([cache_k.astype(k.dtype), k], axis=1)
    v_all = jnp.concatenate([cache_v.astype(v.dtype), v], axis=1)
    out_b = dilated_attn_sample(q.reshape(Bn, T, N_DIL, KV_HEADS, HEAD_DIM), k_all, v_all, L).reshape(Bn, T, KV_W)
    conv, conv_state = short_conv(cc * cx, conv_prev, c_conv_w, c_conv_b)
    out_c = cb * conv
    y = merge_branches(out_a, out_b, out_c, ga, gb, gc, b_gate, w_br_a, w_br_b, w_br_c, w_o)
    return y, k, v, conv_state, vn


def swiglu(h, wg, wu, wd):
    return (jax.nn.silu(h @ wg) * (h @ wu)) @ wd


def setup_inputs(seed: int = 0) -> dict:
    key = jax.random.key(seed)
    ks = jax.random.split(key, 24)
    f32 = jnp.float32

    def nrm(i, shape, scale):
        return jax.random.normal(ks[i], shape, f32) * scale

    win = min(MAX_WINDOW, PAST_LEN)
    return {
        'x_prompt': nrm(0, (BATCH, SEQ, D_MODEL), 1.0),
        'x_sample': nrm(1, (DEC_BATCH, DEC_SEQ, D_MODEL), 1.0),
        'cache_k_win': nrm(2, (DEPTH, DEC_BATCH, win, KV_HEADS, HEAD_DIM), 1.0),
        'cache_v_win': nrm(3, (DEPTH, DEC_BATCH, win, KV_HEADS, HEAD_DIM), 1.0),
        'state_conv': nrm(4, (DEPTH, DEC_BATCH, CONV_W - 1, C_WIDTH), 1.0),
        'g_pre_mix': 1.0 + nrm(5, (DEPTH, D_MODEL), 0.02),
        'g_post_mix': 1.0 + nrm(6, (DEPTH, D_MODEL), 0.02),
        'g_pre_ffn': 1.0 + nrm(7, (DEPTH, D_MODEL), 0.02),
        'g_post_ffn': 1.0 + nrm(8, (DEPTH, D_MODEL), 0.02),
        'w_in': nrm(9, (DEPTH, D_MODEL, IN_W), D_MODEL ** -0.5),
        'a_ln_g': 1.0 + nrm(10, (DEPTH, A_WIDTH), 0.02),
        'a_ln_b': nrm(11, (DEPTH, A_WIDTH), 0.02),
        'a_ws': nrm(12, (DEPTH, A_GROUPS, CHUNK, CHUNK), CHUNK ** -0.5),
        'a_bs': 1.0 + nrm(13, (DEPTH, A_GROUPS, CHUNK), 0.02),
        'c_conv_w': nrm(14, (DEPTH, CONV_W, C_WIDTH), CONV_W ** -0.5),
        'c_conv_b': nrm(15, (DEPTH, C_WIDTH), 0.02),
        'w_br_a': nrm(16, (DEPTH, A_WIDTH, D_MODEL), A_WIDTH ** -0.5),
        'w_br_b': nrm(17, (DEPTH, KV_W, D_MODEL), KV_W ** -0.5),
        'w_br_c': nrm(18, (DEPTH, C_WIDTH, D_MODEL), C_WIDTH ** -0.5),
        'b_gate': nrm(19, (DEPTH, 3 * D_MODEL), 0.02),
        'w_o': nrm(20, (DEPTH, D_MODEL, D_MODEL), D_MODEL ** -0.5),
        'w_ff_gate': nrm(21, (DEPTH, D_MODEL, D_FF), D_MODEL ** -0.5),
        'w_ff_up': nrm(22, (DEPTH, D_MODEL, D_FF), D_MODEL ** -0.5),
        'w_ff_down': nrm(23, (DEPTH, D_FF, D_MODEL), D_FF ** -0.5),
    }


def reference(x_prompt, x_sample, cache_k_win, cache_v_win, state_conv,
              g_pre_mix, g_post_mix, g_pre_ffn, g_post_ffn, w_in,
              a_ln_g, a_ln_b, a_ws, a_bs, c_conv_w, c_conv_b,
              w_br_a, w_br_b, w_br_c, b_gate, w_o, w_ff_gate, w_ff_up, w_ff_down):
    xp, xs = x_prompt, x_sample
    kp_l, vp_l, ks_l, vs_l, cp_l, cs_l, av_l = [], [], [], [], [], [], []
    for l in range(DEPTH):
        mix_w = (w_in[l], a_ln_g[l], a_ln_b[l], a_ws[l], a_bs[l], c_conv_w[l], c_conv_b[l],
                 w_br_a[l], w_br_b[l], w_br_c[l], b_gate[l], w_o[l])
        mp, kp, vp, cp = mixer_prompt(rms_norm(xp, g_pre_mix[l]), *mix_w)
        ms, kn, vnw, cs, av = mixer_sample(rms_norm(xs, g_pre_mix[l]), cache_k_win[l], cache_v_win[l],
                                           state_conv[l], *mix_w)
        xp = xp + rms_norm(mp, g_post_mix[l])
        xs = xs + rms_norm(ms, g_post_mix[l])
        xp = xp + rms_norm(swiglu(rms_norm(xp, g_pre_ffn[l]), w_ff_gate[l], w_ff_up[l], w_ff_down[l]), g_post_ffn[l])
        xs = xs + rms_norm(swiglu(rms_norm(xs, g_pre_ffn[l]), w_ff_gate[l], w_ff_up[l], w_ff_down[l]), g_post_ffn[l])
        kp_l.append(kp)
        vp_l.append(vp)
        ks_l.append(kn)
        vs_l.append(vnw)
        cp_l.append(cp)
        cs_l.append(cs)
        av_l.append(av)
    return (xp, xs, jnp.stack(kp_l), jnp.stack(vp_l), jnp.stack(ks_l), jnp.stack(vs_l),
            jnp.stack(cp_l), jnp.stack(cs_l), jnp.stack(av_l))
```

```python
import os
import numpy as np
import ml_dtypes
from contextlib import ExitStack
import concourse.bass as bass
import concourse.mybir as mybir
from concourse.bass_utils import run_bass_kernel_spmd

F32 = mybir.dt.float32
BF16 = mybir.dt.bfloat16
AF = mybir.ActivationFunctionType
ALU = mybir.AluOpType
AX = mybir.AxisListType

NCORES = 8
D = 1024
DEPTH = 2
NPR = 2048
NSM = 16
NT = NPR + NSM
INW = 6912
DFF = 2816
NTILES = [(0, 512), (512, 512), (1024, 512), (1536, 512), (2048, 16)]
TTILES = [(i * 128, 128) for i in range(16)] + [(2048, 16)]
DILS = (1, 4, 16)
EPS = 1e-6
XC = 9568
C_UA, C_VA, C_Q, C_K, C_V, C_CX, C_CB, C_CC, C_GA, C_GB, C_GC = 0, 512, 1024, 1792, 2048, 2304, 2816, 3328, 3840, 4864, 5888
SAME_ENG_SYNC = True


KSTOP = int(os.environ.get("KSTOP", "-1"))


class Stop(Exception):
    pass


class Res:
    __slots__ = ("name", "w", "r")

    def __init__(self, name):
        self.name = name
        self.w = None
        self.r = []


class DSem:
    def __init__(self, handle, idx):
        self.h = handle
        self.idx = idx
        self.total = 0


class Prog:
    ENG = ("pe", "act", "dve", "pool", "sp")
    NPOOL = 24

    def __init__(self, nc, es):
        self.nc, self.es = nc, es
        self.sem = {e: es.enter_context(nc.semaphore("s_" + e)) for e in self.ENG}
        self.handles = []
        self.named = {}
        self.reset(None, None)

    def reset(self, cur, eobj):
        self.cur, self.eobj = cur, eobj
        self.cnt = {e: 0 for e in self.ENG}
        self.seen = {}
        self.dsems = []
        self.hidx = 0
        self.pool = None
        self.pool_i = 0
        self.pend = None

    def named_sem(self, name):
        if name not in self.named:
            self.named[name] = self.es.enter_context(self.nc.semaphore(name))
        return self.named[name]

    def new_dsem(self):
        if self.hidx >= len(self.handles):
            self.handles.append(self.es.enter_context(self.nc.semaphore("d%d" % len(self.handles))))
        d = DSem(self.handles[self.hidx], self.hidx)
        self.hidx += 1
        self.dsems.append(d)
        return d

    @staticmethod
    def _flat(rs):
        out = []
        for r in rs:
            if isinstance(r, (list, tuple)):
                out.extend(Prog._flat(r))
            else:
                out.append(r)
        return out

    def _waits(self, eng, reads, writes):
        reads, writes = self._flat(reads), self._flat(writes)
        toks = []
        for r in reads:
            if r.w is not None:
                toks.append(r.w)
        for r in writes:
            if r.w is not None:
                toks.append(r.w)
            toks.extend(r.r)
        need = {}
        for (h, val, key, src) in toks:
            if src == eng and (eng == "pe" or not SAME_ENG_SYNC) and key == "c_" + eng:
                continue
            if need.get(key, (None, 0))[1] < val:
                need[key] = (h, val)
        waits = []
        for key, (h, val) in need.items():
            if self.seen.get((eng, key), 0) >= val:
                continue
            self.seen[(eng, key)] = val
            waits.append((h, val))
        return waits

    def _commit(self, tok, reads, writes):
        reads, writes = self._flat(reads), self._flat(writes)
        for r in reads:
            r.r.append(tok)
        for r in writes:
            r.w = tok
            r.r = []

    def _emit(self, eng, waits, fn, sh, inc):
        if eng != self.cur:
            return
        for h, v in waits:
            self.eobj.wait_ge(h, v)
        if fn is not None:
            ins = fn(self.eobj)
            if inc:
                ins.then_inc(sh, inc)

    def op(self, eng, fn, reads=(), writes=()):
        self.flush()
        waits = self._waits(eng, reads, writes)
        self.cnt[eng] += 1
        tok = (self.sem[eng], self.cnt[eng], "c_" + eng, eng)
        self._emit(eng, waits, fn, self.sem[eng], 1)
        self._commit(tok, reads, writes)
        return tok

    def dma(self, eng, out, in_, reads=(), writes=(), dsem=None, ring=False):
        if not ring:
            self.flush()
        extra = []
        if dsem is None:
            if self.pool is None:
                self.pool = {"sp": [self.new_dsem() for _ in range(self.NPOOL)],
                             "pool": [self.new_dsem() for _ in range(16)],
                             "act": [self.new_dsem() for _ in range(4)]}
                self.pool_i = {"sp": 0, "pool": 0, "act": 0}
            pl = self.pool[eng]
            dsem = pl[self.pool_i[eng] % len(pl)]
            self.pool_i[eng] += 1
            key = "d_%d" % dsem.idx
            if dsem.total > self.seen.get((eng, key), 0):
                self.seen[(eng, key)] = dsem.total
                extra.append((dsem.h, dsem.total))
        waits = extra + self._waits(eng, reads, writes)
        dsem.total += 16
        tok = (dsem.h, dsem.total, "d_%d" % dsem.idx, "dma")
        self._emit(eng, waits, lambda e: e.dma_start(out=out, in_=in_, allow_slow_non_contiguous=True), dsem.h, 16)
        self._commit(tok, reads, writes)
        return tok

    def chain(self, eng, fns, reads=(), writes=()):
        cr = Res("chain")
        tok = None
        for fn in fns:
            tok = self.op(eng, fn, reads=list(reads) + [cr], writes=list(writes) + [cr])
        return tok

    def custom(self, eng, fn, sem_h, inc, key, total, reads=(), writes=()):
        self.flush()
        waits = self._waits(eng, reads, writes)
        tok = (sem_h, total, key, "dma")
        self._emit(eng, waits, fn, sem_h, inc)
        self._commit(tok, reads, writes)
        return tok

    def barrier(self):
        self.flush()
        self.pend = (dict(self.cnt), [(d, d.total) for d in self.dsems])

    def flush(self):
        if self.pend is None:
            return
        cnt, dtot = self.pend
        self.pend = None
        for e in self.ENG:
            waits = []
            for x in self.ENG:
                if x == e:
                    continue
                key = "c_" + x
                if cnt[x] > self.seen.get((e, key), 0):
                    self.seen[(e, key)] = cnt[x]
                    waits.append((self.sem[x], cnt[x]))
            for d, tot in dtot:
                key = "d_%d" % d.idx
                if tot > self.seen.get((e, key), 0):
                    self.seen[(e, key)] = tot
                    waits.append((d.h, tot))
            self._emit(e, waits, None, None, 0)


def alibi_slopes():
    return np.array([2.0 ** (-8.0 * (i + 1) / 12) for i in range(12)], np.float64).reshape(3, 4)


def host_consts():
    bf = ml_dtypes.bfloat16
    c = {}
    c["identb"] = np.eye(128, dtype=np.float32).astype(bf)
    c["identf"] = np.eye(128, dtype=np.float32)
    c["onesb"] = np.ones((128, 128), np.float32).astype(bf)
    c["onesf"] = np.ones((128, 64), np.float32)
    c["tril"] = np.tril(np.ones((128, 128), np.float32))
    sl = alibi_slopes()
    k = np.arange(128)[:, None]
    q = np.arange(128)[None, :]
    ep = np.zeros((128, 3, 4, 2, 128), np.float64)
    for g, d in enumerate(DILS):
        for h in range(4):
            dist0 = q + 128 - k
            ep[:, g, h, 0, :] = np.where(k >= q, np.exp(-sl[g, h] * d * dist0), 0.0)
            dist1 = q - k
            ep[:, g, h, 1, :] = np.where(k <= q, np.exp(-sl[g, h] * d * dist1), 0.0)
    c["ep"] = ep.reshape(128, 3 * 4 * 2 * 128).astype(np.float32).astype(bf)
    es_ = np.zeros((128, 9, 4, 4), np.float64)
    p = np.arange(128)
    for h in range(4):
        for t in range(4):
            tap = 128 + t - p
            es_[:, 0, h, t] = np.where(p >= t, np.exp(-sl[0, h] * 1 * tap), 0.0)
            tap = 128 - p
            es_[:, 1 + t, h, t] = np.exp(-sl[1, h] * 4 * tap)
            es_[:, 5 + t, h, t] = np.exp(-sl[2, h] * 16 * tap)
    c["es"] = es_.reshape(128, 144).astype(np.float32).astype(bf)
    en = np.zeros((16, 3, 4, 16), np.float64)
    for b in range(4):
        for t2 in range(4):
            for t in range(4):
                if t2 > t:
                    continue
                for g, d in enumerate(DILS):
                    if (t - t2) % d != 0 or (t - t2) // d > 128:
                        continue
                    for h in range(4):
                        en[b * 4 + t2, g, h, b * 4 + t] = np.exp(-sl[g, h] * (t - t2))
    c["en"] = en.reshape(16, 192).astype(np.float32).astype(bf)
    return c


def build():
    nc = bass.Bass("TRN2", target_bir_lowering=False)
    es = ExitStack()
    P = Prog(nc, es)

    def din(name, shape, dt=F32):
        return nc.dram_tensor(name, list(shape), dt, kind="ExternalInput")

    def dout(name, shape, dt=F32):
        return nc.dram_tensor(name, list(shape), dt, kind="ExternalOutput")

    xp_d = din("xp", [NPR, D]); xs_d = din("xs", [NSM, D])
    ck_d = din("ck", [DEPTH, 4, 2048, 256]); cv_d = din("cv", [DEPTH, 4, 2048, 256])
    sc_d = din("sc", [DEPTH, 8, 512])
    gpm_d = din("g_pre_mix", [DEPTH, D]); gqm_d = din("g_post_mix", [DEPTH, D])
    gpf_d = din("g_pre_ffn", [DEPTH, D]); gqf_d = din("g_post_ffn", [DEPTH, D])
    win_d = din("w_in", [DEPTH, D, INW])
    alg_d = din("a_ln_g", [DEPTH, 512]); alb_d = din("a_ln_b", [DEPTH, 512])
    aws_d = din("a_ws", [DEPTH, 4, 128, 128]); abs_d = din("a_bs", [DEPTH, 4, 128])
    cw_d = din("c_conv_w", [DEPTH, 3, 512]); cbias_d = din("c_conv_b", [DEPTH, 512])
    wba_d = din("w_br_a", [DEPTH, 512, D]); wbb_d = din("w_br_b", [DEPTH, 256, D]); wbc_d = din("w_br_c", [DEPTH, 512, D])
    bg_d = din("b_gate", [DEPTH, 3 * D]); wo_d = din("w_o", [DEPTH, D, D])
    wg_d = din("w_ff_gate", [DEPTH, D, DFF]); wu_d = din("w_ff_up", [DEPTH, D, DFF]); wd_d = din("w_ff_down", [DEPTH, DFF, D])
    flags_d = din("flags", [1, 2])
    xp1_d = din("xp1", [NPR, D]); xp2_d = din("xp2", [NPR, D])
    identb_d = din("identb", [128, 128], BF16); identf_d = din("identf", [128, 128])
    onesb_d = din("onesb", [128, 128], BF16); onesf_d = din("onesf", [128, 64])
    tril_d = din("tril", [128, 128]); ep_d = din("ep", [128, 3072], BF16)
    es_d = din("es", [128, 144], BF16); en_d = din("en", [16, 192], BF16)

    yp_d = dout("yp", [NPR, D]); ys_d = dout("ys", [NSM, D])
    kv_d = dout("kv", [DEPTH, NT, 512]); zo_d = dout("zo", [DEPTH, 10, 512]); vn_d = dout("vns", [DEPTH, NSM, 512])

    bounce_d = [nc.dram_tensor("bounce%d" % i, [128, XC], BF16) for i in range(4)]
    xpark_ds = [nc.dram_tensor("xpark%d" % i, [128, 5 * 8 * 512], F32) for i in range(3)]

    def sb(name, shape, dt=F32):
        return es.enter_context(nc.sbuf_tensor(name, list(shape), dt))

    identb = sb("identb_s", [128, 128], BF16); identf = sb("identf_s", [128, 128])
    onesb = sb("onesb_s", [128, 128], BF16); onesf = sb("onesf_s", [128, 64])
    tril = sb("tril_s", [128, 128]); ep = sb("ep_s", [128, 3, 4, 2, 128], BF16)
    es_s = sb("es_s", [128, 9, 4, 4], BF16); en_s = sb("en_s", [16, 3, 4, 16], BF16)
    flagb = sb("flagb", [128, 2])
    vecs = sb("vecs", [128, DEPTH, 4, 8])
    bgc = sb("bgc", [128, DEPTH, 24])
    cwc = sb("cwc", [128, DEPTH, 4, 4])
    lng = sb("lng", [128, 512]); lnb = sb("lnb", [128, 512])
    bsrf = sb("bsrf", [1, 4, 128]); bsr = sb("bsr", [1, 4, 128], BF16)
    wst = sb("wst", [128, 4, 128], BF16)
    wss = sb("wss", [16, 4, 16], BF16)
    bss = sb("bss", [1, 4, 16], BF16)
    RING_N = 3
    ring = [sb("ring%d" % i, [128, 4608], BF16) for i in range(RING_N)]
    ring_res = [None] * RING_N
    ring_sem = [None] * RING_N
    ring_i = [0]
    ARN = 83760
    AR = sb("arena", [128, ARN], BF16)

    banks = [es.enter_context(nc.psum_tensor("bank%d" % i, [128, 512], F32)) for i in range(8)]
    bank_res = [None] * 8
    bank_i = [0]

    def bank():
        i = bank_i[0] % 8
        bank_i[0] += 1
        return banks[i], bank_res[i]

    def ring_slot():
        i = ring_i[0] % RING_N
        ring_i[0] += 1
        return i

    def wload(src_ap, slot=None, off=0):
        i = ring_slot() if slot is None else slot
        n = 1
        for s in src_ap.shape[1:]:
            n *= s
        dst = ring[i][0:src_ap.shape[0], off:off + n]
        if len(src_ap.shape) == 3:
            dst = dst.rearrange("p (a b) -> p a b", a=src_ap.shape[1])
        elif len(src_ap.shape) == 4:
            dst = dst.rearrange("p (a b c) -> p a b c", a=src_ap.shape[1], b=src_ap.shape[2])
        P.dma("pool", dst, src_ap, writes=[ring_res[i]], dsem=ring_sem[i], ring=True)
        return dst, ring_res[i], i

    def win_cols(l, c0, n):
        return win_d[l, :, c0:c0 + n].rearrange("(k p) c -> p k c", p=128)

    class Carver:
        def __init__(self, base):
            self.o = base
        def take(self, nelem, dt=BF16, parts=128):
            if dt == F32:
                v = AR[0:parts, self.o:self.o + 2 * nelem].bitcast(F32)
                self.o += 2 * nelem
            else:
                v = AR[0:parts, self.o:self.o + nelem]
                self.o += nelem
            assert self.o <= ARN, self.o
            return v

    xparks = [x.ap().rearrange("p (n k t) -> p n k t", n=5, k=8) for x in xpark_ds]
    XTrs = [[None] * 5 for _ in range(3)]

    def ntile_of(tok):
        return min(tok // 512, 4)

    def program():
      for i in range(RING_N):
          ring_res[i] = Res("ring%d" % i)
          ring_sem[i] = P.new_dsem()
      ring_i[0] = 0
      for i in range(8):
          bank_res[i] = Res("bank%d" % i)
      bank_i[0] = 0
      for q_ in range(3):
          XTrs[q_][:] = [Res("XT%d_%d" % (q_, i)) for i in range(5)]
      try:
          program_body()
      except Stop:
          P.barrier()
      P.flush()

    CI = [0]

    def mark(k):
        if KSTOP == k or KSTOP == CI[0] * 20 + k:
            raise Stop()

    def program_body():
        CI[0] = -10
        cres = Res("consts")
        for dst, src in ((identb, identb_d), (identf, identf_d), (onesb, onesb_d), (onesf, onesf_d), (tril, tril_d)):
            P.dma("sp", dst[:, :], src[:, :], writes=[cres])
        P.dma("sp", ep[:].rearrange("p a b c d -> p (a b c d)"), ep_d[:, :], writes=[cres])
        P.dma("sp", es_s[:].rearrange("p a b c -> p (a b c)"), es_d[:, :], writes=[cres])
        P.dma("sp", en_s[:].rearrange("p a b c -> p (a b c)"), en_d[:, :], writes=[cres])
        P.dma("sp", flagb[:, :], flags_d[0:1, :].partition_broadcast(128), writes=[cres])
        with nc.allow_non_contiguous_dma(reason="tiny per-partition parameter columns"):
            for wi, gd in enumerate((gpm_d, gqm_d, gpf_d, gqf_d)):
                for l in range(DEPTH):
                    P.dma("sp", vecs[:, l, wi, :], gd[l, :].rearrange("(k p) -> p k", p=128), writes=[cres])
            for l in range(DEPTH):
                P.dma("sp", bgc[:, l, :], bg_d[l, :].rearrange("(k p) -> p k", p=128), writes=[cres])
                for i in range(3):
                    P.dma("sp", cwc[:, l, i, :], cw_d[l, i, :].rearrange("(k p) -> p k", p=128), writes=[cres])
                P.dma("sp", cwc[:, l, 3, :], cbias_d[l, :].rearrange("(k p) -> p k", p=128), writes=[cres])
        P.barrier()
        mark(1)

        out_sem = None

        cv0 = Carver(0)
        xst = [cv0.take(1024, F32) for _ in range(2)]
        xst_res = [Res("xst0"), Res("xst1")]
        xt = [cv0.take(4096, F32).rearrange("p (k t) -> p k t", k=8) for _ in range(2)]
        xtr = [Res("xt0"), Res("xt1")]
        for (xsrc_d, xpark, XTr) in ((xp2_d, xparks[2], XTrs[2]), (xp1_d, xparks[1], XTrs[1]), (xp_d, xparks[0], XTrs[0])):
          for ti, (t0, m) in enumerate(TTILES):
            s = ti % 2
            ni = ntile_of(t0)
            xs_ = ni % 2
            lc = t0 - NTILES[ni][0]
            src = xsrc_d[t0:t0 + m, :] if ti < 16 else xs_d[:, :]
            P.dma("sp", xst[s][0:m, :], src, writes=[xst_res[s]])
            for half in range(2):
                bk, br = bank()
                def tr(e, s=s, m=m, half=half, bk=bk):
                    ins = None
                    for j in range(4):
                        k = half * 4 + j
                        ins = e.transpose(bk[:, j * 128:j * 128 + m], xst[s][0:m, k * 128:(k + 1) * 128], identf[0:m, 0:m])
                    return ins
                P.op("pe", tr, reads=[xst_res[s], cres], writes=[br])
                P.op("dve", lambda e, half=half, bk=bk, lc=lc, m=m, xs_=xs_: e.tensor_copy(
                    out=xt[xs_][:, half * 4:half * 4 + 4, lc:lc + m],
                    in_=bk[:, :].rearrange("p (j t) -> p j t", j=4)[:, :, 0:m]), reads=[br], writes=[xtr[xs_]])
            if lc + m == NTILES[ni][1]:
                n = NTILES[ni][1]
                P.dma("sp", xpark[:, ni, :, 0:n], xt[xs_][:, :, 0:n], reads=[xtr[xs_]], writes=[XTr[ni]])
        P.barrier()

        mark(2)
        TWIN = {}

        def norm_stats(src3, srcr, n, sqb, sqr, rs, rsr):
            sqr_b = TWIN.setdefault(id(sqr), Res("sq_twin"))
            P.op("act", lambda e: e.activation(out=sqb[:, 0:4, 0:n], in_=src3[:, 0:4, :], func=AF.Square), reads=[srcr], writes=[sqr])
            P.op("dve", lambda e: e.tensor_tensor(out=sqb[:, 4:8, 0:n], in0=src3[:, 4:8, :], in1=src3[:, 4:8, :], op=ALU.mult), reads=[srcr], writes=[sqr_b])
            bk, br = bank()
            def mm(e):
                ins = None
                for k in range(8):
                    ins = e.matmul(bk[:, 0:n], onesb[:, :], sqb[:, k, 0:n], start=(k == 0), stop=(k == 7))
                return ins
            P.op("pe", mm, reads=[sqr, sqr_b, cres], writes=[br])
            P.op("act", lambda e: e.activation(out=rs[:, 0:n], in_=bk[:, 0:n], func=AF.Sqrt, bias=EPS, scale=1.0 / D),
                 reads=[br], writes=[rsr])
            P.op("dve", lambda e: e.reciprocal(out=rs[:, 0:n], in_=rs[:, 0:n]), reads=[rsr], writes=[rsr])

        CUR = [None, None]

        def rmsnorm_h(gcol, hT, hTr, sqb, sqr, rs, rsr, xt, xtr, tiles=(0, 1, 2, 3, 4)):
            xpark, XTr = CUR
            for ni in tiles:
                c0, n = NTILES[ni]
                s = ni % 2
                P.dma("sp", xt[s][:, :, 0:n], xpark[:, ni, :, 0:n], reads=[XTr[ni]], writes=[xtr[s]])
                norm_stats(xt[s][:, :, 0:n], xtr[s], n, sqb[s], sqr[s], rs[s], rsr[s])
                def sc(e, c0=c0, n=n, s=s):
                    ins = None
                    for k in range(8):
                        ins = e.scalar_tensor_tensor(out=hT[:, k, c0:c0 + n], in0=xt[s][:, k, 0:n], scalar=gcol[:, k:k + 1],
                                                     in1=rs[s][:, 0:n], op0=ALU.mult, op1=ALU.mult)
                    return ins
                P.op("dve", sc, reads=[xtr[s], rsr[s], cres], writes=[hTr[ni]])

        def proj_fm(wslot, wres, wc0, hT, hTr, ni, kch=8):
            c0, n = NTILES[ni]
            bk, br = bank()
            def mm(e):
                ins = None
                for k in range(kch):
                    ins = e.matmul(bk[:, 0:n], wslot[:, k, wc0:wc0 + 128], hT[:, k, c0:c0 + n], start=(k == 0), stop=(k == kch - 1))
                return ins
            P.op("pe", mm, reads=[wres, hTr[ni]], writes=[br])
            return bk, br

        def post_norm_residual(mixf, mixr, gcol, ni, sqb, sqr, rs, rsr, xt, xtr):
            xpark, XTr = CUR
            c0, n = NTILES[ni]
            s = ni % 2
            P.dma("sp", xt[s][:, :, 0:n], xpark[:, ni, :, 0:n], reads=[XTr[ni]], writes=[xtr[s]])
            sqb, sqr, rs, rsr = sqb[s], sqr[s], rs[s], rsr[s]
            norm_stats(mixf[:, :, 0:n], mixr, n, sqb, sqr, rs, rsr)
            def sc1(e):
                ins = None
                for k in range(8):
                    ins = e.scalar_tensor_tensor(out=mixf[:, k, 0:n], in0=mixf[:, k, 0:n], scalar=gcol[:, k:k + 1],
                                                 in1=rs[:, 0:n], op0=ALU.mult, op1=ALU.mult)
                return ins
            P.chain("dve", [sc1, lambda e: e.tensor_tensor(out=xt[s][:, :, 0:n], in0=xt[s][:, :, 0:n], in1=mixf[:, :, 0:n], op=ALU.add)],
                    reads=[rsr, cres], writes=[mixr, xtr[s]])
            P.dma("sp", xpark[:, ni, :, 0:n], xt[s][:, :, 0:n], reads=[xtr[s]], writes=[XTr[ni]])

        def layer(l, q_, bin_d, fcol, bout_d, mode, outputs):
            CUR[0], CUR[1] = xparks[q_], XTrs[q_]
            CI[0] = CI[0] + 1 if CI[0] >= 0 else 0
            cvA = Carver(0)
            hT = cvA.take(8 * NT).rearrange("p (k t) -> p k t", k=8); hTr = [Res("hT%d" % i) for i in range(5)]
            outb = cvA.take(4 * NT).rearrange("p (h t) -> p h t", h=4); outbr = Res("outb")
            zp = cvA.take(4 * 2050).rearrange("p (c t) -> p c t", c=4); zpr = Res("zp")
            zs = cvA.take(4 * 24).rearrange("p (c b t) -> p c b t", c=4, b=4); zsr = Res("zs")
            zsel = cvA.take(40, F32).rearrange("p (c t) -> p c t", c=4)
            base3 = cvA.o
            KT = cvA.take(2 * 4096).rearrange("p (h t) -> p h t", h=2); KTr = Res("KT")
            KTs = cvA.take(2 * 16).rearrange("p (h t) -> p h t", h=2)
            hx = cvA.take(5472); hxr = Res("hx")
            vh = hx[:, 0:5460].rearrange("p (j h d) -> p j h d", j=21, h=4)
            baseU = cvA.o
            cvP = Carver(baseU)
            kvst = [cvP.take(512, F32) for _ in range(2)]; kvstr = [Res("kvst0"), Res("kvst1")]
            gst = cvP.take(8 * 512).rearrange("p (r c) -> p r c", r=8); gstr = Res("gst")
            tmpb = cvP.take(512); tmpr = Res("tmpb")
            sqb = [cvP.take(8 * 512).rearrange("p (k t) -> p k t", k=8) for _ in range(2)]; sqr = [Res("sqb0"), Res("sqb1")]
            rs = [cvP.take(512, F32) for _ in range(2)]; rsr = [Res("rs0"), Res("rs1")]
            xt = [cvP.take(4096, F32).rearrange("p (k t) -> p k t", k=8) for _ in range(2)]; xtr = [Res("xt0"), Res("xt1")]

            gpm = vecs[:, l, 0, :]; gqm = vecs[:, l, 1, :]; gpf = vecs[:, l, 2, :]; gqf = vecs[:, l, 3, :]

            rmsnorm_h(gpm, hT, hTr, sqb, sqr, rs, rsr, xt, xtr, tiles=(0, 1))
            mark(3)
            if mode == "full":
                wf32 = cvP.take(128, F32); wf32r = Res("wf32")
                gres_ = Res("gmlp_consts")
                P.dma("sp", lng[:, :], alg_d[l:l + 1, :].partition_broadcast(128), writes=[gres_])
                P.dma("sp", lnb[:, :], alb_d[l:l + 1, :].partition_broadcast(128), writes=[gres_])
                P.dma("sp", bsrf[0:1, :, :], abs_d[l:l + 1, :, :], writes=[gres_])
                P.op("dve", lambda e: e.tensor_copy(out=bsr[:], in_=bsrf[:]), reads=[gres_], writes=[gres_])
                wsf = [cvP.take(128, F32) for _ in range(4)]; wsfr = [Res("wsf%d" % g) for g in range(4)]
                wsb = [cvP.take(128) for _ in range(4)]; wsbr = [Res("wsb%d" % g) for g in range(4)]
                for g in range(4):
                    P.dma("sp", wsf[g][:, :], aws_d[l, g, :, :], writes=[wsfr[g]])
                    P.op("dve", lambda e, g=g: e.tensor_tensor(out=wsb[g][:, :], in0=wsf[g][:, :], in1=tril[:, :], op=ALU.mult), reads=[wsfr[g], cres], writes=[wsbr[g]])
            wkv, wkvr, _ = wload(win_cols(l, C_K, 512))
            for ni, (c0, n) in enumerate(NTILES):
                if ni + 2 < 5:
                    rmsnorm_h(gpm, hT, hTr, sqb, sqr, rs, rsr, xt, xtr, tiles=(ni + 2,))
                for ti, (t0, m) in enumerate(TTILES):
                    if not outputs or ntile_of(t0) != ni:
                        continue
                    bk, br = bank()
                    def mm(e, t0=t0, m=m, bk=bk):
                        ins = None
                        for k in range(8):
                            ins = e.matmul(bk[0:m, :], hT[:, k, t0:t0 + m], wkv[:, k, :], start=(k == 0), stop=(k == 7))
                        return ins
                    P.op("pe", mm, reads=[wkvr, hTr[ni]], writes=[br])
                    s = ti % 2
                    P.op("act", lambda e, m=m, bk=bk, s=s: e.copy(out=kvst[s][0:m, :], in_=bk[0:m, :]), reads=[br], writes=[kvstr[s]])
                    P.dma("sp", kv_d[l, t0:t0 + m, :], kvst[s][0:m, :], reads=[kvstr[s]], dsem=out_sem)
                for hp in range(2):
                    bk, br = proj_fm(wkv, wkvr, hp * 128, hT, hTr, ni)
                    if ni < 4:
                        P.op("dve", lambda e, bk=bk, hp=hp, c0=c0, n=n: e.tensor_copy(out=KT[:, hp, 2048 + c0:2048 + c0 + n], in_=bk[:, 0:n]),
                             reads=[br], writes=[KTr])
                    else:
                        P.op("dve", lambda e, bk=bk, hp=hp: e.tensor_copy(out=KTs[:, hp, :], in_=bk[:, 0:16]), reads=[br], writes=[KTr])
            mark(4)
            P.op("dve", lambda e: e.memset(hx[:, :], 1.0), writes=[hxr])
            j = 0
            for g, d in enumerate(DILS):
                nb = 16 // d
                for r in range(d):
                    st = 128 * (nb - 1) * d + r
                    bk, br = bank()
                    def mm(e, st=st, d=d, bk=bk):
                        ins = None
                        for k in range(8):
                            ins = e.matmul(bk[:, 0:256], hT[:, k, st:st + 127 * d + 1:d], wkv[:, k, 256:512], start=(k == 0), stop=(k == 7))
                        return ins
                    P.op("pe", mm, reads=[wkvr] + hTr[0:4], writes=[br])
                    P.op("act", lambda e, bk=bk, j=j: e.copy(out=vh[:, j, :, 0:64], in_=bk[:, 0:256].rearrange("p (h d) -> p h d", h=4)),
                         reads=[br], writes=[hxr])
                    j += 1
            wcx, wcxr, _ = wload(win_cols(l, C_CX, 512))
            wcc, wccr, _ = wload(win_cols(l, C_CC, 512))
            for c in range(4):
                for ni, (c0, n) in enumerate(NTILES):
                    if mode == "halo" and ni != 3:
                        continue
                    bka, bra = proj_fm(wcx, wcxr, c * 128, hT, hTr, ni)
                    bkb, brb = proj_fm(wcc, wccr, c * 128, hT, hTr, ni)
                    P.op("act", lambda e, bka=bka, n=n: e.copy(out=tmpb[:, 0:n], in_=bka[:, 0:n]), reads=[bra], writes=[tmpr])
                    if ni < 4:
                        P.op("dve", lambda e, bkb=bkb, c=c, c0=c0, n=n: e.tensor_tensor(out=zp[:, c, 2 + c0:2 + c0 + n], in0=bkb[:, 0:n], in1=tmpb[:, 0:n], op=ALU.mult),
                             reads=[brb, tmpr], writes=[zpr])
                        if ni == 3:
                            P.op("dve", lambda e, bkb=bkb, c=c: e.tensor_tensor(out=zsel[:, c, 0:2], in0=bkb[:, 510:512], in1=tmpb[:, 510:512], op=ALU.mult),
                                 reads=[brb, tmpr], writes=[zsr])
                    else:
                        def zsm(e, bkb=bkb, c=c):
                            e.tensor_tensor(out=zs[:, c, :, 2:6], in0=bkb[:, 0:16].rearrange("p (b t) -> p b t", b=4),
                                            in1=tmpb[:, 0:16].rearrange("p (b t) -> p b t", b=4), op=ALU.mult)
                            return e.tensor_tensor(out=zsel[:, c, 2:10].rearrange("p (b t) -> p b t", b=4),
                                                   in0=bkb[:, 0:16].rearrange("p (b t) -> p b t", b=4)[:, :, 2:4],
                                                   in1=tmpb[:, 0:16].rearrange("p (b t) -> p b t", b=4)[:, :, 2:4], op=ALU.mult)
                        P.op("dve", zsm, reads=[brb, tmpr], writes=[zsr])
            if mode == "full":
                bk, br = bank()
                def ztr(e, bk=bk):
                    ins = None
                    for c in range(4):
                        ins = e.transpose(bk[0:10, c * 128:(c + 1) * 128], zsel[:, c, :], identf[:, :])
                    return ins
                P.op("pe", ztr, reads=[zsr, cres], writes=[br])
                P.op("act", lambda e, bk=bk: e.copy(out=kvst[0][0:10, :], in_=bk[0:10, :]), reads=[br], writes=[kvstr[0]])
                if outputs:
                    P.dma("sp", zo_d[l, :, :], kvst[0][0:10, :], reads=[kvstr[0]], dsem=out_sem)
                P.dma("sp", kvst[1][0:8, :], sc_d[l, :, :], writes=[kvstr[1]])
                bk, br = bank()
                def str_(e, bk=bk):
                    ins = None
                    for c in range(4):
                        ins = e.transpose(bk[:, c * 8:(c + 1) * 8], kvst[1][0:8, c * 128:(c + 1) * 128], identf[0:8, 0:8])
                    return ins
                P.op("pe", str_, reads=[kvstr[1], cres], writes=[br])
                P.op("dve", lambda e, bk=bk: e.tensor_copy(out=zs[:, :, :, 0:2], in_=bk[:, 0:32].rearrange("p (c b j) -> p c b j", c=4, b=4)),
                     reads=[br], writes=[zsr])

            if mode == "full":
                for g in range(4):
                    bk, br = bank()
                    bkb = bk[:, :].bitcast(BF16)
                    P.op("pe", lambda e, bkb=bkb, g=g: e.transpose(bkb[:, 0:128], wsb[g][:, :], identb[:, :]), reads=[wsbr[g], cres], writes=[br])
                    P.op("act", lambda e, bkb=bkb, g=g: e.copy(out=wst[:, g, :], in_=bkb[:, 0:128]), reads=[br], writes=[gres_])
                P.op("dve", lambda e: e.memset(wss[:, :, :], 0.0), writes=[gres_])
                for g in range(4):
                    for b in range(4):
                        P.dma("sp", wss[b * 4:b * 4 + 4, g, b * 4:b * 4 + 4], wst[0:4, g, 0:4], reads=[gres_, cres], writes=[gres_])
                for b in range(4):
                    P.op("dve", lambda e, b=b: e.tensor_copy(out=bss[0:1, :, b * 4:b * 4 + 4], in_=bsr[0:1, :, 0:4]), reads=[gres_, cres], writes=[gres_])
            mark(5)
            if bout_d is not None:
                bres = Res("bounce")
                for hp in range(2):
                    P.dma("sp", bout_d[:, hp * 2048:(hp + 1) * 2048], KT[:, hp, 2048:4096], reads=[KTr], writes=[bres])
                P.op("dve", lambda e: e.tensor_copy(out=hx[:, 5460:5468].rearrange("p (c t) -> p c t", c=4), in_=zp[:, :, 2048:2050]),
                     reads=[zpr], writes=[hxr])
                P.dma("sp", bout_d[:, 4096:XC], hx[:, :], reads=[hxr], writes=[bres])
            P.barrier()
            if mode == "halo":
                return
            for hp in range(2):
                P.dma("sp", KT[:, hp, 0:2048], bin_d[:, hp * 2048:(hp + 1) * 2048], writes=[KTr])
            P.dma("sp", hx[:, :], bin_d[:, 4096:XC], writes=[hxr])
            for i in range(0, 5472, 1368):
                P.op("dve", lambda e, i=i: e.tensor_scalar(hx[:, i:i + 1368], hx[:, i:i + 1368], fcol, None, ALU.mult),
                     reads=[hxr, cres], writes=[hxr])
            P.op("dve", lambda e: e.tensor_copy(out=zp[:, :, 0:2], in_=hx[:, 5460:5468].rearrange("p (c t) -> p c t", c=4)),
                 reads=[hxr], writes=[zpr])
            P.barrier()

            mark(6)
            cvB = Carver(baseU)
            QT = cvB.take(3 * NT).rearrange("p (g t) -> p g t", g=3); QTr = Res("QT")
            numT = cvB.take(2 * NT, F32).rearrange("p (h t) -> p h t", h=2); numr = Res("numT")
            VAs = cvB.take(130).rearrange("p (h d) -> p h d", h=2)
            VT = cvB.take(NT); VTr = Res("VT")
            baseV = cvB.o
            for hp in range(2):
                cvB = Carver(baseV)
                VA = [cvB.take(n * 130).rearrange("p (j h d) -> p j h d", j=n, h=2) for n in (17, 20, 32)]; VAr = Res("VA")
                pexp = [cvB.take(512) for _ in range(2)]; pexpr = [Res("pexp0"), Res("pexp1")]
                ptm = [cvB.take(512) for _ in range(2)]; ptr_ = [Res("pt0"), Res("pt1")]
                wqa, wqar, _ = wload(win_cols(l, C_Q, 512))
                wqb, wqbr, _ = wload(win_cols(l, C_Q + 512, 256))
                for g in range(3):
                    qc = g * 256 + hp * 128
                    ws_, wr_, wc0 = (wqa, wqar, qc) if qc < 512 else (wqb, wqbr, qc - 512)
                    for ni, (c0, n) in enumerate(NTILES):
                        bk, br = proj_fm(ws_, wr_, wc0, hT, hTr, ni)
                        P.op("act", lambda e, bk=bk, g=g, c0=c0, n=n: e.copy(out=QT[:, g, c0:c0 + n], in_=bk[:, 0:n]), reads=[br], writes=[QTr])
                wv, wvr, _ = wload(win_cols(l, C_V + hp * 128, 128))
                if outputs:
                    cvK = Carver(baseV + 11018)
                    kc = cvK.take(36 * 128).rearrange("p (j c) -> p j c", j=36)
                    vc = cvK.take(36 * 128).rearrange("p (j c) -> p j c", j=36)
                    if hp == 0:
                        kcr = Res("kc"); vcr = Res("vc")
                    cache_loads = []
                    for b in range(4):
                        for (cd, dst, rr) in ((ck_d, kc, kcr), (cv_d, vc, vcr)):
                            cs = slice(hp * 128, hp * 128 + 128)
                            cache_loads.append((dst[:, b * 9, :], cd[l, b, 1920:2048, cs], rr))
                            cache_loads.append((dst[:, b * 9 + 1:b * 9 + 5, :], cd[l, b, 1536:2048, cs].rearrange("(p r) c -> p r c", r=4), rr))
                            cache_loads.append((dst[:, b * 9 + 5:b * 9 + 9, :], cd[l, b, :, cs].rearrange("(p s) c -> p s c", s=16)[:, 0:4, :], rr))
                else:
                    cache_loads = []
                for g in range(3):
                    P.op("dve", lambda e, g=g: e.memset(VA[g][:, :, :, 64:65], 1.0), writes=[VAr])
                P.op("dve", lambda e: e.memset(VAs[0:16, :, 64:65], 1.0), writes=[VAr])
                for ni, (c0, n) in enumerate(NTILES):
                    bk, br = proj_fm(wv, wvr, 0, hT, hTr, ni)
                    P.op("act", lambda e, bk=bk, c0=c0, n=n: e.copy(out=VT[:, c0:c0 + n], in_=bk[:, 0:n]), reads=[br], writes=[VTr])
                for g, d in enumerate(DILS):
                    nb = 16 // d
                    blocks = [(r, n) for r in range(d) for n in range(nb)]
                    for b0 in range(0, 16, 4):
                        bk, br = bank()
                        bkb = bk[:, :].bitcast(BF16)
                        def trv(e, b0=b0, bkb=bkb, d=d, blocks=blocks):
                            ins = None
                            for jj in range(4):
                                r, n = blocks[b0 + jj]
                                st = 128 * n * d + r
                                ins = e.transpose(bkb[:, jj * 128:(jj + 1) * 128], VT[:, st:st + 127 * d + 1:d], identb[:, :])
                            return ins
                        P.op("pe", trv, reads=[VTr, cres], writes=[br])
                        r0, n0 = blocks[b0]
                        t_first = r0 * (nb + 1) + n0 + 1
                        step = 1 if nb >= 4 else (nb + 1)
                        P.op("dve", lambda e, bkb=bkb, g=g, t_first=t_first, step=step: e.tensor_copy(
                            out=VA[g][:, t_first:t_first + 3 * step + 1:step, :, 0:64],
                            in_=bkb[:, 0:512].rearrange("p (j h d) -> p j h d", j=4, h=2)), reads=[br], writes=[VAr])
                bk, br = bank()
                bkb = bk[:, :].bitcast(BF16)
                P.op("pe", lambda e, bkb=bkb: e.transpose(bkb[0:16, 0:128], VT[:, 2048:2064], identb[:, :]), reads=[VTr, cres], writes=[br])
                P.op("dve", lambda e, bkb=bkb: e.tensor_copy(out=VAs[0:16, :, 0:64], in_=bkb[0:16, 0:128].rearrange("p (h d) -> p h d", h=2)),
                     reads=[br], writes=[VAr])
                P.op("dve", lambda e, hp=hp: e.tensor_copy(out=VA[0][:, 0, :, :], in_=vh[:, 0, hp * 2:hp * 2 + 2, :]), reads=[hxr], writes=[VAr])
                P.op("dve", lambda e, hp=hp: e.tensor_copy(out=VA[1][:, 0:20:5, :, :], in_=vh[:, 1:5, hp * 2:hp * 2 + 2, :]), reads=[hxr], writes=[VAr])
                P.op("dve", lambda e, hp=hp: e.tensor_copy(out=VA[2][:, 0:32:2, :, :], in_=vh[:, 5:21, hp * 2:hp * 2 + 2, :]), reads=[hxr], writes=[VAr])

                mark(7)
                blks = []
                for g, d in enumerate(DILS):
                    nb = 16 // d
                    for r in range(d):
                        for n in range(nb):
                            st = 128 * n * d + r
                            blks.append(dict(
                                g=g, tl=r * (nb + 1) + n, s=len(blks) % 2,
                                sl_q=slice(st, st + 127 * d + 1, d),
                                sl_k=[slice(2048 + st - 128 * d, 2048 + st - 128 * d + 127 * d + 1, d),
                                      slice(2048 + st, 2048 + st + 127 * d + 1, d)]))
                pexh = [[Res("pexp%d_%d" % (s_, h_)) for h_ in range(2)] for s_ in range(2)]

                def S_stage(B):
                    s = B["s"]
                    for hh in range(2):
                        bk, br = bank()
                        def mmS(e):
                            ins = None
                            for kb in range(2):
                                ins = e.matmul(bk[:, kb * 128:(kb + 1) * 128],
                                               KT[hh * 64:(hh + 1) * 64, hp, B["sl_k"][kb]], QT[hh * 64:(hh + 1) * 64, B["g"], B["sl_q"]],
                                               start=True, stop=True)
                            return ins
                        P.op("pe", mmS, reads=[KTr, QTr], writes=[br])
                        P.op("act", lambda e: e.activation(out=pexp[s][:, hh * 256:(hh + 1) * 256], in_=bk[:, 0:256], func=AF.Exp, scale=0.125),
                             reads=[br], writes=[pexh[s][hh]])

                def M_stage(B):
                    s = B["s"]
                    P.op("dve" if B["s"] == 0 else "pool", lambda e: e.tensor_tensor(
                        out=ptm[s][:, :], in0=pexp[s][:, :],
                        in1=ep[:, B["g"], hp * 2:hp * 2 + 2, :, :].rearrange("p h k q -> p (h k q)"), op=ALU.mult),
                        reads=[pexh[s][0], pexh[s][1], cres], writes=[ptr_[s]])

                def V_stage(B):
                    s = B["s"]
                    bk2, br2 = bank()
                    def mmV(e):
                        ins = None
                        for hh in range(2):
                            for kb in range(2):
                                ins = e.matmul(bk2[0:65, hh * 128:(hh + 1) * 128], VA[B["g"]][:, B["tl"] + kb, hh, :],
                                               ptm[s][:, (hh * 2 + kb) * 128:(hh * 2 + kb + 1) * 128],
                                               start=(kb == 0), stop=(kb == 1))
                        return ins
                    P.op("pe", mmV, reads=[VAr, ptr_[s]], writes=[br2])
                    if B["g"] == 0:
                        P.op("dve", lambda e: e.tensor_copy(
                            out=numT[0:65, :, B["sl_q"]], in_=bk2[0:65, 0:256].rearrange("p (h q) -> p h q", h=2)),
                            reads=[br2], writes=[numr])
                    else:
                        P.op("dve", lambda e: e.tensor_tensor(
                            out=numT[0:65, :, B["sl_q"]], in0=numT[0:65, :, B["sl_q"]],
                            in1=bk2[0:65, 0:256].rearrange("p (h q) -> p h q", h=2), op=ALU.add),
                            reads=[br2], writes=[numr])

                NB_ = len(blks)
                for i in range(NB_ + 2):
                    if i % 2 == 0 and cache_loads:
                        dst_, src_, rr_ = cache_loads.pop(0)
                        P.dma("pool", dst_, src_, writes=[rr_])
                    if i < NB_:
                        S_stage(blks[i])
                    if 0 <= i - 1 < NB_:
                        M_stage(blks[i - 1])
                    if 0 <= i - 2 < NB_:
                        V_stage(blks[i - 2])
                while cache_loads:
                    dst_, src_, rr_ = cache_loads.pop(0)
                    P.dma("pool", dst_, src_, writes=[rr_])
                P.barrier()
                mark(8)
                if not outputs:
                    P.op("dve", lambda e: e.memset(numT[0:65, :, 2048:2064], 1.0), writes=[numr])
                else:
                    cvS = Carver(baseV)
                    ktc = cvS.take(36 * 128).rearrange("p (j c) -> p j c", j=36); ktcr = Res("ktc")
                    vas = cvS.take(36 * 130).rearrange("p (j h d) -> p j h d", j=36, h=2); vasr = Res("vas")
                    psm = cvS.take(288); psmr = Res("psm")
                    psn = cvS.take(96); psnr = Res("psn")
                    P.op("dve", lambda e: e.memset(vas[:, :, :, 64:65], 1.0), writes=[vasr])
                    P.op("dve", lambda e: e.tensor_copy(out=vas[:, :, :, 0:64], in_=vc[:, :, :].rearrange("p j (h d) -> p j h d", h=2)),
                         reads=[vcr], writes=[vasr])
                    for j0 in range(0, 36, 8):
                        nj = min(8, 36 - j0)
                        bk, br = bank()
                        bkb = bk[:, :].bitcast(BF16)
                        def trk(e, bkb=bkb, j0=j0, nj=nj):
                            ins = None
                            for jj in range(nj):
                                ins = e.transpose(bkb[:, jj * 128:(jj + 1) * 128], kc[:, j0 + jj, :], identb[:, :])
                            return ins
                        P.op("pe", trk, reads=[kcr, cres], writes=[br])
                        P.op("act", lambda e, bkb=bkb, j0=j0, nj=nj: e.copy(out=ktc[:, j0:j0 + nj, :], in_=bkb[:, 0:nj * 128].rearrange("p (j c) -> p j c", j=nj)),
                             reads=[br], writes=[ktcr])
                    for bh in range(2):
                      for hh in range(2):
                        bk, br = bank()
                        def mmS2(e, bk=bk, bh=bh, hp=hp, hh=hh):
                            ins = None
                            for bb in range(2):
                                b = bh * 2 + bb
                                for tile in range(9):
                                    g = 0 if tile == 0 else (1 if tile < 5 else 2)
                                    col = (bb * 9 + tile) * 4
                                    ins = e.matmul(bk[:, col:col + 4], ktc[hh * 64:(hh + 1) * 64, b * 9 + tile, :],
                                                   QT[hh * 64:(hh + 1) * 64, g, 2048 + b * 4:2048 + b * 4 + 4], start=True, stop=True)
                            return ins
                        P.op("pe", mmS2, reads=[ktcr, QTr], writes=[br])
                        P.op("act", lambda e, bk=bk, bh=bh, hh=hh: e.activation(
                            out=psm[:, bh * 144:(bh + 1) * 144].rearrange("p (j h t) -> p j h t", j=18, h=2)[:, :, hh, :],
                            in_=bk[:, 0:72].rearrange("p (j t) -> p j t", j=18), func=AF.Exp, scale=0.125),
                             reads=[br], writes=[psmr])
                    for b in range(4):
                        P.op("dve", lambda e, b=b, hp=hp: e.tensor_tensor(
                            out=psm[:, b * 72:(b + 1) * 72].rearrange("p (j h t) -> p j h t", j=9, h=2),
                            in0=psm[:, b * 72:(b + 1) * 72].rearrange("p (j h t) -> p j h t", j=9, h=2),
                            in1=es_s[:, :, hp * 2:hp * 2 + 2, :], op=ALU.mult), reads=[psmr, cres], writes=[psmr])
                    for hh in range(2):
                        bk, br = bank()
                        def mmN(e, bk=bk, hp=hp, hh=hh):
                            ins = None
                            for g in range(3):
                                ins = e.matmul(bk[0:16, g * 16:(g + 1) * 16], KTs[hh * 64:(hh + 1) * 64, hp, :], QT[hh * 64:(hh + 1) * 64, g, 2048:2064],
                                               start=True, stop=True)
                            return ins
                        P.op("pe", mmN, reads=[KTr, QTr], writes=[br])
                        P.op("act", lambda e, bk=bk, hh=hh: e.activation(
                            out=psn[0:16, :].rearrange("p (g h q) -> p g h q", g=3, h=2)[:, :, hh, :],
                            in_=bk[0:16, 0:48].rearrange("p (g q) -> p g q", g=3), func=AF.Exp, scale=0.125), reads=[br], writes=[psnr])
                    P.op("dve", lambda e, hp=hp: e.tensor_tensor(out=psn[0:16, :].rearrange("p (g h q) -> p g h q", g=3, h=2),
                                                                  in0=psn[0:16, :].rearrange("p (g h q) -> p g h q", g=3, h=2),
                                                                  in1=en_s[0:16, :, hp * 2:hp * 2 + 2, :], op=ALU.mult), reads=[psnr, cres], writes=[psnr])
                    bk, br = bank()
                    def mmPV(e, bk=bk):
                        ins = None
                        first = True
                        for hh in range(2):
                            for g in range(3):
                                col = (g * 2 + hh) * 16
                                ins = e.matmul(bk[0:65, hh * 16:(hh + 1) * 16], VAs[0:16, hh, :], psn[0:16, col:col + 16], start=first, stop=False,
                                               skip_group_check=True)
                                first = False
                            for b in range(4):
                                for tile in range(9):
                                    col = ((b * 9 + tile) * 2 + hh) * 4
                                    ins = e.matmul(bk[0:65, hh * 16 + b * 4:hh * 16 + b * 4 + 4], vas[:, b * 9 + tile, hh, :], psm[:, col:col + 4],
                                                   start=False, stop=(tile == 8 and b == 3 and hh == 1), skip_group_check=True)
                        return ins
                    P.op("pe", mmPV, reads=[VAr, vasr, psmr, psnr], writes=[br])
                    P.op("dve", lambda e, bk=bk: e.tensor_copy(out=numT[0:65, :, 2048:2064], in_=bk[0:65, 0:32].rearrange("p (h q) -> p h q", h=2)),
                         reads=[br], writes=[numr])

                mark(9)
                P.op("dve", lambda e: e.reciprocal(out=numT[64:65, :, :], in_=numT[64:65, :, :]), reads=[numr], writes=[numr])
                for hh in range(2):
                    for ni, (c0, n) in enumerate(NTILES):
                        bk, br = bank()
                        P.op("pe", lambda e, bk=bk, hh=hh, c0=c0, n=n: e.matmul(bk[0:64, 0:n], onesf[64:65, 0:64], numT[64:65, hh, c0:c0 + n], start=True, stop=True),
                             reads=[numr, cres], writes=[br])
                        P.op("dve", lambda e, bk=bk, hh=hh, c0=c0, n=n, hp=hp: e.tensor_tensor(
                            out=outb[0:64, hp * 2 + hh, c0:c0 + n], in0=numT[0:64, hh, c0:c0 + n], in1=bk[0:64, 0:n], op=ALU.mult),
                            reads=[br, numr], writes=[outbr])
                P.barrier()

            mark(10)
            cvC = Carver(base3)
            uT = cvC.take(4 * NT).rearrange("p (c t) -> p c t", c=4); uTr = Res("uT")
            outc = cvC.take(4 * NT).rearrange("p (c t) -> p c t", c=4); outcr = Res("outc")
            merged = cvC.take(8 * NT).rearrange("p (c t) -> p c t", c=8); mergedr = [Res("mg%d" % i) for i in range(5)]
            sg = [cvC.take(512) for _ in range(3)]; sgr = [Res("sg%d" % i) for i in range(3)]
            mt = [cvC.take(512, F32) for _ in range(2)]; mtr = [Res("mt0"), Res("mt1")]
            vnb = [cvC.take(512) for _ in range(2)]; vnbr = [Res("vnb0"), Res("vnb1")]
            vtmp = [cvC.take(512, F32) for _ in range(2)]; vtmpr = [Res("vt0"), Res("vt1")]
            stats = cvC.take(16, F32); statr = Res("stats")
            ctmp = cvC.take(512, F32); ctmpr = Res("ctmp")
            wf32 = cvC.take(128, F32); wf32r = Res("wf32")
            tmpb = cvC.take(512); tmpr = Res("tmpb2")

            wua, wuar, _ = wload(win_cols(l, C_UA, 512))
            wva, wvar, _ = wload(win_cols(l, C_VA, 512))
            wcb, wcbr, _ = wload(win_cols(l, C_CB, 512))
            stats2 = [stats[:, 0:16], cvC.take(16, F32)]; statr2 = [Res("stats0"), Res("stats1")]

            def u_group(c, ni):
                c0, n = NTILES[ni]
                bk, br = proj_fm(wua, wuar, c * 128, hT, hTr, ni)
                P.op("act", lambda e: e.activation(out=uT[:, c, c0:c0 + n], in_=bk[:, 0:n], func=AF.Gelu_apprx_tanh),
                     reads=[br], writes=[uTr])

            def v_front(ti):
                t0, m = TTILES[ti]
                s = ti % 2
                st_, str2 = stats2[s], statr2[s]
                bk, br = bank()
                def mm(e):
                    ins = None
                    for k in range(8):
                        ins = e.matmul(bk[0:m, :], hT[:, k, t0:t0 + m], wva[:, k, :], start=(k == 0), stop=(k == 7))
                    return ins
                P.op("pe", mm, reads=[wvar, hTr[ntile_of(t0)]], writes=[br])
                P.op("act", lambda e: e.activation(out=vtmp[s][0:m, :], in_=bk[0:m, :], func=AF.Gelu_apprx_tanh), reads=[br], writes=[vtmpr[s]])
                P.chain("dve", [
                    lambda e: e.bn_stats(out=st_[0:m, 0:6], in_=vtmp[s][0:m, :]),
                    lambda e: e.bn_aggr(out=st_[0:m, 6:8], in_=st_[0:m, 0:6]),
                    lambda e: e.tensor_scalar(st_[0:m, 8:9], st_[0:m, 7:8], EPS, None, ALU.add)],
                    reads=[vtmpr[s]], writes=[str2])
                P.op("act", lambda e: e.activation(out=st_[0:m, 9:10], in_=st_[0:m, 8:9], func=AF.Sqrt), reads=[str2], writes=[str2])
                fns = [
                    lambda e: e.reciprocal(out=st_[0:m, 9:10], in_=st_[0:m, 9:10]),
                    lambda e: e.scalar_tensor_tensor(out=vtmp[s][0:m, :], in0=vtmp[s][0:m, :], scalar=st_[0:m, 6:7], in1=lng[0:m, :],
                                                     op0=ALU.subtract, op1=ALU.mult)]
                if ti == 16:
                    fns.append(lambda e: e.scalar_tensor_tensor(out=vtmp[s][0:m, :], in0=vtmp[s][0:m, :], scalar=st_[0:m, 9:10], in1=lnb[0:m, :],
                                                                op0=ALU.mult, op1=ALU.add))
                    fns.append(lambda e: e.tensor_copy(out=vnb[s][0:m, :], in_=vtmp[s][0:m, :]))
                else:
                    fns.append(lambda e: e.scalar_tensor_tensor(out=vnb[s][0:m, :], in0=vtmp[s][0:m, :], scalar=st_[0:m, 9:10], in1=lnb[0:m, :],
                                                                op0=ALU.mult, op1=ALU.add))
                P.chain("dve", fns, reads=[str2, cres, gres_], writes=[vtmpr[s], vnbr[s], str2])
                if ti == 16 and outputs:
                    P.dma("sp", vn_d[l, :, :], vtmp[s][0:16, :], reads=[vtmpr[s]], dsem=out_sem)

            def v_back(ti):
                t0, m = TTILES[ti]
                s = ti % 2
                bk2, br2 = bank()
                def mmg(e):
                    ins = None
                    for g in range(4):
                        if ti < 16:
                            e.matmul(bk2[:, g * 128:(g + 1) * 128], vnb[s][:, g * 128:(g + 1) * 128], wst[:, g, :], start=True, stop=False)
                            ins = e.matmul(bk2[:, g * 128:(g + 1) * 128], onesb[0:1, 0:128], bsr[0:1, g, :], start=False, stop=True)
                        else:
                            e.matmul(bk2[:, g * 16:(g + 1) * 16], vnb[s][0:16, g * 128:(g + 1) * 128], wss[0:16, g, :], start=True, stop=False)
                            ins = e.matmul(bk2[:, g * 16:(g + 1) * 16], onesb[0:1, 0:128], bss[0:1, g, :], start=False, stop=True)
                    return ins
                P.op("pe", mmg, reads=[vnbr[s], cres, gres_], writes=[br2])
                P.op("dve", lambda e: e.tensor_tensor(
                    out=uT[:, :, t0:t0 + m], in0=uT[:, :, t0:t0 + m], in1=bk2[:, 0:4 * m].rearrange("p (g t) -> p g t", g=4), op=ALU.mult),
                    reads=[br2, uTr], writes=[uTr])

            def cb_group(c, ni):
                c0, n = NTILES[ni]
                bk, br = proj_fm(wcb, wcbr, c * 128, hT, hTr, ni)
                if ni < 4:
                    z0, z1, z2 = zp[:, c, c0:c0 + n], zp[:, c, c0 + 1:c0 + 1 + n], zp[:, c, c0 + 2:c0 + 2 + n]
                    a = ctmp[:, 0:n]; o_ = outc[:, c, c0:c0 + n]; cbp = bk[:, 0:n]
                else:
                    z0, z1, z2 = zs[:, c, :, 0:4], zs[:, c, :, 1:5], zs[:, c, :, 2:6]
                    a = ctmp[:, 0:16].rearrange("p (b t) -> p b t", b=4)
                    o_ = outc[:, c, 2048:2064].rearrange("p (b t) -> p b t", b=4)
                    cbp = bk[:, 0:16].rearrange("p (b t) -> p b t", b=4)
                P.chain("dve", [
                    lambda e: e.tensor_scalar(a, z2, cwc[:, l, 2, c:c + 1], cwc[:, l, 3, c:c + 1], ALU.mult, ALU.add),
                    lambda e: e.scalar_tensor_tensor(out=a, in0=z1, scalar=cwc[:, l, 1, c:c + 1], in1=a, op0=ALU.mult, op1=ALU.add),
                    lambda e: e.scalar_tensor_tensor(out=a, in0=z0, scalar=cwc[:, l, 0, c:c + 1], in1=a, op0=ALU.mult, op1=ALU.add),
                    lambda e: e.tensor_tensor(out=o_, in0=a, in1=cbp, op=ALU.mult)],
                    reads=[br, zpr, zsr, cres], writes=[ctmpr, outcr])

            pending = None
            for ni in range(5):
                for c in range(4):
                    u_group(c, ni)
                tiles = [ti for ti, (t0, m) in enumerate(TTILES) if ntile_of(t0) == ni]
                for idx, ti in enumerate(tiles):
                    v_front(ti)
                    if ni < 4:
                        cb_group(idx, ni)
                    else:
                        for c in range(4):
                            cb_group(c, ni)
                    if pending is not None:
                        v_back(pending)
                    pending = ti
            v_back(pending)
            mark(11)
            mark(12)
            gview = win_d[l, :, C_GA:C_GA + 3072].rearrange("(k p) (b c) -> p k b c", p=128, b=3)
            with nc.allow_non_contiguous_dma(reason="128-column weight slices"):
                for f in range(8):
                    sg_i = ring_slot()
                    wg3 = []
                    for bq, cg in enumerate((C_GA, C_GB, C_GC)):
                        wq_, wg3r, _ = wload(win_cols(l, cg + f * 128, 128), slot=sg_i, off=bq * 1024)
                        wg3.append(wq_)
                    si = sg_i
                    wA, wbr_r, _ = wload(wba_d[l, :, f * 128:(f + 1) * 128].rearrange("(k p) c -> p k c", p=128), slot=si, off=3072)
                    wC, _, _ = wload(wbc_d[l, :, f * 128:(f + 1) * 128].rearrange("(k p) c -> p k c", p=128), slot=si, off=3584)
                    wB, _, _ = wload(wbb_d[l, :, f * 128:(f + 1) * 128].rearrange("(k p) c -> p k c", p=64), slot=si, off=4096)
                    for ni, (c0, n) in enumerate(NTILES):
                        prods = []
                        for bi_ in range(3):
                            bk, br = bank()
                            def mmgate(e, bk=bk, bi_=bi_, c0=c0, n=n, wg3=wg3):
                                ins = None
                                for k in range(8):
                                    ins = e.matmul(bk[:, 0:n], wg3[bi_][:, k, :], hT[:, k, c0:c0 + n], start=(k == 0), stop=(k == 7))
                                return ins
                            P.op("pe", mmgate, reads=[wg3r, hTr[ni]], writes=[br])
                            P.op("act", lambda e, bk=bk, bi_=bi_, n=n, f=f, l=l: e.activation(out=sg[bi_][:, 0:n], in_=bk[:, 0:n], func=AF.Sigmoid,
                                                                                          bias=bgc[:, l, bi_ * 8 + f:bi_ * 8 + f + 1]),
                                 reads=[br, cres], writes=[sgr[bi_]])
                            bk2, br2 = bank()
                            def mmbr(e, bk2=bk2, bi_=bi_, c0=c0, n=n, wA=wA, wB=wB, wC=wC):
                                ins = None
                                for k in range(4):
                                    if bi_ == 0:
                                        ins = e.matmul(bk2[:, 0:n], wA[:, k, :], uT[:, k, c0:c0 + n], start=(k == 0), stop=(k == 3))
                                    elif bi_ == 1:
                                        ins = e.matmul(bk2[:, 0:n], wB[0:64, k, :], outb[0:64, k, c0:c0 + n], start=(k == 0), stop=(k == 3))
                                    else:
                                        ins = e.matmul(bk2[:, 0:n], wC[:, k, :], outc[:, k, c0:c0 + n], start=(k == 0), stop=(k == 3))
                                return ins
                            P.op("pe", mmbr, reads=[wbr_r, uTr, outbr, outcr], writes=[br2])
                            prods.append((bk2, br2))
                        P.chain("dve", [
                            lambda e: e.tensor_tensor(out=mt[0][:, 0:n], in0=sg[0][:, 0:n], in1=prods[0][0][:, 0:n], op=ALU.mult),
                            lambda e: e.tensor_tensor(out=mt[1][:, 0:n], in0=sg[1][:, 0:n], in1=prods[1][0][:, 0:n], op=ALU.mult),
                            lambda e: e.tensor_tensor(out=mt[0][:, 0:n], in0=mt[0][:, 0:n], in1=mt[1][:, 0:n], op=ALU.add),
                            lambda e: e.tensor_tensor(out=mt[1][:, 0:n], in0=sg[2][:, 0:n], in1=prods[2][0][:, 0:n], op=ALU.mult),
                            lambda e: e.tensor_tensor(out=merged[:, f, c0:c0 + n], in0=mt[0][:, 0:n], in1=mt[1][:, 0:n], op=ALU.add)],
                            reads=[sgr[0], sgr[1], sgr[2], prods[0][1], prods[1][1], prods[2][1]], writes=[mtr[0], mtr[1], mergedr[ni]])
            P.barrier()
            mark(13)
            cvD = Carver(0)
            mixf2 = [cvD.take(8 * 512, F32).rearrange("p (k t) -> p k t", k=8) for _ in range(2)]; mixr2 = [[Res("mixf0a"), Res("mixf0b")], [Res("mixf1a"), Res("mixf1b")]]
            sqb2 = [cvD.take(8 * 512).rearrange("p (k t) -> p k t", k=8) for _ in range(2)]; sqr2 = [Res("sqb20"), Res("sqb21")]
            rs2 = [cvD.take(512, F32) for _ in range(2)]; rsr2 = [Res("rs20"), Res("rs21")]
            assert cvD.o <= base3
            cvD2 = Carver(base3)
            xt2 = [cvD2.take(4096, F32).rearrange("p (k t) -> p k t", k=8) for _ in range(2)]; xtr2 = [Res("xt20"), Res("xt21")]
            wos = [wload(wo_d[l, :, half * 512:(half + 1) * 512].rearrange("(k p) c -> p k c", p=128)) for half in range(2)]
            def wo_tile(ni):
                c0, n = NTILES[ni]
                mixf, mixr = mixf2[ni % 2], mixr2[ni % 2]
                for f in range(8):
                    bk, br = bank()
                    def mm(e):
                        ins = None
                        for k in range(8):
                            ins = e.matmul(bk[:, 0:n], wos[f // 4][0][:, k, (f % 4) * 128:(f % 4 + 1) * 128], merged[:, k, c0:c0 + n], start=(k == 0), stop=(k == 7))
                        return ins
                    P.op("pe", mm, reads=[wos[f // 4][1], mergedr[ni]], writes=[br])
                    if f % 2 == 0:
                        P.op("act", lambda e: e.copy(out=mixf[:, f, 0:n], in_=bk[:, 0:n]), reads=[br], writes=[mixr[0]])
                    else:
                        P.op("dve", lambda e: e.tensor_copy(out=mixf[:, f, 0:n], in_=bk[:, 0:n]), reads=[br], writes=[mixr[1]])
            for ni in range(5):
                wo_tile(ni)
                if ni >= 1:
                    post_norm_residual(mixf2[(ni - 1) % 2], mixr2[(ni - 1) % 2], gqm, ni - 1, sqb2, sqr2, rs2, rsr2, xt2, xtr2)
            post_norm_residual(mixf2[0], mixr2[0], gqm, 4, sqb2, sqr2, rs2, rsr2, xt2, xtr2)
            P.barrier()

            mark(14)
            cvF = Carver(0)
            hF = cvF.take(8 * NT).rearrange("p (k t) -> p k t", k=8); hFr = [Res("hF%d" % i) for i in range(5)]
            sqb = [cvF.take(8 * 512).rearrange("p (k t) -> p k t", k=8) for _ in range(2)]; sqr = [Res("sqbF0"), Res("sqbF1")]
            rs = [cvF.take(512, F32) for _ in range(2)]; rsr = [Res("rsF0"), Res("rsF1")]
            actb = cvF.take(22 * 1040).rearrange("p (j t) -> p j t", j=22); actr = Res("actb")
            mixf = cvF.take(8 * 512, F32).rearrange("p (k t) -> p k t", k=8); mixr = [Res("mixFa"), Res("mixFb")]
            stmp = [cvF.take(512) for _ in range(2)]; stmpr = [Res("st0"), Res("st1")]
            xt = [cvF.take(4096, F32).rearrange("p (k t) -> p k t", k=8) for _ in range(2)]; xtr = [Res("xtF0"), Res("xtF1")]
            rmsnorm_h(gpf, hF, hFr, sqb, sqr, rs, rsr, xt, xtr, tiles=(0, 1))
            for hh_, nis in enumerate(((0, 1), (2, 3, 4))):
                base = NTILES[nis[0]][0]
                for j4 in range(0, 22, 2):
                    nj = 2
                    sgu = ring_slot()
                    wgt, wgtr, _ = wload(wg_d[l, :, j4 * 128:(j4 + nj) * 128].rearrange("(k p) c -> p k c", p=128), slot=sgu, off=0)
                    wup, wupr, _ = wload(wu_d[l, :, j4 * 128:(j4 + nj) * 128].rearrange("(k p) c -> p k c", p=128), slot=sgu, off=2048)
                    for jj in range(nj):
                        j = j4 + jj
                        for ni in nis:
                            c0, n = NTILES[ni]
                            bkg, brg = proj_fm(wgt, wgtr, jj * 128, hF, hFr, ni)
                            bku, bru = proj_fm(wup, wupr, jj * 128, hF, hFr, ni)
                            s = (j + ni) % 2
                            P.op("act", lambda e, bkg=bkg, s=s, n=n: e.activation(out=stmp[s][:, 0:n], in_=bkg[:, 0:n], func=AF.Silu), reads=[brg], writes=[stmpr[s]])
                            P.op("dve", lambda e, bku=bku, s=s, n=n, j=j, c0=c0, base=base: e.tensor_tensor(
                                out=actb[:, j, c0 - base:c0 - base + n], in0=stmp[s][:, 0:n], in1=bku[:, 0:n], op=ALU.mult),
                                reads=[bru, stmpr[s]], writes=[actr])
                    if hh_ == 0 and j4 in (0, 2, 4):
                        rmsnorm_h(gpf, hF, hFr, sqb, sqr, rs, rsr, xt, xtr, tiles=(2 + j4 // 2,))
                for ni in nis:
                    c0, n = NTILES[ni]
                    bks = [bank() for _ in range(8)]
                    for j4 in range(0, 22, 4):
                        nj = min(4, 22 - j4)
                        wdn, wdnr, _ = wload(wd_d[l, j4 * 128:(j4 + nj) * 128, :].rearrange("(j p) c -> p j c", p=128))
                        def mm(e, bks=bks, j4=j4, nj=nj, c0=c0, n=n, base=base, wdn=wdn):
                            ins = None
                            for f in range(8):
                                for jj in range(nj):
                                    j = j4 + jj
                                    ins = e.matmul(bks[f][0][:, 0:n], wdn[:, jj, f * 128:(f + 1) * 128], actb[:, j, c0 - base:c0 - base + n],
                                                   start=(j == 0), stop=(j == 21))
                            return ins
                        P.op("pe", mm, reads=[wdnr, actr], writes=[b[1] for b in bks])
                    for f in range(8):
                        if f % 2 == 0:
                            P.op("act", lambda e, f=f, n=n, bks=bks: e.copy(out=mixf[:, f, 0:n], in_=bks[f][0][:, 0:n]), reads=[bks[f][1]], writes=[mixr[0]])
                        else:
                            P.op("dve", lambda e, f=f, n=n, bks=bks: e.tensor_copy(out=mixf[:, f, 0:n], in_=bks[f][0][:, 0:n]), reads=[bks[f][1]], writes=[mixr[1]])
                    post_norm_residual(mixf, mixr, gqf, ni, sqb, sqr, rs, rsr, xt, xtr)
            P.barrier()

        f1 = flagb[:, 0:1]; f2 = flagb[:, 1:2]
        layer(0, 2, None, None, bounce_d[0], "halo", False)
        layer(0, 1, bounce_d[0], f2, bounce_d[1], "full", False)
        layer(0, 0, bounce_d[1], f1, None, "full", True)
        layer(1, 1, None, None, bounce_d[2], "halo", False)
        layer(1, 0, bounce_d[2], f1, None, "full", True)
        xpark, XTr = xparks[0], XTrs[0]
        cvO = Carver(0)
        ost = [cvO.take(1024, F32) for _ in range(2)]; ostr = [Res("ost0"), Res("ost1")]
        xt = [cvO.take(4096, F32).rearrange("p (k t) -> p k t", k=8) for _ in range(2)]; xtr = [Res("xtO0"), Res("xtO1")]
        for ni, (c0, n) in enumerate(NTILES):
            xs_ = ni % 2
            P.dma("sp", xt[xs_][:, :, 0:n], xpark[:, ni, :, 0:n], reads=[XTr[ni]], writes=[xtr[xs_]])
            for lt in range(0, n, 128):
                m = min(128, n - lt)
                t0 = c0 + lt
                ti = t0 // 128
                s = ti % 2
                for half in range(2):
                    bk, br = bank()
                    def tr(e, bk=bk, half=half, lt=lt, m=m, xs_=xs_):
                        ins = None
                        for jj in range(4):
                            k = half * 4 + jj
                            ins = e.transpose(bk[0:m, jj * 128:(jj + 1) * 128], xt[xs_][:, k, lt:lt + m], identf[:, :])
                        return ins
                    P.op("pe", tr, reads=[xtr[xs_], cres], writes=[br])
                    P.op("act", lambda e, bk=bk, half=half, s=s, m=m: e.copy(out=ost[s][0:m, half * 512:(half + 1) * 512], in_=bk[0:m, :]),
                         reads=[br], writes=[ostr[s]])
                dst = yp_d[t0:t0 + m, :] if ni < 4 else ys_d[:, :]
                P.dma("sp", dst, ost[s][0:m, :], reads=[ostr[s]], dsem=out_sem)
        P.barrier()

    def mk(name):
        def body(eng):
            P.reset(name, eng)
            program()
        return body
    with nc.Block() as block:
        block.tensor(mk("pe"))
        block.scalar(mk("act"))
        block.vector(mk("dve"))
        block.gpsimd(mk("pool"))
        block.sync(mk("sp"))
    es.close()
    return nc


_NC_CACHE = {}


def kernel(**inputs):
    f32 = np.float32
    inp = {k: np.ascontiguousarray(np.asarray(v), dtype=f32) for k, v in inputs.items()}
    if "nc" not in _NC_CACHE:
        _NC_CACHE["nc"] = build()
    nc = _NC_CACHE["nc"]
    consts = host_consts()
    shared = {k: inp[k] for k in ("g_pre_mix", "g_post_mix", "g_pre_ffn", "g_post_ffn", "w_in", "a_ln_g", "a_ln_b", "a_ws", "a_bs",
                                  "c_conv_w", "c_conv_b", "w_br_a", "w_br_b", "w_br_c", "b_gate", "w_o", "w_ff_gate", "w_ff_up", "w_ff_down")}
    in_maps = []
    for c in range(NCORES):
        b, q = c // 4, c % 4
        m = dict(shared)
        m.update(consts)
        m["xp"] = np.ascontiguousarray(inp["x_prompt"][b, q * NPR:(q + 1) * NPR, :])
        m["xs"] = np.ascontiguousarray(inp["x_sample"][c * 4:(c + 1) * 4].reshape(NSM, D))
        m["ck"] = np.ascontiguousarray(inp["cache_k_win"][:, c * 4:(c + 1) * 4].reshape(DEPTH, 4, 2048, 256))
        m["cv"] = np.ascontiguousarray(inp["cache_v_win"][:, c * 4:(c + 1) * 4].reshape(DEPTH, 4, 2048, 256))
        m["sc"] = np.ascontiguousarray(inp["state_conv"][:, c * 4:(c + 1) * 4].reshape(DEPTH, 8, 512))
        q1, q2 = max(q - 1, 0), max(q - 2, 0)
        m["xp1"] = np.ascontiguousarray(inp["x_prompt"][b, q1 * NPR:(q1 + 1) * NPR, :])
        m["xp2"] = np.ascontiguousarray(inp["x_prompt"][b, q2 * NPR:(q2 + 1) * NPR, :])
        m["flags"] = np.array([[1.0 if q >= 1 else 0.0, 1.0 if q >= 2 else 0.0]], f32)
        in_maps.append(m)
    res = run_bass_kernel_spmd(nc, in_maps, core_ids=list(range(NCORES)))
    shp = {"yp": (NPR, D), "ys": (NSM, D), "kv": (DEPTH, NT, 512), "zo": (DEPTH, 10, 512), "vns": (DEPTH, NSM, 512)}
    R = [{k: np.asarray(r[k]).reshape(shp[k]) for k in shp} for r in res.results]
    yp = np.stack([np.concatenate([R[b * 4 + q]["yp"] for q in range(4)], axis=0) for b in range(2)]).astype(f32)
    ys = np.concatenate([R[c]["ys"].reshape(4, 4, D) for c in range(NCORES)], axis=0).astype(f32)
    kp = np.stack([np.stack([R[b * 4 + 3]["kv"][l, 0:NPR, 0:256].reshape(NPR, 4, 64) for b in range(2)]) for l in range(DEPTH)]).astype(f32)
    vp = np.stack([np.stack([R[b * 4 + 3]["kv"][l, 0:NPR, 256:512].reshape(NPR, 4, 64) for b in range(2)]) for l in range(DEPTH)]).astype(f32)
    ksm = np.stack([np.concatenate([R[c]["kv"][l, NPR:NT, 0:256].reshape(4, 4, 4, 64) for c in range(NCORES)], axis=0) for l in range(DEPTH)]).astype(f32)
    vsm = np.stack([np.concatenate([R[c]["kv"][l, NPR:NT, 256:512].reshape(4, 4, 4, 64) for c in range(NCORES)], axis=0) for l in range(DEPTH)]).astype(f32)
    cp = np.stack([np.stack([R[b * 4 + 3]["zo"][l, 0:2, :] for b in range(2)]) for l in range(DEPTH)]).astype(f32)
    cs = np.stack([np.concatenate([R[c]["zo"][l, 2:10, :].reshape(4, 2, 512) for c in range(NCORES)], axis=0) for l in range(DEPTH)]).astype(f32)
    av = np.stack([np.concatenate([R[c]["vns"][l].reshape(4, 4, 512) for c in range(NCORES)], axis=0) for l in range(DEPTH)]).astype(f32)
    return (yp, ys, kp, vp, ksm, vsm, cp, cs, av)
```

```python
import os
import numpy as np
import ml_dtypes
from contextlib import ExitStack
import concourse.bass as bass
import concourse.mybir as mybir
from concourse.bass_utils import run_bass_kernel_spmd

F32 = mybir.dt.float32
BF16 = mybir.dt.bfloat16
AF = mybir.ActivationFunctionType
ALU = mybir.AluOpType
AX = mybir.AxisListType

NCORES = 8
D = 1024
DEPTH = 2
NPR = 2048
NSM = 16
NT = NPR + NSM
INW = 6912
DFF = 2816
NTILES = [(0, 512), (512, 512), (1024, 512), (1536, 512), (2048, 16)]
TTILES = [(i * 128, 128) for i in range(16)] + [(2048, 16)]
DILS = (1, 4, 16)
EPS = 1e-6
XC = 9568
C_UA, C_VA, C_Q, C_K, C_V, C_CX, C_CB, C_CC, C_GA, C_GB, C_GC = 0, 512, 1024, 1792, 2048, 2304, 2816, 3328, 3840, 4864, 5888
SAME_ENG_SYNC = True


KSTOP = int(os.environ.get("KSTOP", "-1"))


class Stop(Exception):
    pass


class Res:
    __slots__ = ("name", "w", "r")

    def __init__(self, name):
        self.name = name
        self.w = None
        self.r = []


class DSem:
    def __init__(self, handle, idx):
        self.h = handle
        self.idx = idx
        self.total = 0


class Prog:
    ENG = ("pe", "act", "dve", "pool", "sp")
    NPOOL = 24

    def __init__(self, nc, es):
        self.nc, self.es = nc, es
        self.sem = {e: es.enter_context(nc.semaphore("s_" + e)) for e in self.ENG}
        self.handles = []
        self.named = {}
        self.reset(None, None)

    def reset(self, cur, eobj):
        self.cur, self.eobj = cur, eobj
        self.cnt = {e: 0 for e in self.ENG}
        self.seen = {}
        self.dsems = []
        self.hidx = 0
        self.pool = None
        self.pool_i = 0
        self.pend = None

    def named_sem(self, name):
        if name not in self.named:
            self.named[name] = self.es.enter_context(self.nc.semaphore(name))
        return self.named[name]

    def new_dsem(self):
        if self.hidx >= len(self.handles):
            self.handles.append(self.es.enter_context(self.nc.semaphore("d%d" % len(self.handles))))
        d = DSem(self.handles[self.hidx], self.hidx)
        self.hidx += 1
        self.dsems.append(d)
        return d

    @staticmethod
    def _flat(rs):
        out = []
        for r in rs:
            if isinstance(r, (list, tuple)):
                out.extend(Prog._flat(r))
            else:
                out.append(r)
        return out

    def _waits(self, eng, reads, writes):
        reads, writes = self._flat(reads), self._flat(writes)
        toks = []
        for r in reads:
            if r.w is not None:
                toks.append(r.w)
        for r in writes:
            if r.w is not None:
                toks.append(r.w)
            toks.extend(r.r)
        need = {}
        for (h, val, key, src) in toks:
            if src == eng and (eng == "pe" or not SAME_ENG_SYNC) and key == "c_" + eng:
                continue
            if need.get(key, (None, 0))[1] < val:
                need[key] = (h, val)
        waits = []
        for key, (h, val) in need.items():
            if self.seen.get((eng, key), 0) >= val:
                continue
            self.seen[(eng, key)] = val
            waits.append((h, val))
        return waits

    def _commit(self, tok, reads, writes):
        reads, writes = self._flat(reads), self._flat(writes)
        for r in reads:
            r.r.append(tok)
        for r in writes:
            r.w = tok
            r.r = []

    def _emit(self, eng, waits, fn, sh, inc):
        if eng != self.cur:
            return
        for h, v in waits:
            self.eobj.wait_ge(h, v)
        if fn is not None:
            ins = fn(self.eobj)
            if inc:
                ins.then_inc(sh, inc)

    def op(self, eng, fn, reads=(), writes=()):
        self.flush()
        waits = self._waits(eng, reads, writes)
        self.cnt[eng] += 1
        tok = (self.sem[eng], self.cnt[eng], "c_" + eng, eng)
        self._emit(eng, waits, fn, self.sem[eng], 1)
        self._commit(tok, reads, writes)
        return tok

    def dma(self, eng, out, in_, reads=(), writes=(), dsem=None, ring=False):
        if not ring:
            self.flush()
        extra = []
        if dsem is None:
            if self.pool is None:
                self.pool = {"sp": [self.new_dsem() for _ in range(self.NPOOL)],
                             "pool": [self.new_dsem() for _ in range(16)],
                             "act": [self.new_dsem() for _ in range(4)]}
                self.pool_i = {"sp": 0, "pool": 0, "act": 0}
            pl = self.pool[eng]
            dsem = pl[self.pool_i[eng] % len(pl)]
            self.pool_i[eng] += 1
            key = "d_%d" % dsem.idx
            if dsem.total > self.seen.get((eng, key), 0):
                self.seen[(eng, key)] = dsem.total
                extra.append((dsem.h, dsem.total))
        waits = extra + self._waits(eng, reads, writes)
        dsem.total += 16
        tok = (dsem.h, dsem.total, "d_%d" % dsem.idx, "dma")
        self._emit(eng, waits, lambda e: e.dma_start(out=out, in_=in_, allow_slow_non_contiguous=True), dsem.h, 16)
        self._commit(tok, reads, writes)
        return tok

    def chain(self, eng, fns, reads=(), writes=()):
        cr = Res("chain")
        tok = None
        for fn in fns:
            tok = self.op(eng, fn, reads=list(reads) + [cr], writes=list(writes) + [cr])
        return tok

    def custom(self, eng, fn, sem_h, inc, key, total, reads=(), writes=()):
        self.flush()
        waits = self._waits(eng, reads, writes)
        tok = (sem_h, total, key, "dma")
        self._emit(eng, waits, fn, sem_h, inc)
        self._commit(tok, reads, writes)
        return tok

    def barrier(self):
        self.flush()
        self.pend = (dict(self.cnt), [(d, d.total) for d in self.dsems])

    def flush(self):
        if self.pend is None:
            return
        cnt, dtot = self.pend
        self.pend = None
        for e in self.ENG:
            waits = []
            for x in self.ENG:
                if x == e:
                    continue
                key = "c_" + x
                if cnt[x] > self.seen.get((e, key), 0):
                    self.seen[(e, key)] = cnt[x]
                    waits.append((self.sem[x], cnt[x]))
            for d, tot in dtot:
                key = "d_%d" % d.idx
                if tot > self.seen.get((e, key), 0):
                    self.seen[(e, key)] = tot
                    waits.append((d.h, tot))
            self._emit(e, waits, None, None, 0)


def alibi_slopes():
    return np.array([2.0 ** (-8.0 * (i + 1) / 12) for i in range(12)], np.float64).reshape(3, 4)


def host_consts():
    bf = ml_dtypes.bfloat16
    c = {}
    c["identb"] = np.eye(128, dtype=np.float32).astype(bf)
    c["identf"] = np.eye(128, dtype=np.float32)
    c["onesb"] = np.ones((128, 128), np.float32).astype(bf)
    c["onesf"] = np.ones((128, 64), np.float32)
    c["tril"] = np.tril(np.ones((128, 128), np.float32))
    sl = alibi_slopes()
    k = np.arange(128)[:, None]
    q = np.arange(128)[None, :]
    ep = np.zeros((128, 3, 4, 2, 128), np.float64)
    for g, d in enumerate(DILS):
        for h in range(4):
            dist0 = q + 128 - k
            ep[:, g, h, 0, :] = np.where(k >= q, np.exp(-sl[g, h] * d * dist0), 0.0)
            dist1 = q - k
            ep[:, g, h, 1, :] = np.where(k <= q, np.exp(-sl[g, h] * d * dist1), 0.0)
    c["ep"] = ep.reshape(128, 3 * 4 * 2 * 128).astype(np.float32).astype(bf)
    es_ = np.zeros((128, 9, 4, 4), np.float64)
    p = np.arange(128)
    for h in range(4):
        for t in range(4):
            tap = 128 + t - p
            es_[:, 0, h, t] = np.where(p >= t, np.exp(-sl[0, h] * 1 * tap), 0.0)
            tap = 128 - p
            es_[:, 1 + t, h, t] = np.exp(-sl[1, h] * 4 * tap)
            es_[:, 5 + t, h, t] = np.exp(-sl[2, h] * 16 * tap)
    c["es"] = es_.reshape(128, 144).astype(np.float32).astype(bf)
    en = np.zeros((16, 3, 4, 16), np.float64)
    for b in range(4):
        for t2 in range(4):
            for t in range(4):
                if t2 > t:
                    continue
                for g, d in enumerate(DILS):
                    if (t - t2) % d != 0 or (t - t2) // d > 128:
                        continue
                    for h in range(4):
                        en[b * 4 + t2, g, h, b * 4 + t] = np.exp(-sl[g, h] * (t - t2))
    c["en"] = en.reshape(16, 192).astype(np.float32).astype(bf)
    return c


def build():
    nc = bass.Bass("TRN2", target_bir_lowering=False)
    es = ExitStack()
    P = Prog(nc, es)

    def din(name, shape, dt=F32):
        return nc.dram_tensor(name, list(shape), dt, kind="ExternalInput")

    def dout(name, shape, dt=F32):
        return nc.dram_tensor(name, list(shape), dt, kind="ExternalOutput")

    xp_d = din("xp", [NPR, D]); xs_d = din("xs", [NSM, D])
    ck_d = din("ck", [DEPTH, 4, 2048, 256]); cv_d = din("cv", [DEPTH, 4, 2048, 256])
    sc_d = din("sc", [DEPTH, 8, 512])
    gpm_d = din("g_pre_mix", [DEPTH, D]); gqm_d = din("g_post_mix", [DEPTH, D])
    gpf_d = din("g_pre_ffn", [DEPTH, D]); gqf_d = din("g_post_ffn", [DEPTH, D])
    win_d = din("w_in", [DEPTH, D, INW])
    alg_d = din("a_ln_g", [DEPTH, 512]); alb_d = din("a_ln_b", [DEPTH, 512])
    aws_d = din("a_ws", [DEPTH, 4, 128, 128]); abs_d = din("a_bs", [DEPTH, 4, 128])
    cw_d = din("c_conv_w", [DEPTH, 3, 512]); cbias_d = din("c_conv_b", [DEPTH, 512])
    wba_d = din("w_br_a", [DEPTH, 512, D]); wbb_d = din("w_br_b", [DEPTH, 256, D]); wbc_d = din("w_br_c", [DEPTH, 512, D])
    bg_d = din("b_gate", [DEPTH, 3 * D]); wo_d = din("w_o", [DEPTH, D, D])
    wg_d = din("w_ff_gate", [DEPTH, D, DFF]); wu_d = din("w_ff_up", [DEPTH, D, DFF]); wd_d = din("w_ff_down", [DEPTH, DFF, D])
    flags_d = din("flags", [1, 2])
    xp1_d = din("xp1", [NPR, D]); xp2_d = din("xp2", [NPR, D])
    identb_d = din("identb", [128, 128], BF16); identf_d = din("identf", [128, 128])
    onesb_d = din("onesb", [128, 128], BF16); onesf_d = din("onesf", [128, 64])
    tril_d = din("tril", [128, 128]); ep_d = din("ep", [128, 3072], BF16)
    es_d = din("es", [128, 144], BF16); en_d = din("en", [16, 192], BF16)

    yp_d = dout("yp", [NPR, D]); ys_d = dout("ys", [NSM, D])
    kv_d = dout("kv", [DEPTH, NT, 512]); zo_d = dout("zo", [DEPTH, 10, 512]); vn_d = dout("vns", [DEPTH, NSM, 512])

    bounce_d = [nc.dram_tensor("bounce%d" % i, [128, XC], BF16) for i in range(4)]
    xpark_ds = [nc.dram_tensor("xpark%d" % i, [128, 5 * 8 * 512], F32) for i in range(3)]

    def sb(name, shape, dt=F32):
        return es.enter_context(nc.sbuf_tensor(name, list(shape), dt))

    identb = sb("identb_s", [128, 128], BF16); identf = sb("identf_s", [128, 128])
    onesb = sb("onesb_s", [128, 128], BF16); onesf = sb("onesf_s", [128, 64])
    tril = sb("tril_s", [128, 128]); ep = sb("ep_s", [128, 3, 4, 2, 128], BF16)
    es_s = sb("es_s", [128, 9, 4, 4], BF16); en_s = sb("en_s", [16, 3, 4, 16], BF16)
    flagb = sb("flagb", [128, 2])
    vecs = sb("vecs", [128, DEPTH, 4, 8])
    bgc = sb("bgc", [128, DEPTH, 24])
    cwc = sb("cwc", [128, DEPTH, 4, 4])
    lng = sb("lng", [128, 512]); lnb = sb("lnb", [128, 512])
    bsrf = sb("bsrf", [1, 4, 128]); bsr = sb("bsr", [1, 4, 128], BF16)
    wst = sb("wst", [128, 4, 128], BF16)
    wss = sb("wss", [16, 4, 16], BF16)
    bss = sb("bss", [1, 4, 16], BF16)
    RING_N = 3
    ring = [sb("ring%d" % i, [128, 4608], BF16) for i in range(RING_N)]
    ring_res = [None] * RING_N
    ring_sem = [None] * RING_N
    ring_i = [0]
    ARN = 83760
    AR = sb("arena", [128, ARN], BF16)

    banks = [es.enter_context(nc.psum_tensor("bank%d" % i, [128, 512], F32)) for i in range(8)]
    bank_res = [None] * 8
    bank_i = [0]

    def bank():
        i = bank_i[0] % 8
        bank_i[0] += 1
        return banks[i], bank_res[i]

    def ring_slot():
        i = ring_i[0] % RING_N
        ring_i[0] += 1
        return i

    def wload(src_ap, slot=None, off=0):
        i = ring_slot() if slot is None else slot
        n = 1
        for s in src_ap.shape[1:]:
            n *= s
        dst = ring[i][0:src_ap.shape[0], off:off + n]
        if len(src_ap.shape) == 3:
            dst = dst.rearrange("p (a b) -> p a b", a=src_ap.shape[1])
        elif len(src_ap.shape) == 4:
            dst = dst.rearrange("p (a b c) -> p a b c", a=src_ap.shape[1], b=src_ap.shape[2])
        P.dma("pool", dst, src_ap, writes=[ring_res[i]], dsem=ring_sem[i], ring=True)
        return dst, ring_res[i], i

    def win_cols(l, c0, n):
        return win_d[l, :, c0:c0 + n].rearrange("(k p) c -> p k c", p=128)

    class Carver:
        def __init__(self, base):
            self.o = base
        def take(self, nelem, dt=BF16, parts=128):
            if dt == F32:
                v = AR[0:parts, self.o:self.o + 2 * nelem].bitcast(F32)
                self.o += 2 * nelem
            else:
                v = AR[0:parts, self.o:self.o + nelem]
                self.o += nelem
            assert self.o <= ARN, self.o
            return v

    xparks = [x.ap().rearrange("p (n k t) -> p n k t", n=5, k=8) for x in xpark_ds]
    XTrs = [[None] * 5 for _ in range(3)]

    def ntile_of(tok):
        return min(tok // 512, 4)

    def program():
      for i in range(RING_N):
          ring_res[i] = Res("ring%d" % i)
          ring_sem[i] = P.new_dsem()
      ring_i[0] = 0
      for i in range(8):
          bank_res[i] = Res("bank%d" % i)
      bank_i[0] = 0
      for q_ in range(3):
          XTrs[q_][:] = [Res("XT%d_%d" % (q_, i)) for i in range(5)]
      try:
          program_body()
      except Stop:
          P.barrier()
      P.flush()

    CI = [0]

    def mark(k):
        if KSTOP == k or KSTOP == CI[0] * 20 + k:
            raise Stop()

    def program_body():
        CI[0] = -10
        cres = Res("consts")
        for dst, src in ((identb, identb_d), (identf, identf_d), (onesb, onesb_d), (onesf, onesf_d), (tril, tril_d)):
            P.dma("sp", dst[:, :], src[:, :], writes=[cres])
        P.dma("sp", ep[:].rearrange("p a b c d -> p (a b c d)"), ep_d[:, :], writes=[cres])
        P.dma("sp", es_s[:].rearrange("p a b c -> p (a b c)"), es_d[:, :], writes=[cres])
        P.dma("sp", en_s[:].rearrange("p a b c -> p (a b c)"), en_d[:, :], writes=[cres])
        P.dma("sp", flagb[:, :], flags_d[0:1, :].partition_broadcast(128), writes=[cres])
        with nc.allow_non_contiguous_dma(reason="tiny per-partition parameter columns"):
            for wi, gd in enumerate((gpm_d, gqm_d, gpf_d, gqf_d)):
                for l in range(DEPTH):
                    P.dma("sp", vecs[:, l, wi, :], gd[l, :].rearrange("(k p) -> p k", p=128), writes=[cres])
            for l in range(DEPTH):
                P.dma("sp", bgc[:, l, :], bg_d[l, :].rearrange("(k p) -> p k", p=128), writes=[cres])
                for i in range(3):
                    P.dma("sp", cwc[:, l, i, :], cw_d[l, i, :].rearrange("(k p) -> p k", p=128), writes=[cres])
                P.dma("sp", cwc[:, l, 3, :], cbias_d[l, :].rearrange("(k p) -> p k", p=128), writes=[cres])
        P.barrier()
        mark(1)

        out_sem = None

        cv0 = Carver(0)
        xst = [cv0.take(1024, F32) for _ in range(2)]
        xst_res = [Res("xst0"), Res("xst1")]
        xt = [cv0.take(4096, F32).rearrange("p (k t) -> p k t", k=8) for _ in range(2)]
        xtr = [Res("xt0"), Res("xt1")]
        for (xsrc_d, xpark, XTr) in ((xp2_d, xparks[2], XTrs[2]), (xp1_d, xparks[1], XTrs[1]), (xp_d, xparks[0], XTrs[0])):
          for ti, (t0, m) in enumerate(TTILES):
            s = ti % 2
            ni = ntile_of(t0)
            xs_ = ni % 2
            lc = t0 - NTILES[ni][0]
            src = xsrc_d[t0:t0 + m, :] if ti < 16 else xs_d[:, :]
            P.dma("sp", xst[s][0:m, :], src, writes=[xst_res[s]])
            for half in range(2):
                bk, br = bank()
                def tr(e, s=s, m=m, half=half, bk=bk):
                    ins = None
                    for j in range(4):
                        k = half * 4 + j
                        ins = e.transpose(bk[:, j * 128:j * 128 + m], xst[s][0:m, k * 128:(k + 1) * 128], identf[0:m, 0:m])
                    return ins
                P.op("pe", tr, reads=[xst_res[s], cres], writes=[br])
                P.op("dve", lambda e, half=half, bk=bk, lc=lc, m=m, xs_=xs_: e.tensor_copy(
                    out=xt[xs_][:, half * 4:half * 4 + 4, lc:lc + m],
                    in_=bk[:, :].rearrange("p (j t) -> p j t", j=4)[:, :, 0:m]), reads=[br], writes=[xtr[xs_]])
            if lc + m == NTILES[ni][1]:
                n = NTILES[ni][1]
                P.dma("act", xpark[:, ni, :, 0:n], xt[xs_][:, :, 0:n], reads=[xtr[xs_]], writes=[XTr[ni]])
        P.barrier()

        mark(2)
        TWIN = {}

        def norm_stats(src3, srcr, n, sqb, sqr, rs, rsr):
            sqr_b = TWIN.setdefault(id(sqr), Res("sq_twin"))
            P.op("act", lambda e: e.activation(out=sqb[:, 0:4, 0:n], in_=src3[:, 0:4, :], func=AF.Square), reads=[srcr], writes=[sqr])
            P.op("dve", lambda e: e.tensor_tensor(out=sqb[:, 4:8, 0:n], in0=src3[:, 4:8, :], in1=src3[:, 4:8, :], op=ALU.mult), reads=[srcr], writes=[sqr_b])
            bk, br = bank()
            def mm(e):
                ins = None
                for k in range(8):
                    ins = e.matmul(bk[:, 0:n], onesb[:, :], sqb[:, k, 0:n], start=(k == 0), stop=(k == 7))
                return ins
            P.op("pe", mm, reads=[sqr, sqr_b, cres], writes=[br])
            P.op("act", lambda e: e.activation(out=rs[:, 0:n], in_=bk[:, 0:n], func=AF.Sqrt, bias=EPS, scale=1.0 / D),
                 reads=[br], writes=[rsr])
            P.op("dve", lambda e: e.reciprocal(out=rs[:, 0:n], in_=rs[:, 0:n]), reads=[rsr], writes=[rsr])

        CUR = [None, None]

        def rmsnorm_h(gcol, hT, hTr, sqb, sqr, rs, rsr, xt, xtr, tiles=(0, 1, 2, 3, 4)):
            xpark, XTr = CUR
            for ni in tiles:
                c0, n = NTILES[ni]
                s = ni % 2
                P.dma("sp", xt[s][:, :, 0:n], xpark[:, ni, :, 0:n], reads=[XTr[ni]], writes=[xtr[s]])
                norm_stats(xt[s][:, :, 0:n], xtr[s], n, sqb[s], sqr[s], rs[s], rsr[s])
                def sc(e, c0=c0, n=n, s=s):
                    ins = None
                    for k in range(8):
                        ins = e.scalar_tensor_tensor(out=hT[:, k, c0:c0 + n], in0=xt[s][:, k, 0:n], scalar=gcol[:, k:k + 1],
                                                     in1=rs[s][:, 0:n], op0=ALU.mult, op1=ALU.mult)
                    return ins
                P.op("dve", sc, reads=[xtr[s], rsr[s], cres], writes=[hTr[ni]])

        def proj_fm(wslot, wres, wc0, hT, hTr, ni, kch=8):
            c0, n = NTILES[ni]
            bk, br = bank()
            def mm(e):
                ins = None
                for k in range(kch):
                    ins = e.matmul(bk[:, 0:n], wslot[:, k, wc0:wc0 + 128], hT[:, k, c0:c0 + n], start=(k == 0), stop=(k == kch - 1))
                return ins
            P.op("pe", mm, reads=[wres, hTr[ni]], writes=[br])
            return bk, br

        XTL = set()

        def post_norm_residual(mixf, mixr, gcol, ni, sqb, sqr, rs, rsr, xt, xtr, nxt=None):
            xpark, XTr = CUR
            c0, n = NTILES[ni]
            s = ni % 2
            key = (id(xtr[0]), ni)
            if key in XTL:
                XTL.discard(key)
            else:
                P.dma("sp", xt[s][:, :, 0:n], xpark[:, ni, :, 0:n], reads=[XTr[ni]], writes=[xtr[s]])
            if nxt is not None:
                n2 = NTILES[nxt][1]
                XTL.add((id(xtr[0]), nxt))
                P.dma("sp", xt[nxt % 2][:, :, 0:n2], xpark[:, nxt, :, 0:n2], reads=[XTr[nxt]], writes=[xtr[nxt % 2]])
            sqb, sqr, rs, rsr = sqb[s], sqr[s], rs[s], rsr[s]
            norm_stats(mixf[:, :, 0:n], mixr, n, sqb, sqr, rs, rsr)
            def sc1(e):
                ins = None
                for k in range(8):
                    ins = e.scalar_tensor_tensor(out=mixf[:, k, 0:n], in0=mixf[:, k, 0:n], scalar=gcol[:, k:k + 1],
                                                 in1=rs[:, 0:n], op0=ALU.mult, op1=ALU.mult)
                return ins
            P.chain("dve", [sc1, lambda e: e.tensor_tensor(out=xt[s][:, :, 0:n], in0=xt[s][:, :, 0:n], in1=mixf[:, :, 0:n], op=ALU.add)],
                    reads=[rsr, cres], writes=[mixr, xtr[s]])
            P.dma("sp", xpark[:, ni, :, 0:n], xt[s][:, :, 0:n], reads=[xtr[s]], writes=[XTr[ni]])

        def layer(l, q_, bin_d, fcol, bout_d, mode, outputs):
            CUR[0], CUR[1] = xparks[q_], XTrs[q_]
            CI[0] = CI[0] + 1 if CI[0] >= 0 else 0
            cvA = Carver(0)
            hT = cvA.take(8 * NT).rearrange("p (k t) -> p k t", k=8); hTr = [Res("hT%d" % i) for i in range(5)]
            outb = cvA.take(4 * NT).rearrange("p (h t) -> p h t", h=4); outbr = Res("outb")
            zp = cvA.take(4 * 2050).rearrange("p (c t) -> p c t", c=4); zpr = Res("zp")
            zs = cvA.take(4 * 24).rearrange("p (c b t) -> p c b t", c=4, b=4); zsr = Res("zs")
            zsel = cvA.take(40, F32).rearrange("p (c t) -> p c t", c=4)
            base3 = cvA.o
            KT = cvA.take(2 * 4096).rearrange("p (h t) -> p h t", h=2); KTr = Res("KT")
            KTs = cvA.take(2 * 16).rearrange("p (h t) -> p h t", h=2)
            hx = cvA.take(5472); hxr = Res("hx")
            vh = hx[:, 0:5460].rearrange("p (j h d) -> p j h d", j=21, h=4)
            baseU = cvA.o
            cvP = Carver(baseU)
            kvst = [cvP.take(512, F32) for _ in range(2)]; kvstr = [Res("kvst0"), Res("kvst1")]
            gst = cvP.take(8 * 512).rearrange("p (r c) -> p r c", r=8); gstr = Res("gst")
            tmpb = cvP.take(512); tmpr = Res("tmpb")
            sqb = [cvP.take(8 * 512).rearrange("p (k t) -> p k t", k=8) for _ in range(2)]; sqr = [Res("sqb0"), Res("sqb1")]
            rs = [cvP.take(512, F32) for _ in range(2)]; rsr = [Res("rs0"), Res("rs1")]
            xt = [cvP.take(4096, F32).rearrange("p (k t) -> p k t", k=8) for _ in range(2)]; xtr = [Res("xt0"), Res("xt1")]

            gpm = vecs[:, l, 0, :]; gqm = vecs[:, l, 1, :]; gpf = vecs[:, l, 2, :]; gqf = vecs[:, l, 3, :]

            rmsnorm_h(gpm, hT, hTr, sqb, sqr, rs, rsr, xt, xtr, tiles=(0, 1))
            mark(3)
            if mode == "full":
                wf32 = cvP.take(128, F32); wf32r = Res("wf32")
                gres_ = Res("gmlp_consts")
                P.dma("sp", lng[:, :], alg_d[l:l + 1, :].partition_broadcast(128), writes=[gres_])
                P.dma("sp", lnb[:, :], alb_d[l:l + 1, :].partition_broadcast(128), writes=[gres_])
                P.dma("sp", bsrf[0:1, :, :], abs_d[l:l + 1, :, :], writes=[gres_])
                P.op("dve", lambda e: e.tensor_copy(out=bsr[:], in_=bsrf[:]), reads=[gres_], writes=[gres_])
                wsf = [cvP.take(128, F32) for _ in range(4)]; wsfr = [Res("wsf%d" % g) for g in range(4)]
                wsb = [cvP.take(128) for _ in range(4)]; wsbr = [Res("wsb%d" % g) for g in range(4)]
                for g in range(4):
                    P.dma("sp", wsf[g][:, :], aws_d[l, g, :, :], writes=[wsfr[g]])
                    P.op("dve", lambda e, g=g: e.tensor_tensor(out=wsb[g][:, :], in0=wsf[g][:, :], in1=tril[:, :], op=ALU.mult), reads=[wsfr[g], cres], writes=[wsbr[g]])
            wkv, wkvr, _ = wload(win_cols(l, C_K, 512))
            for ni, (c0, n) in enumerate(NTILES):
                if ni + 2 < 5:
                    rmsnorm_h(gpm, hT, hTr, sqb, sqr, rs, rsr, xt, xtr, tiles=(ni + 2,))
                for ti, (t0, m) in enumerate(TTILES):
                    if not outputs or ntile_of(t0) != ni:
                        continue
                    bk, br = bank()
                    def mm(e, t0=t0, m=m, bk=bk):
                        ins = None
                        for k in range(8):
                            ins = e.matmul(bk[0:m, :], hT[:, k, t0:t0 + m], wkv[:, k, :], start=(k == 0), stop=(k == 7))
                        return ins
                    P.op("pe", mm, reads=[wkvr, hTr[ni]], writes=[br])
                    s = ti % 2
                    P.op("act", lambda e, m=m, bk=bk, s=s: e.copy(out=kvst[s][0:m, :], in_=bk[0:m, :]), reads=[br], writes=[kvstr[s]])
                    P.dma("sp", kv_d[l, t0:t0 + m, :], kvst[s][0:m, :], reads=[kvstr[s]], dsem=out_sem)
                for hp in range(2):
                    bk, br = proj_fm(wkv, wkvr, hp * 128, hT, hTr, ni)
                    if ni < 4:
                        P.op("dve", lambda e, bk=bk, hp=hp, c0=c0, n=n: e.tensor_copy(out=KT[:, hp, 2048 + c0:2048 + c0 + n], in_=bk[:, 0:n]),
                             reads=[br], writes=[KTr])
                    else:
                        P.op("dve", lambda e, bk=bk, hp=hp: e.tensor_copy(out=KTs[:, hp, :], in_=bk[:, 0:16]), reads=[br], writes=[KTr])
            mark(4)
            P.op("dve", lambda e: e.memset(hx[:, :], 1.0), writes=[hxr])
            j = 0
            for g, d in enumerate(DILS):
                nb = 16 // d
                for r in range(d):
                    st = 128 * (nb - 1) * d + r
                    bk, br = bank()
                    def mm(e, st=st, d=d, bk=bk):
                        ins = None
                        for k in range(8):
                            ins = e.matmul(bk[:, 0:256], hT[:, k, st:st + 127 * d + 1:d], wkv[:, k, 256:512], start=(k == 0), stop=(k == 7))
                        return ins
                    P.op("pe", mm, reads=[wkvr] + hTr[0:4], writes=[br])
                    P.op("act", lambda e, bk=bk, j=j: e.copy(out=vh[:, j, :, 0:64], in_=bk[:, 0:256].rearrange("p (h d) -> p h d", h=4)),
                         reads=[br], writes=[hxr])
                    j += 1
            wcx, wcxr, _ = wload(win_cols(l, C_CX, 512))
            wcc, wccr, _ = wload(win_cols(l, C_CC, 512))
            for c in range(4):
                for ni, (c0, n) in enumerate(NTILES):
                    if mode == "halo" and ni != 3:
                        continue
                    bka, bra = proj_fm(wcx, wcxr, c * 128, hT, hTr, ni)
                    bkb, brb = proj_fm(wcc, wccr, c * 128, hT, hTr, ni)
                    P.op("act", lambda e, bka=bka, n=n: e.copy(out=tmpb[:, 0:n], in_=bka[:, 0:n]), reads=[bra], writes=[tmpr])
                    if ni < 4:
                        P.op("dve", lambda e, bkb=bkb, c=c, c0=c0, n=n: e.tensor_tensor(out=zp[:, c, 2 + c0:2 + c0 + n], in0=bkb[:, 0:n], in1=tmpb[:, 0:n], op=ALU.mult),
                             reads=[brb, tmpr], writes=[zpr])
                        if ni == 3:
                            P.op("dve", lambda e, bkb=bkb, c=c: e.tensor_tensor(out=zsel[:, c, 0:2], in0=bkb[:, 510:512], in1=tmpb[:, 510:512], op=ALU.mult),
                                 reads=[brb, tmpr], writes=[zsr])
                    else:
                        def zsm(e, bkb=bkb, c=c):
                            e.tensor_tensor(out=zs[:, c, :, 2:6], in0=bkb[:, 0:16].rearrange("p (b t) -> p b t", b=4),
                                            in1=tmpb[:, 0:16].rearrange("p (b t) -> p b t", b=4), op=ALU.mult)
                            return e.tensor_tensor(out=zsel[:, c, 2:10].rearrange("p (b t) -> p b t", b=4),
                                                   in0=bkb[:, 0:16].rearrange("p (b t) -> p b t", b=4)[:, :, 2:4],
                                                   in1=tmpb[:, 0:16].rearrange("p (b t) -> p b t", b=4)[:, :, 2:4], op=ALU.mult)
                        P.op("dve", zsm, reads=[brb, tmpr], writes=[zsr])
            if mode == "full":
                bk, br = bank()
                def ztr(e, bk=bk):
                    ins = None
                    for c in range(4):
                        ins = e.transpose(bk[0:10, c * 128:(c + 1) * 128], zsel[:, c, :], identf[:, :])
                    return ins
                P.op("pe", ztr, reads=[zsr, cres], writes=[br])
                P.op("act", lambda e, bk=bk: e.copy(out=kvst[0][0:10, :], in_=bk[0:10, :]), reads=[br], writes=[kvstr[0]])
                if outputs:
                    P.dma("sp", zo_d[l, :, :], kvst[0][0:10, :], reads=[kvstr[0]], dsem=out_sem)
                P.dma("sp", kvst[1][0:8, :], sc_d[l, :, :], writes=[kvstr[1]])
                bk, br = bank()
                def str_(e, bk=bk):
                    ins = None
                    for c in range(4):
                        ins = e.transpose(bk[:, c * 8:(c + 1) * 8], kvst[1][0:8, c * 128:(c + 1) * 128], identf[0:8, 0:8])
                    return ins
                P.op("pe", str_, reads=[kvstr[1], cres], writes=[br])
                P.op("dve", lambda e, bk=bk: e.tensor_copy(out=zs[:, :, :, 0:2], in_=bk[:, 0:32].rearrange("p (c b j) -> p c b j", c=4, b=4)),
                     reads=[br], writes=[zsr])

            if mode == "full":
                for g in range(4):
                    bk, br = bank()
                    bkb = bk[:, :].bitcast(BF16)
                    P.op("pe", lambda e, bkb=bkb, g=g: e.transpose(bkb[:, 0:128], wsb[g][:, :], identb[:, :]), reads=[wsbr[g], cres], writes=[br])
                    P.op("act", lambda e, bkb=bkb, g=g: e.copy(out=wst[:, g, :], in_=bkb[:, 0:128]), reads=[br], writes=[gres_])
                P.op("dve", lambda e: e.memset(wss[:, :, :], 0.0), writes=[gres_])
                for g in range(4):
                    for b in range(4):
                        P.dma("sp", wss[b * 4:b * 4 + 4, g, b * 4:b * 4 + 4], wst[0:4, g, 0:4], reads=[gres_, cres], writes=[gres_])
                for b in range(4):
                    P.op("dve", lambda e, b=b: e.tensor_copy(out=bss[0:1, :, b * 4:b * 4 + 4], in_=bsr[0:1, :, 0:4]), reads=[gres_, cres], writes=[gres_])
            mark(5)
            if bout_d is not None:
                bres = Res("bounce")
                for hp in range(2):
                    P.dma("sp", bout_d[:, hp * 2048:(hp + 1) * 2048], KT[:, hp, 2048:4096], reads=[KTr], writes=[bres])
                P.op("dve", lambda e: e.tensor_copy(out=hx[:, 5460:5468].rearrange("p (c t) -> p c t", c=4), in_=zp[:, :, 2048:2050]),
                     reads=[zpr], writes=[hxr])
                P.dma("sp", bout_d[:, 4096:XC], hx[:, :], reads=[hxr], writes=[bres])
            P.barrier()
            if mode == "halo":
                return
            for hp in range(2):
                P.dma("sp", KT[:, hp, 0:2048], bin_d[:, hp * 2048:(hp + 1) * 2048], writes=[KTr])
            P.dma("sp", hx[:, :], bin_d[:, 4096:XC], writes=[hxr])
            for i in range(0, 5472, 1368):
                P.op("dve", lambda e, i=i: e.tensor_scalar(hx[:, i:i + 1368], hx[:, i:i + 1368], fcol, None, ALU.mult),
                     reads=[hxr, cres], writes=[hxr])
            P.op("dve", lambda e: e.tensor_copy(out=zp[:, :, 0:2], in_=hx[:, 5460:5468].rearrange("p (c t) -> p c t", c=4)),
                 reads=[hxr], writes=[zpr])
            P.barrier()

            mark(6)
            cvB = Carver(baseU)
            QT = cvB.take(3 * NT).rearrange("p (g t) -> p g t", g=3); QTr = Res("QT")
            numT = cvB.take(2 * NT, F32).rearrange("p (h t) -> p h t", h=2); numr = Res("numT")
            VAs = cvB.take(130).rearrange("p (h d) -> p h d", h=2)
            VT = cvB.take(NT); VTr = Res("VT")
            baseV = cvB.o
            for hp in range(2):
                cvB = Carver(baseV)
                VA = [cvB.take(n * 130).rearrange("p (j h d) -> p j h d", j=n, h=2) for n in (17, 20, 32)]; VAr = Res("VA")
                pexp = [cvB.take(512) for _ in range(2)]; pexpr = [Res("pexp0"), Res("pexp1")]
                ptm = [cvB.take(512) for _ in range(2)]; ptr_ = [Res("pt0"), Res("pt1")]
                wqa, wqar, _ = wload(win_cols(l, C_Q, 512))
                wqb, wqbr, _ = wload(win_cols(l, C_Q + 512, 256))
                for g in range(3):
                    qc = g * 256 + hp * 128
                    ws_, wr_, wc0 = (wqa, wqar, qc) if qc < 512 else (wqb, wqbr, qc - 512)
                    for ni, (c0, n) in enumerate(NTILES):
                        bk, br = proj_fm(ws_, wr_, wc0, hT, hTr, ni)
                        P.op("act", lambda e, bk=bk, g=g, c0=c0, n=n: e.copy(out=QT[:, g, c0:c0 + n], in_=bk[:, 0:n]), reads=[br], writes=[QTr])
                wv, wvr, _ = wload(win_cols(l, C_V + hp * 128, 128))
                if outputs:
                    cvK = Carver(baseV + 11018)
                    kc = cvK.take(36 * 128).rearrange("p (j c) -> p j c", j=36)
                    vc = cvK.take(36 * 128).rearrange("p (j c) -> p j c", j=36)
                    if hp == 0:
                        kcr = Res("kc"); vcr = Res("vc")
                    cache_loads = []
                    for b in range(4):
                        for (cd, dst, rr) in ((ck_d, kc, kcr), (cv_d, vc, vcr)):
                            cs = slice(hp * 128, hp * 128 + 128)
                            cache_loads.append((dst[:, b * 9, :], cd[l, b, 1920:2048, cs], rr))
                            cache_loads.append((dst[:, b * 9 + 1:b * 9 + 5, :], cd[l, b, 1536:2048, cs].rearrange("(p r) c -> p r c", r=4), rr))
                            cache_loads.append((dst[:, b * 9 + 5:b * 9 + 9, :], cd[l, b, :, cs].rearrange("(p s) c -> p s c", s=16)[:, 0:4, :], rr))
                else:
                    cache_loads = []
                for g in range(3):
                    P.op("dve", lambda e, g=g: e.memset(VA[g][:, :, :, 64:65], 1.0), writes=[VAr])
                P.op("dve", lambda e: e.memset(VAs[0:16, :, 64:65], 1.0), writes=[VAr])
                for ni, (c0, n) in enumerate(NTILES):
                    bk, br = proj_fm(wv, wvr, 0, hT, hTr, ni)
                    P.op("act", lambda e, bk=bk, c0=c0, n=n: e.copy(out=VT[:, c0:c0 + n], in_=bk[:, 0:n]), reads=[br], writes=[VTr])
                for g, d in enumerate(DILS):
                    nb = 16 // d
                    blocks = [(r, n) for r in range(d) for n in range(nb)]
                    for b0 in range(0, 16, 4):
                        bk, br = bank()
                        bkb = bk[:, :].bitcast(BF16)
                        def trv(e, b0=b0, bkb=bkb, d=d, blocks=blocks):
                            ins = None
                            for jj in range(4):
                                r, n = blocks[b0 + jj]
                                st = 128 * n * d + r
                                ins = e.transpose(bkb[:, jj * 128:(jj + 1) * 128], VT[:, st:st + 127 * d + 1:d], identb[:, :])
                            return ins
                        P.op("pe", trv, reads=[VTr, cres], writes=[br])
                        r0, n0 = blocks[b0]
                        t_first = r0 * (nb + 1) + n0 + 1
                        step = 1 if nb >= 4 else (nb + 1)
                        P.op("dve", lambda e, bkb=bkb, g=g, t_first=t_first, step=step: e.tensor_copy(
                            out=VA[g][:, t_first:t_first + 3 * step + 1:step, :, 0:64],
                            in_=bkb[:, 0:512].rearrange("p (j h d) -> p j h d", j=4, h=2)), reads=[br], writes=[VAr])
                bk, br = bank()
                bkb = bk[:, :].bitcast(BF16)
                P.op("pe", lambda e, bkb=bkb: e.transpose(bkb[0:16, 0:128], VT[:, 2048:2064], identb[:, :]), reads=[VTr, cres], writes=[br])
                P.op("dve", lambda e, bkb=bkb: e.tensor_copy(out=VAs[0:16, :, 0:64], in_=bkb[0:16, 0:128].rearrange("p (h d) -> p h d", h=2)),
                     reads=[br], writes=[VAr])
                P.op("dve", lambda e, hp=hp: e.tensor_copy(out=VA[0][:, 0, :, :], in_=vh[:, 0, hp * 2:hp * 2 + 2, :]), reads=[hxr], writes=[VAr])
                P.op("dve", lambda e, hp=hp: e.tensor_copy(out=VA[1][:, 0:20:5, :, :], in_=vh[:, 1:5, hp * 2:hp * 2 + 2, :]), reads=[hxr], writes=[VAr])
                P.op("dve", lambda e, hp=hp: e.tensor_copy(out=VA[2][:, 0:32:2, :, :], in_=vh[:, 5:21, hp * 2:hp * 2 + 2, :]), reads=[hxr], writes=[VAr])

                mark(7)
                blks = []
                for g, d in enumerate(DILS):
                    nb = 16 // d
                    for r in range(d):
                        for n in range(nb):
                            st = 128 * n * d + r
                            blks.append(dict(
                                g=g, tl=r * (nb + 1) + n, s=len(blks) % 2,
                                sl_q=slice(st, st + 127 * d + 1, d),
                                sl_k=[slice(2048 + st - 128 * d, 2048 + st - 128 * d + 127 * d + 1, d),
                                      slice(2048 + st, 2048 + st + 127 * d + 1, d)]))
                pexh = [[Res("pexp%d_%d" % (s_, h_)) for h_ in range(2)] for s_ in range(2)]

                def S_stage(B):
                    s = B["s"]
                    for hh in range(2):
                        bk, br = bank()
                        def mmS(e):
                            ins = None
                            for kb in range(2):
                                ins = e.matmul(bk[:, kb * 128:(kb + 1) * 128],
                                               KT[hh * 64:(hh + 1) * 64, hp, B["sl_k"][kb]], QT[hh * 64:(hh + 1) * 64, B["g"], B["sl_q"]],
                                               start=True, stop=True)
                            return ins
                        P.op("pe", mmS, reads=[KTr, QTr], writes=[br])
                        P.op("act", lambda e: e.activation(out=pexp[s][:, hh * 256:(hh + 1) * 256], in_=bk[:, 0:256], func=AF.Exp, scale=0.125),
                             reads=[br], writes=[pexh[s][hh]])

                def M_stage(B):
                    s = B["s"]
                    P.op("dve" if B["s"] == 0 else "pool", lambda e: e.tensor_tensor(
                        out=ptm[s][:, :], in0=pexp[s][:, :],
                        in1=ep[:, B["g"], hp * 2:hp * 2 + 2, :, :].rearrange("p h k q -> p (h k q)"), op=ALU.mult),
                        reads=[pexh[s][0], pexh[s][1], cres], writes=[ptr_[s]])

                def V_stage(B):
                    s = B["s"]
                    bk2, br2 = bank()
                    def mmV(e):
                        ins = None
                        for hh in range(2):
                            for kb in range(2):
                                ins = e.matmul(bk2[0:65, hh * 128:(hh + 1) * 128], VA[B["g"]][:, B["tl"] + kb, hh, :],
                                               ptm[s][:, (hh * 2 + kb) * 128:(hh * 2 + kb + 1) * 128],
                                               start=(kb == 0), stop=(kb == 1))
                        return ins
                    P.op("pe", mmV, reads=[VAr, ptr_[s]], writes=[br2])
                    if B["g"] == 0:
                        P.op("dve", lambda e: e.tensor_copy(
                            out=numT[0:65, :, B["sl_q"]], in_=bk2[0:65, 0:256].rearrange("p (h q) -> p h q", h=2)),
                            reads=[br2], writes=[numr])
                    else:
                        P.op("dve", lambda e: e.tensor_tensor(
                            out=numT[0:65, :, B["sl_q"]], in0=numT[0:65, :, B["sl_q"]],
                            in1=bk2[0:65, 0:256].rearrange("p (h q) -> p h q", h=2), op=ALU.add),
                            reads=[br2], writes=[numr])

                NB_ = len(blks)
                for i in range(NB_ + 2):
                    if i % 2 == 0 and cache_loads:
                        dst_, src_, rr_ = cache_loads.pop(0)
                        P.dma("pool", dst_, src_, writes=[rr_])
                    if i < NB_:
                        S_stage(blks[i])
                    if 0 <= i - 1 < NB_:
                        M_stage(blks[i - 1])
                    if 0 <= i - 2 < NB_:
                        V_stage(blks[i - 2])
                while cache_loads:
                    dst_, src_, rr_ = cache_loads.pop(0)
                    P.dma("pool", dst_, src_, writes=[rr_])
                P.barrier()
                mark(8)
                if not outputs:
                    P.op("dve", lambda e: e.memset(numT[0:65, :, 2048:2064], 1.0), writes=[numr])
                else:
                    cvS = Carver(baseV)
                    ktc = cvS.take(36 * 128).rearrange("p (j c) -> p j c", j=36); ktcr = Res("ktc")
                    vas = cvS.take(36 * 130).rearrange("p (j h d) -> p j h d", j=36, h=2); vasr = Res("vas")
                    psm = cvS.take(288); psmr = Res("psm")
                    psn = cvS.take(96); psnr = Res("psn")
                    P.op("dve", lambda e: e.memset(vas[:, :, :, 64:65], 1.0), writes=[vasr])
                    P.op("dve", lambda e: e.tensor_copy(out=vas[:, :, :, 0:64], in_=vc[:, :, :].rearrange("p j (h d) -> p j h d", h=2)),
                         reads=[vcr], writes=[vasr])
                    for j0 in range(0, 36, 8):
                        nj = min(8, 36 - j0)
                        bk, br = bank()
                        bkb = bk[:, :].bitcast(BF16)
                        def trk(e, bkb=bkb, j0=j0, nj=nj):
                            ins = None
                            for jj in range(nj):
                                ins = e.transpose(bkb[:, jj * 128:(jj + 1) * 128], kc[:, j0 + jj, :], identb[:, :])
                            return ins
                        P.op("pe", trk, reads=[kcr, cres], writes=[br])
                        P.op("act", lambda e, bkb=bkb, j0=j0, nj=nj: e.copy(out=ktc[:, j0:j0 + nj, :], in_=bkb[:, 0:nj * 128].rearrange("p (j c) -> p j c", j=nj)),
                             reads=[br], writes=[ktcr])
                    for bh in range(2):
                      for hh in range(2):
                        bk, br = bank()
                        def mmS2(e, bk=bk, bh=bh, hp=hp, hh=hh):
                            ins = None
                            for bb in range(2):
                                b = bh * 2 + bb
                                for tile in range(9):
                                    g = 0 if tile == 0 else (1 if tile < 5 else 2)
                                    col = (bb * 9 + tile) * 4
                                    ins = e.matmul(bk[:, col:col + 4], ktc[hh * 64:(hh + 1) * 64, b * 9 + tile, :],
                                                   QT[hh * 64:(hh + 1) * 64, g, 2048 + b * 4:2048 + b * 4 + 4], start=True, stop=True)
                            return ins
                        P.op("pe", mmS2, reads=[ktcr, QTr], writes=[br])
                        P.op("act", lambda e, bk=bk, bh=bh, hh=hh: e.activation(
                            out=psm[:, bh * 144:(bh + 1) * 144].rearrange("p (j h t) -> p j h t", j=18, h=2)[:, :, hh, :],
                            in_=bk[:, 0:72].rearrange("p (j t) -> p j t", j=18), func=AF.Exp, scale=0.125),
                             reads=[br], writes=[psmr])
                    for b in range(4):
                        P.op("dve", lambda e, b=b, hp=hp: e.tensor_tensor(
                            out=psm[:, b * 72:(b + 1) * 72].rearrange("p (j h t) -> p j h t", j=9, h=2),
                            in0=psm[:, b * 72:(b + 1) * 72].rearrange("p (j h t) -> p j h t", j=9, h=2),
                            in1=es_s[:, :, hp * 2:hp * 2 + 2, :], op=ALU.mult), reads=[psmr, cres], writes=[psmr])
                    for hh in range(2):
                        bk, br = bank()
                        def mmN(e, bk=bk, hp=hp, hh=hh):
                            ins = None
                            for g in range(3):
                                ins = e.matmul(bk[0:16, g * 16:(g + 1) * 16], KTs[hh * 64:(hh + 1) * 64, hp, :], QT[hh * 64:(hh + 1) * 64, g, 2048:2064],
                                               start=True, stop=True)
                            return ins
                        P.op("pe", mmN, reads=[KTr, QTr], writes=[br])
                        P.op("act", lambda e, bk=bk, hh=hh: e.activation(
                            out=psn[0:16, :].rearrange("p (g h q) -> p g h q", g=3, h=2)[:, :, hh, :],
                            in_=bk[0:16, 0:48].rearrange("p (g q) -> p g q", g=3), func=AF.Exp, scale=0.125), reads=[br], writes=[psnr])
                    P.op("dve", lambda e, hp=hp: e.tensor_tensor(out=psn[0:16, :].rearrange("p (g h q) -> p g h q", g=3, h=2),
                                                                  in0=psn[0:16, :].rearrange("p (g h q) -> p g h q", g=3, h=2),
                                                                  in1=en_s[0:16, :, hp * 2:hp * 2 + 2, :], op=ALU.mult), reads=[psnr, cres], writes=[psnr])
                    bk, br = bank()
                    def mmPV(e, bk=bk):
                        ins = None
                        first = True
                        for hh in range(2):
                            for g in range(3):
                                col = (g * 2 + hh) * 16
                                ins = e.matmul(bk[0:65, hh * 16:(hh + 1) * 16], VAs[0:16, hh, :], psn[0:16, col:col + 16], start=first, stop=False,
                                               skip_group_check=True)
                                first = False
                            for b in range(4):
                                for tile in range(9):
                                    col = ((b * 9 + tile) * 2 + hh) * 4
                                    ins = e.matmul(bk[0:65, hh * 16 + b * 4:hh * 16 + b * 4 + 4], vas[:, b * 9 + tile, hh, :], psm[:, col:col + 4],
                                                   start=False, stop=(tile == 8 and b == 3 and hh == 1), skip_group_check=True)
                        return ins
                    P.op("pe", mmPV, reads=[VAr, vasr, psmr, psnr], writes=[br])
                    P.op("dve", lambda e, bk=bk: e.tensor_copy(out=numT[0:65, :, 2048:2064], in_=bk[0:65, 0:32].rearrange("p (h q) -> p h q", h=2)),
                         reads=[br], writes=[numr])

                mark(9)
                P.op("dve", lambda e: e.reciprocal(out=numT[64:65, :, :], in_=numT[64:65, :, :]), reads=[numr], writes=[numr])
                for hh in range(2):
                    for ni, (c0, n) in enumerate(NTILES):
                        bk, br = bank()
                        P.op("pe", lambda e, bk=bk, hh=hh, c0=c0, n=n: e.matmul(bk[0:64, 0:n], onesf[64:65, 0:64], numT[64:65, hh, c0:c0 + n], start=True, stop=True),
                             reads=[numr, cres], writes=[br])
                        P.op("dve", lambda e, bk=bk, hh=hh, c0=c0, n=n, hp=hp: e.tensor_tensor(
                            out=outb[0:64, hp * 2 + hh, c0:c0 + n], in0=numT[0:64, hh, c0:c0 + n], in1=bk[0:64, 0:n], op=ALU.mult),
                            reads=[br, numr], writes=[outbr])
                P.barrier()

            mark(10)
            cvC = Carver(base3)
            uT = cvC.take(4 * NT).rearrange("p (c t) -> p c t", c=4); uTr = Res("uT")
            outc = cvC.take(4 * NT).rearrange("p (c t) -> p c t", c=4); outcr = Res("outc")
            merged = cvC.take(8 * NT).rearrange("p (c t) -> p c t", c=8); mergedr = [Res("mg%d" % i) for i in range(5)]
            sg = [cvC.take(512) for _ in range(3)]; sgr = [Res("sg%d" % i) for i in range(3)]
            mt = [cvC.take(512, F32) for _ in range(2)]; mtr = [Res("mt0"), Res("mt1")]
            vnb = [cvC.take(512) for _ in range(2)]; vnbr = [Res("vnb0"), Res("vnb1")]
            vtmp = [cvC.take(512, F32) for _ in range(2)]; vtmpr = [Res("vt0"), Res("vt1")]
            stats = cvC.take(16, F32); statr = Res("stats")
            ctmp = cvC.take(512, F32); ctmpr = Res("ctmp")
            wf32 = cvC.take(128, F32); wf32r = Res("wf32")
            tmpb = cvC.take(512); tmpr = Res("tmpb2")

            wua, wuar, _ = wload(win_cols(l, C_UA, 512))
            wva, wvar, _ = wload(win_cols(l, C_VA, 512))
            wcb, wcbr, _ = wload(win_cols(l, C_CB, 512))
            stats2 = [stats[:, 0:16], cvC.take(16, F32)]; statr2 = [Res("stats0"), Res("stats1")]

            def u_group(c, ni):
                c0, n = NTILES[ni]
                bk, br = proj_fm(wua, wuar, c * 128, hT, hTr, ni)
                P.op("act", lambda e: e.activation(out=uT[:, c, c0:c0 + n], in_=bk[:, 0:n], func=AF.Gelu_apprx_tanh),
                     reads=[br], writes=[uTr])

            def v_front(ti):
                t0, m = TTILES[ti]
                s = ti % 2
                st_, str2 = stats2[s], statr2[s]
                bk, br = bank()
                def mm(e):
                    ins = None
                    for k in range(8):
                        ins = e.matmul(bk[0:m, :], hT[:, k, t0:t0 + m], wva[:, k, :], start=(k == 0), stop=(k == 7))
                    return ins
                P.op("pe", mm, reads=[wvar, hTr[ntile_of(t0)]], writes=[br])
                P.op("act", lambda e: e.activation(out=vtmp[s][0:m, :], in_=bk[0:m, :], func=AF.Gelu_apprx_tanh), reads=[br], writes=[vtmpr[s]])
                P.chain("dve", [
                    lambda e: e.bn_stats(out=st_[0:m, 0:6], in_=vtmp[s][0:m, :]),
                    lambda e: e.bn_aggr(out=st_[0:m, 6:8], in_=st_[0:m, 0:6]),
                    lambda e: e.tensor_scalar(st_[0:m, 8:9], st_[0:m, 7:8], EPS, None, ALU.add)],
                    reads=[vtmpr[s]], writes=[str2])
                P.op("act", lambda e: e.activation(out=st_[0:m, 9:10], in_=st_[0:m, 8:9], func=AF.Sqrt), reads=[str2], writes=[str2])
                fns = [
                    lambda e: e.reciprocal(out=st_[0:m, 9:10], in_=st_[0:m, 9:10]),
                    lambda e: e.scalar_tensor_tensor(out=vtmp[s][0:m, :], in0=vtmp[s][0:m, :], scalar=st_[0:m, 6:7], in1=lng[0:m, :],
                                                     op0=ALU.subtract, op1=ALU.mult)]
                if ti == 16:
                    fns.append(lambda e: e.scalar_tensor_tensor(out=vtmp[s][0:m, :], in0=vtmp[s][0:m, :], scalar=st_[0:m, 9:10], in1=lnb[0:m, :],
                                                                op0=ALU.mult, op1=ALU.add))
                    fns.append(lambda e: e.tensor_copy(out=vnb[s][0:m, :], in_=vtmp[s][0:m, :]))
                else:
                    fns.append(lambda e: e.scalar_tensor_tensor(out=vnb[s][0:m, :], in0=vtmp[s][0:m, :], scalar=st_[0:m, 9:10], in1=lnb[0:m, :],
                                                                op0=ALU.mult, op1=ALU.add))
                P.chain("dve", fns, reads=[str2, cres, gres_], writes=[vtmpr[s], vnbr[s], str2])
                if ti == 16 and outputs:
                    P.dma("sp", vn_d[l, :, :], vtmp[s][0:16, :], reads=[vtmpr[s]], dsem=out_sem)

            def v_back(ti):
                t0, m = TTILES[ti]
                s = ti % 2
                bk2, br2 = bank()
                def mmg(e):
                    ins = None
                    for g in range(4):
                        if ti < 16:
                            e.matmul(bk2[:, g * 128:(g + 1) * 128], vnb[s][:, g * 128:(g + 1) * 128], wst[:, g, :], start=True, stop=False)
                            ins = e.matmul(bk2[:, g * 128:(g + 1) * 128], onesb[0:1, 0:128], bsr[0:1, g, :], start=False, stop=True)
                        else:
                            e.matmul(bk2[:, g * 16:(g + 1) * 16], vnb[s][0:16, g * 128:(g + 1) * 128], wss[0:16, g, :], start=True, stop=False)
                            ins = e.matmul(bk2[:, g * 16:(g + 1) * 16], onesb[0:1, 0:128], bss[0:1, g, :], start=False, stop=True)
                    return ins
                P.op("pe", mmg, reads=[vnbr[s], cres, gres_], writes=[br2])
                P.op("dve", lambda e: e.tensor_tensor(
                    out=uT[:, :, t0:t0 + m], in0=uT[:, :, t0:t0 + m], in1=bk2[:, 0:4 * m].rearrange("p (g t) -> p g t", g=4), op=ALU.mult),
                    reads=[br2, uTr], writes=[uTr])

            def cb_group(c, ni):
                c0, n = NTILES[ni]
                bk, br = proj_fm(wcb, wcbr, c * 128, hT, hTr, ni)
                if ni < 4:
                    z0, z1, z2 = zp[:, c, c0:c0 + n], zp[:, c, c0 + 1:c0 + 1 + n], zp[:, c, c0 + 2:c0 + 2 + n]
                    a = ctmp[:, 0:n]; o_ = outc[:, c, c0:c0 + n]; cbp = bk[:, 0:n]
                else:
                    z0, z1, z2 = zs[:, c, :, 0:4], zs[:, c, :, 1:5], zs[:, c, :, 2:6]
                    a = ctmp[:, 0:16].rearrange("p (b t) -> p b t", b=4)
                    o_ = outc[:, c, 2048:2064].rearrange("p (b t) -> p b t", b=4)
                    cbp = bk[:, 0:16].rearrange("p (b t) -> p b t", b=4)
                P.chain("dve", [
                    lambda e: e.tensor_scalar(a, z2, cwc[:, l, 2, c:c + 1], cwc[:, l, 3, c:c + 1], ALU.mult, ALU.add),
                    lambda e: e.scalar_tensor_tensor(out=a, in0=z1, scalar=cwc[:, l, 1, c:c + 1], in1=a, op0=ALU.mult, op1=ALU.add),
                    lambda e: e.scalar_tensor_tensor(out=a, in0=z0, scalar=cwc[:, l, 0, c:c + 1], in1=a, op0=ALU.mult, op1=ALU.add),
                    lambda e: e.tensor_tensor(out=o_, in0=a, in1=cbp, op=ALU.mult)],
                    reads=[br, zpr, zsr, cres], writes=[ctmpr, outcr])

            pending = None
            for ni in range(5):
                for c in range(4):
                    u_group(c, ni)
                tiles = [ti for ti, (t0, m) in enumerate(TTILES) if ntile_of(t0) == ni]
                for idx, ti in enumerate(tiles):
                    v_front(ti)
                    if ni < 4:
                        cb_group(idx, ni)
                    else:
                        for c in range(4):
                            cb_group(c, ni)
                    if pending is not None:
                        v_back(pending)
                    pending = ti
            v_back(pending)
            mark(11)
            mark(12)
            gview = win_d[l, :, C_GA:C_GA + 3072].rearrange("(k p) (b c) -> p k b c", p=128, b=3)
            with nc.allow_non_contiguous_dma(reason="128-column weight slices"):
                for f in range(8):
                    sg_i = ring_slot()
                    wg3 = []
                    for bq, cg in enumerate((C_GA, C_GB, C_GC)):
                        wq_, wg3r, _ = wload(win_cols(l, cg + f * 128, 128), slot=sg_i, off=bq * 1024)
                        wg3.append(wq_)
                    si = sg_i
                    wA, wbr_r, _ = wload(wba_d[l, :, f * 128:(f + 1) * 128].rearrange("(k p) c -> p k c", p=128), slot=si, off=3072)
                    wC, _, _ = wload(wbc_d[l, :, f * 128:(f + 1) * 128].rearrange("(k p) c -> p k c", p=128), slot=si, off=3584)
                    wB, _, _ = wload(wbb_d[l, :, f * 128:(f + 1) * 128].rearrange("(k p) c -> p k c", p=64), slot=si, off=4096)
                    for ni, (c0, n) in enumerate(NTILES):
                        prods = []
                        for bi_ in range(3):
                            bk, br = bank()
                            def mmgate(e, bk=bk, bi_=bi_, c0=c0, n=n, wg3=wg3):
                                ins = None
                                for k in range(8):
                                    ins = e.matmul(bk[:, 0:n], wg3[bi_][:, k, :], hT[:, k, c0:c0 + n], start=(k == 0), stop=(k == 7))
                                return ins
                            P.op("pe", mmgate, reads=[wg3r, hTr[ni]], writes=[br])
                            P.op("act", lambda e, bk=bk, bi_=bi_, n=n, f=f, l=l: e.activation(out=sg[bi_][:, 0:n], in_=bk[:, 0:n], func=AF.Sigmoid,
                                                                                          bias=bgc[:, l, bi_ * 8 + f:bi_ * 8 + f + 1]),
                                 reads=[br, cres], writes=[sgr[bi_]])
                            bk2, br2 = bank()
                            def mmbr(e, bk2=bk2, bi_=bi_, c0=c0, n=n, wA=wA, wB=wB, wC=wC):
                                ins = None
                                for k in range(4):
                                    if bi_ == 0:
                                        ins = e.matmul(bk2[:, 0:n], wA[:, k, :], uT[:, k, c0:c0 + n], start=(k == 0), stop=(k == 3))
                                    elif bi_ == 1:
                                        ins = e.matmul(bk2[:, 0:n], wB[0:64, k, :], outb[0:64, k, c0:c0 + n], start=(k == 0), stop=(k == 3))
                                    else:
                                        ins = e.matmul(bk2[:, 0:n], wC[:, k, :], outc[:, k, c0:c0 + n], start=(k == 0), stop=(k == 3))
                                return ins
                            P.op("pe", mmbr, reads=[wbr_r, uTr, outbr, outcr], writes=[br2])
                            prods.append((bk2, br2))
                        P.chain("dve", [
                            lambda e: e.tensor_tensor(out=mt[0][:, 0:n], in0=sg[0][:, 0:n], in1=prods[0][0][:, 0:n], op=ALU.mult),
                            lambda e: e.tensor_tensor(out=mt[1][:, 0:n], in0=sg[1][:, 0:n], in1=prods[1][0][:, 0:n], op=ALU.mult),
                            lambda e: e.tensor_tensor(out=mt[0][:, 0:n], in0=mt[0][:, 0:n], in1=mt[1][:, 0:n], op=ALU.add),
                            lambda e: e.tensor_tensor(out=mt[1][:, 0:n], in0=sg[2][:, 0:n], in1=prods[2][0][:, 0:n], op=ALU.mult),
                            lambda e: e.tensor_tensor(out=merged[:, f, c0:c0 + n], in0=mt[0][:, 0:n], in1=mt[1][:, 0:n], op=ALU.add)],
                            reads=[sgr[0], sgr[1], sgr[2], prods[0][1], prods[1][1], prods[2][1]], writes=[mtr[0], mtr[1], mergedr[ni]])
            P.barrier()
            mark(13)
            cvD = Carver(0)
            mixf2 = [cvD.take(8 * 512, F32).rearrange("p (k t) -> p k t", k=8) for _ in range(2)]; mixr2 = [[Res("mixf0a"), Res("mixf0b")], [Res("mixf1a"), Res("mixf1b")]]
            sqb2 = [cvD.take(8 * 512).rearrange("p (k t) -> p k t", k=8) for _ in range(2)]; sqr2 = [Res("sqb20"), Res("sqb21")]
            rs2 = [cvD.take(512, F32) for _ in range(2)]; rsr2 = [Res("rs20"), Res("rs21")]
            assert cvD.o <= base3
            cvD2 = Carver(base3)
            xt2 = [cvD2.take(4096, F32).rearrange("p (k t) -> p k t", k=8) for _ in range(2)]; xtr2 = [Res("xt20"), Res("xt21")]
            wos = [wload(wo_d[l, :, half * 512:(half + 1) * 512].rearrange("(k p) c -> p k c", p=128)) for half in range(2)]
            def wo_tile(ni):
                c0, n = NTILES[ni]
                mixf, mixr = mixf2[ni % 2], mixr2[ni % 2]
                for f in range(8):
                    bk, br = bank()
                    def mm(e):
                        ins = None
                        for k in range(8):
                            ins = e.matmul(bk[:, 0:n], wos[f // 4][0][:, k, (f % 4) * 128:(f % 4 + 1) * 128], merged[:, k, c0:c0 + n], start=(k == 0), stop=(k == 7))
                        return ins
                    P.op("pe", mm, reads=[wos[f // 4][1], mergedr[ni]], writes=[br])
                    if f % 2 == 0:
                        P.op("act", lambda e: e.copy(out=mixf[:, f, 0:n], in_=bk[:, 0:n]), reads=[br], writes=[mixr[0]])
                    else:
                        P.op("dve", lambda e: e.tensor_copy(out=mixf[:, f, 0:n], in_=bk[:, 0:n]), reads=[br], writes=[mixr[1]])
            for ni in range(5):
                wo_tile(ni)
                if ni >= 1:
                    post_norm_residual(mixf2[(ni - 1) % 2], mixr2[(ni - 1) % 2], gqm, ni - 1, sqb2, sqr2, rs2, rsr2, xt2, xtr2, nxt=ni)
            post_norm_residual(mixf2[0], mixr2[0], gqm, 4, sqb2, sqr2, rs2, rsr2, xt2, xtr2)
            P.barrier()

            mark(14)
            cvF = Carver(0)
            hF = cvF.take(8 * NT).rearrange("p (k t) -> p k t", k=8); hFr = [Res("hF%d" % i) for i in range(5)]
            sqb = [cvF.take(8 * 512).rearrange("p (k t) -> p k t", k=8) for _ in range(2)]; sqr = [Res("sqbF0"), Res("sqbF1")]
            rs = [cvF.take(512, F32) for _ in range(2)]; rsr = [Res("rsF0"), Res("rsF1")]
            actb = cvF.take(22 * 1040).rearrange("p (j t) -> p j t", j=22); actr = Res("actb")
            mixf = cvF.take(8 * 512, F32).rearrange("p (k t) -> p k t", k=8); mixr = [Res("mixFa"), Res("mixFb")]
            stmp = [cvF.take(512) for _ in range(2)]; stmpr = [Res("st0"), Res("st1")]
            xt = [cvF.take(4096, F32).rearrange("p (k t) -> p k t", k=8) for _ in range(2)]; xtr = [Res("xtF0"), Res("xtF1")]
            rmsnorm_h(gpf, hF, hFr, sqb, sqr, rs, rsr, xt, xtr, tiles=(0, 1))
            for hh_, nis in enumerate(((0, 1), (2, 3, 4))):
                base = NTILES[nis[0]][0]
                for j4 in range(0, 22, 2):
                    nj = 2
                    sgu = ring_slot()
                    wgt, wgtr, _ = wload(wg_d[l, :, j4 * 128:(j4 + nj) * 128].rearrange("(k p) c -> p k c", p=128), slot=sgu, off=0)
                    wup, wupr, _ = wload(wu_d[l, :, j4 * 128:(j4 + nj) * 128].rearrange("(k p) c -> p k c", p=128), slot=sgu, off=2048)
                    for jj in range(nj):
                        j = j4 + jj
                        for ni in nis:
                            c0, n = NTILES[ni]
                            bkg, brg = proj_fm(wgt, wgtr, jj * 128, hF, hFr, ni)
                            bku, bru = proj_fm(wup, wupr, jj * 128, hF, hFr, ni)
                            s = (j + ni) % 2
                            P.op("act", lambda e, bkg=bkg, s=s, n=n: e.activation(out=stmp[s][:, 0:n], in_=bkg[:, 0:n], func=AF.Silu), reads=[brg], writes=[stmpr[s]])
                            P.op("dve", lambda e, bku=bku, s=s, n=n, j=j, c0=c0, base=base: e.tensor_tensor(
                                out=actb[:, j, c0 - base:c0 - base + n], in0=stmp[s][:, 0:n], in1=bku[:, 0:n], op=ALU.mult),
                                reads=[bru, stmpr[s]], writes=[actr])
                    if hh_ == 0 and j4 in (0, 2, 4):
                        rmsnorm_h(gpf, hF, hFr, sqb, sqr, rs, rsr, xt, xtr, tiles=(2 + j4 // 2,))
                for ni in nis:
                    c0, n = NTILES[ni]
                    bks = [bank() for _ in range(8)]
                    for j4 in range(0, 22, 4):
                        nj = min(4, 22 - j4)
                        wdn, wdnr, _ = wload(wd_d[l, j4 * 128:(j4 + nj) * 128, :].rearrange("(j p) c -> p j c", p=128))
                        def mm(e, bks=bks, j4=j4, nj=nj, c0=c0, n=n, base=base, wdn=wdn):
                            ins = None
                            for f in range(8):
                                for jj in range(nj):
                                    j = j4 + jj
                                    ins = e.matmul(bks[f][0][:, 0:n], wdn[:, jj, f * 128:(f + 1) * 128], actb[:, j, c0 - base:c0 - base + n],
                                                   start=(j == 0), stop=(j == 21))
                            return ins
                        P.op("pe", mm, reads=[wdnr, actr], writes=[b[1] for b in bks])
                    for f in range(8):
                        if f % 2 == 0:
                            P.op("act", lambda e, f=f, n=n, bks=bks: e.copy(out=mixf[:, f, 0:n], in_=bks[f][0][:, 0:n]), reads=[bks[f][1]], writes=[mixr[0]])
                        else:
                            P.op("dve", lambda e, f=f, n=n, bks=bks: e.tensor_copy(out=mixf[:, f, 0:n], in_=bks[f][0][:, 0:n]), reads=[bks[f][1]], writes=[mixr[1]])
                    nxt_ = nis[nis.index(ni) + 1] if nis.index(ni) + 1 < len(nis) else None
                    post_norm_residual(mixf, mixr, gqf, ni, sqb, sqr, rs, rsr, xt, xtr, nxt=nxt_)
            P.barrier()

        f1 = flagb[:, 0:1]; f2 = flagb[:, 1:2]
        layer(0, 2, None, None, bounce_d[0], "halo", False)
        layer(0, 1, bounce_d[0], f2, bounce_d[1], "full", False)
        layer(0, 0, bounce_d[1], f1, None, "full", True)
        layer(1, 1, None, None, bounce_d[2], "halo", False)
        layer(1, 0, bounce_d[2], f1, None, "full", True)
        xpark, XTr = xparks[0], XTrs[0]
        cvO = Carver(0)
        ost = [cvO.take(1024, F32) for _ in range(2)]; ostr = [Res("ost0"), Res("ost1")]
        xt = [cvO.take(4096, F32).rearrange("p (k t) -> p k t", k=8) for _ in range(2)]; xtr = [Res("xtO0"), Res("xtO1")]
        for ni, (c0, n) in enumerate(NTILES):
            xs_ = ni % 2
            P.dma("sp", xt[xs_][:, :, 0:n], xpark[:, ni, :, 0:n], reads=[XTr[ni]], writes=[xtr[xs_]])
            for lt in range(0, n, 128):
                m = min(128, n - lt)
                t0 = c0 + lt
                ti = t0 // 128
                s = ti % 2
                for half in range(2):
                    bk, br = bank()
                    def tr(e, bk=bk, half=half, lt=lt, m=m, xs_=xs_):
                        ins = None
                        for jj in range(4):
                            k = half * 4 + jj
                            ins = e.transpose(bk[0:m, jj * 128:(jj + 1) * 128], xt[xs_][:, k, lt:lt + m], identf[:, :])
                        return ins
                    P.op("pe", tr, reads=[xtr[xs_], cres], writes=[br])
                    P.op("act", lambda e, bk=bk, half=half, s=s, m=m: e.copy(out=ost[s][0:m, half * 512:(half + 1) * 512], in_=bk[0:m, :]),
                         reads=[br], writes=[ostr[s]])
                dst = yp_d[t0:t0 + m, :] if ni < 4 else ys_d[:, :]
                P.dma("sp", dst, ost[s][0:m, :], reads=[ostr[s]], dsem=out_sem)
        P.barrier()

    def mk(name):
        def body(eng):
            P.reset(name, eng)
            program()
        return body
    with nc.Block() as block:
        block.tensor(mk("pe"))
        block.scalar(mk("act"))
        block.vector(mk("dve"))
        block.gpsimd(mk("pool"))
        block.sync(mk("sp"))
    es.close()
    return nc


_NC_CACHE = {}


def kernel(**inputs):
    f32 = np.float32
    inp = {k: np.ascontiguousarray(np.asarray(v), dtype=f32) for k, v in inputs.items()}
    if "nc" not in _NC_CACHE:
        _NC_CACHE["nc"] = build()
    nc = _NC_CACHE["nc"]
    consts = host_consts()
    shared = {k: inp[k] for k in ("g_pre_mix", "g_post_mix", "g_pre_ffn", "g_post_ffn", "w_in", "a_ln_g", "a_ln_b", "a_ws", "a_bs",
                                  "c_conv_w", "c_conv_b", "w_br_a", "w_br_b", "w_br_c", "b_gate", "w_o", "w_ff_gate", "w_ff_up", "w_ff_down")}
    in_maps = []
    for c in range(NCORES):
        b, q = c // 4, c % 4
        m = dict(shared)
        m.update(consts)
        m["xp"] = np.ascontiguousarray(inp["x_prompt"][b, q * NPR:(q + 1) * NPR, :])
        m["xs"] = np.ascontiguousarray(inp["x_sample"][c * 4:(c + 1) * 4].reshape(NSM, D))
        m["ck"] = np.ascontiguousarray(inp["cache_k_win"][:, c * 4:(c + 1) * 4].reshape(DEPTH, 4, 2048, 256))
        m["cv"] = np.ascontiguousarray(inp["cache_v_win"][:, c * 4:(c + 1) * 4].reshape(DEPTH, 4, 2048, 256))
        m["sc"] = np.ascontiguousarray(inp["state_conv"][:, c * 4:(c + 1) * 4].reshape(DEPTH, 8, 512))
        q1, q2 = max(q - 1, 0), max(q - 2, 0)
        m["xp1"] = np.ascontiguousarray(inp["x_prompt"][b, q1 * NPR:(q1 + 1) * NPR, :])
        m["xp2"] = np.ascontiguousarray(inp["x_prompt"][b, q2 * NPR:(q2 + 1) * NPR, :])
        m["flags"] = np.array([[1.0 if q >= 1 else 0.0, 1.0 if q >= 2 else 0.0]], f32)
        in_maps.append(m)
    res = run_bass_kernel_spmd(nc, in_maps, core_ids=list(range(NCORES)))
    shp = {"yp": (NPR, D), "ys": (NSM, D), "kv": (DEPTH, NT, 512), "zo": (DEPTH, 10, 512), "vns": (DEPTH, NSM, 512)}
    R = [{k: np.asarray(r[k]).reshape(shp[k]) for k in shp} for r in res.results]
    yp = np.stack([np.concatenate([R[b * 4 + q]["yp"] for q in range(4)], axis=0) for b in range(2)]).astype(f32)
    ys = np.concatenate([R[c]["ys"].reshape(4, 4, D) for c in range(NCORES)], axis=0).astype(f32)
    kp = np.stack([np.stack([R[b * 4 + 3]["kv"][l, 0:NPR, 0:256].reshape(NPR, 4, 64) for b in range(2)]) for l in range(DEPTH)]).astype(f32)
    vp = np.stack([np.stack([R[b * 4 + 3]["kv"][l, 0:NPR, 256:512].reshape(NPR, 4, 64) for b in range(2)]) for l in range(DEPTH)]).astype(f32)
    ksm = np.stack([np.concatenate([R[c]["kv"][l, NPR:NT, 0:256].reshape(4, 4, 4, 64) for c in range(NCORES)], axis=0) for l in range(DEPTH)]).astype(f32)
    vsm = np.stack([np.concatenate([R[c]["kv"][l, NPR:NT, 256:512].reshape(4, 4, 4, 64) for c in range(NCORES)], axis=0) for l in range(DEPTH)]).astype(f32)
    cp = np.stack([np.stack([R[b * 4 + 3]["zo"][l, 0:2, :] for b in range(2)]) for l in range(DEPTH)]).astype(f32)
    cs = np.stack([np.concatenate([R[c]["zo"][l, 2:10, :].reshape(4, 2, 512) for c in range(NCORES)], axis=0) for l in range(DEPTH)]).astype(f32)
    av = np.stack([np.concatenate([R[c]["vns"][l].reshape(4, 4, 512) for c in range(NCORES)], axis=0) for l in range(DEPTH)]).astype(f32)
    return (yp, ys, kp, vp, ksm, vsm, cp, cs, av)
```

```python
import os
import numpy as np
import ml_dtypes
from contextlib import ExitStack
import concourse.bass as bass
import concourse.mybir as mybir
from concourse.bass_utils import run_bass_kernel_spmd

F32 = mybir.dt.float32
BF16 = mybir.dt.bfloat16
AF = mybir.ActivationFunctionType
ALU = mybir.AluOpType
AX = mybir.AxisListType

NCORES = 8
D = 1024
DEPTH = 2
NPR = 2048
NSM = 16
NT = NPR + NSM
INW = 6912
DFF = 2816
NTILES = [(0, 512), (512, 512), (1024, 512), (1536, 512), (2048, 16)]
TTILES = [(i * 128, 128) for i in range(16)] + [(2048, 16)]
DILS = (1, 4, 16)
EPS = 1e-6
XC = 9568
C_UA, C_VA, C_Q, C_K, C_V, C_CX, C_CB, C_CC, C_GA, C_GB, C_GC = 0, 512, 1024, 1792, 2048, 2304, 2816, 3328, 3840, 4864, 5888
SAME_ENG_SYNC = True


KSTOP = int(os.environ.get("KSTOP", "-1"))


class Stop(Exception):
    pass


class Res:
    __slots__ = ("name", "w", "r")

    def __init__(self, name):
        self.name = name
        self.w = None
        self.r = []


class DSem:
    def __init__(self, handle, idx):
        self.h = handle
        self.idx = idx
        self.total = 0


class Prog:
    ENG = ("pe", "act", "dve", "pool", "sp")
    NPOOL = 24

    def __init__(self, nc, es):
        self.nc, self.es = nc, es
        self.sem = {e: es.enter_context(nc.semaphore("s_" + e)) for e in self.ENG}
        self.handles = []
        self.named = {}
        self.reset(None, None)

    def reset(self, cur, eobj):
        self.cur, self.eobj = cur, eobj
        self.cnt = {e: 0 for e in self.ENG}
        self.seen = {}
        self.dsems = []
        self.hidx = 0
        self.pool = None
        self.pool_i = 0
        self.pend = None

    def named_sem(self, name):
        if name not in self.named:
            self.named[name] = self.es.enter_context(self.nc.semaphore(name))
        return self.named[name]

    def new_dsem(self):
        if self.hidx >= len(self.handles):
            self.handles.append(self.es.enter_context(self.nc.semaphore("d%d" % len(self.handles))))
        d = DSem(self.handles[self.hidx], self.hidx)
        self.hidx += 1
        self.dsems.append(d)
        return d

    @staticmethod
    def _flat(rs):
        out = []
        for r in rs:
            if isinstance(r, (list, tuple)):
                out.extend(Prog._flat(r))
            else:
                out.append(r)
        return out

    def _waits(self, eng, reads, writes):
        reads, writes = self._flat(reads), self._flat(writes)
        toks = []
        for r in reads:
            if r.w is not None:
                toks.append(r.w)
        for r in writes:
            if r.w is not None:
                toks.append(r.w)
            toks.extend(r.r)
        need = {}
        for (h, val, key, src) in toks:
            if src == eng and (eng == "pe" or not SAME_ENG_SYNC) and key == "c_" + eng:
                continue
            if need.get(key, (None, 0))[1] < val:
                need[key] = (h, val)
        waits = []
        for key, (h, val) in need.items():
            if self.seen.get((eng, key), 0) >= val:
                continue
            self.seen[(eng, key)] = val
            waits.append((h, val))
        return waits

    def _commit(self, tok, reads, writes):
        reads, writes = self._flat(reads), self._flat(writes)
        for r in reads:
            r.r.append(tok)
        for r in writes:
            r.w = tok
            r.r = []

    def _emit(self, eng, waits, fn, sh, inc):
        if eng != self.cur:
            return
        for h, v in waits:
            self.eobj.wait_ge(h, v)
        if fn is not None:
            ins = fn(self.eobj)
            if inc:
                ins.then_inc(sh, inc)

    def op(self, eng, fn, reads=(), writes=()):
        self.flush()
        waits = self._waits(eng, reads, writes)
        self.cnt[eng] += 1
        tok = (self.sem[eng], self.cnt[eng], "c_" + eng, eng)
        self._emit(eng, waits, fn, self.sem[eng], 1)
        self._commit(tok, reads, writes)
        return tok

    def dma(self, eng, out, in_, reads=(), writes=(), dsem=None, ring=False):
        if not ring:
            self.flush()
        extra = []
        if dsem is None:
            if self.pool is None:
                self.pool = {"sp": [self.new_dsem() for _ in range(self.NPOOL)],
                             "pool": [self.new_dsem() for _ in range(16)],
                             "act": [self.new_dsem() for _ in range(8)]}
                self.pool_i = {"sp": 0, "pool": 0, "act": 0}
            pl = self.pool[eng]
            dsem = pl[self.pool_i[eng] % len(pl)]
            self.pool_i[eng] += 1
            key = "d_%d" % dsem.idx
            if dsem.total > self.seen.get((eng, key), 0):
                self.seen[(eng, key)] = dsem.total
                extra.append((dsem.h, dsem.total))
        waits = extra + self._waits(eng, reads, writes)
        dsem.total += 16
        tok = (dsem.h, dsem.total, "d_%d" % dsem.idx, "dma")
        self._emit(eng, waits, lambda e: e.dma_start(out=out, in_=in_, allow_slow_non_contiguous=True), dsem.h, 16)
        self._commit(tok, reads, writes)
        return tok

    def chain(self, eng, fns, reads=(), writes=()):
        cr = Res("chain")
        tok = None
        for fn in fns:
            tok = self.op(eng, fn, reads=list(reads) + [cr], writes=list(writes) + [cr])
        return tok

    def custom(self, eng, fn, sem_h, inc, key, total, reads=(), writes=()):
        self.flush()
        waits = self._waits(eng, reads, writes)
        tok = (sem_h, total, key, "dma")
        self._emit(eng, waits, fn, sem_h, inc)
        self._commit(tok, reads, writes)
        return tok

    def barrier(self):
        self.flush()
        self.pend = (dict(self.cnt), [(d, d.total) for d in self.dsems])

    def flush(self):
        if self.pend is None:
            return
        cnt, dtot = self.pend
        self.pend = None
        for e in self.ENG:
            waits = []
            for x in self.ENG:
                if x == e:
                    continue
                key = "c_" + x
                if cnt[x] > self.seen.get((e, key), 0):
                    self.seen[(e, key)] = cnt[x]
                    waits.append((self.sem[x], cnt[x]))
            for d, tot in dtot:
                key = "d_%d" % d.idx
                if tot > self.seen.get((e, key), 0):
                    self.seen[(e, key)] = tot
                    waits.append((d.h, tot))
            self._emit(e, waits, None, None, 0)


def alibi_slopes():
    return np.array([2.0 ** (-8.0 * (i + 1) / 12) for i in range(12)], np.float64).reshape(3, 4)


def host_consts():
    bf = ml_dtypes.bfloat16
    c = {}
    c["identb"] = np.eye(128, dtype=np.float32).astype(bf)
    c["identf"] = np.eye(128, dtype=np.float32)
    c["onesb"] = np.ones((128, 128), np.float32).astype(bf)
    c["onesf"] = np.ones((128, 64), np.float32)
    c["tril"] = np.tril(np.ones((128, 128), np.float32))
    sl = alibi_slopes()
    k = np.arange(128)[:, None]
    q = np.arange(128)[None, :]
    ep = np.zeros((128, 3, 4, 2, 128), np.float64)
    for g, d in enumerate(DILS):
        for h in range(4):
            dist0 = q + 128 - k
            ep[:, g, h, 0, :] = np.where(k >= q, np.exp(-sl[g, h] * d * dist0), 0.0)
            dist1 = q - k
            ep[:, g, h, 1, :] = np.where(k <= q, np.exp(-sl[g, h] * d * dist1), 0.0)
    c["ep"] = ep.reshape(128, 3 * 4 * 2 * 128).astype(np.float32).astype(bf)
    es_ = np.zeros((128, 9, 4, 4), np.float64)
    p = np.arange(128)
    for h in range(4):
        for t in range(4):
            tap = 128 + t - p
            es_[:, 0, h, t] = np.where(p >= t, np.exp(-sl[0, h] * 1 * tap), 0.0)
            tap = 128 - p
            es_[:, 1 + t, h, t] = np.exp(-sl[1, h] * 4 * tap)
            es_[:, 5 + t, h, t] = np.exp(-sl[2, h] * 16 * tap)
    c["es"] = es_.reshape(128, 144).astype(np.float32).astype(bf)
    en = np.zeros((16, 3, 4, 16), np.float64)
    for b in range(4):
        for t2 in range(4):
            for t in range(4):
                if t2 > t:
                    continue
                for g, d in enumerate(DILS):
                    if (t - t2) % d != 0 or (t - t2) // d > 128:
                        continue
                    for h in range(4):
                        en[b * 4 + t2, g, h, b * 4 + t] = np.exp(-sl[g, h] * (t - t2))
    c["en"] = en.reshape(16, 192).astype(np.float32).astype(bf)
    return c


def build():
    nc = bass.Bass("TRN2", target_bir_lowering=False)
    es = ExitStack()
    P = Prog(nc, es)

    def din(name, shape, dt=F32):
        return nc.dram_tensor(name, list(shape), dt, kind="ExternalInput")

    def dout(name, shape, dt=F32):
        return nc.dram_tensor(name, list(shape), dt, kind="ExternalOutput")

    xp_d = din("xp", [NPR, D]); xs_d = din("xs", [NSM, D])
    ck_d = din("ck", [DEPTH, 4, 2048, 256]); cv_d = din("cv", [DEPTH, 4, 2048, 256])
    sc_d = din("sc", [DEPTH, 8, 512])
    gpm_d = din("g_pre_mix", [DEPTH, D]); gqm_d = din("g_post_mix", [DEPTH, D])
    gpf_d = din("g_pre_ffn", [DEPTH, D]); gqf_d = din("g_post_ffn", [DEPTH, D])
    win_d = din("w_in", [DEPTH, D, INW])
    alg_d = din("a_ln_g", [DEPTH, 512]); alb_d = din("a_ln_b", [DEPTH, 512])
    aws_d = din("a_ws", [DEPTH, 4, 128, 128]); abs_d = din("a_bs", [DEPTH, 4, 128])
    cw_d = din("c_conv_w", [DEPTH, 3, 512]); cbias_d = din("c_conv_b", [DEPTH, 512])
    wba_d = din("w_br_a", [DEPTH, 512, D]); wbb_d = din("w_br_b", [DEPTH, 256, D]); wbc_d = din("w_br_c", [DEPTH, 512, D])
    bg_d = din("b_gate", [DEPTH, 3 * D]); wo_d = din("w_o", [DEPTH, D, D])
    wg_d = din("w_ff_gate", [DEPTH, D, DFF]); wu_d = din("w_ff_up", [DEPTH, D, DFF]); wd_d = din("w_ff_down", [DEPTH, DFF, D])
    flags_d = din("flags", [1, 2])
    xp1_d = din("xp1", [NPR, D]); xp2_d = din("xp2", [NPR, D])
    identb_d = din("identb", [128, 128], BF16); identf_d = din("identf", [128, 128])
    onesb_d = din("onesb", [128, 128], BF16); onesf_d = din("onesf", [128, 64])
    tril_d = din("tril", [128, 128]); ep_d = din("ep", [128, 3072], BF16)
    es_d = din("es", [128, 144], BF16); en_d = din("en", [16, 192], BF16)

    yp_d = dout("yp", [NPR, D]); ys_d = dout("ys", [NSM, D])
    kv_d = dout("kv", [DEPTH, NT, 512]); zo_d = dout("zo", [DEPTH, 10, 512]); vn_d = dout("vns", [DEPTH, NSM, 512])

    bounce_d = [nc.dram_tensor("bounce%d" % i, [128, XC], BF16) for i in range(4)]
    xpark_ds = [nc.dram_tensor("xpark%d" % i, [128, 5 * 8 * 512], F32) for i in range(3)]

    def sb(name, shape, dt=F32):
        return es.enter_context(nc.sbuf_tensor(name, list(shape), dt))

    identb = sb("identb_s", [128, 128], BF16); identf = sb("identf_s", [128, 128])
    onesb = sb("onesb_s", [128, 128], BF16); onesf = sb("onesf_s", [128, 64])
    tril = sb("tril_s", [128, 128]); ep = sb("ep_s", [128, 3, 4, 2, 128], BF16)
    es_s = sb("es_s", [128, 9, 4, 4], BF16); en_s = sb("en_s", [16, 3, 4, 16], BF16)
    flagb = sb("flagb", [128, 2])
    vecs = sb("vecs", [128, DEPTH, 4, 8])
    bgc = sb("bgc", [128, DEPTH, 24])
    cwc = sb("cwc", [128, DEPTH, 4, 4])
    lng = sb("lng", [128, 512]); lnb = sb("lnb", [128, 512])
    bsrf = sb("bsrf", [1, 4, 128]); bsr = sb("bsr", [1, 4, 128], BF16)
    wst = sb("wst", [128, 4, 128], BF16)
    wss = sb("wss", [16, 4, 16], BF16)
    bss = sb("bss", [1, 4, 16], BF16)
    RING_N = 3
    ring = [sb("ring%d" % i, [128, 4608], BF16) for i in range(RING_N)]
    ring_res = [None] * RING_N
    ring_sem = [None] * RING_N
    ring_i = [0]
    ARN = 83760
    AR = sb("arena", [128, ARN], BF16)

    banks = [es.enter_context(nc.psum_tensor("bank%d" % i, [128, 512], F32)) for i in range(8)]
    bank_res = [None] * 8
    bank_i = [0]

    def bank():
        i = bank_i[0] % 8
        bank_i[0] += 1
        return banks[i], bank_res[i]

    def ring_slot():
        i = ring_i[0] % RING_N
        ring_i[0] += 1
        return i

    def wload(src_ap, slot=None, off=0):
        i = ring_slot() if slot is None else slot
        n = 1
        for s in src_ap.shape[1:]:
            n *= s
        dst = ring[i][0:src_ap.shape[0], off:off + n]
        if len(src_ap.shape) == 3:
            dst = dst.rearrange("p (a b) -> p a b", a=src_ap.shape[1])
        elif len(src_ap.shape) == 4:
            dst = dst.rearrange("p (a b c) -> p a b c", a=src_ap.shape[1], b=src_ap.shape[2])
        P.dma("pool", dst, src_ap, writes=[ring_res[i]], dsem=ring_sem[i], ring=True)
        return dst, ring_res[i], i

    def win_cols(l, c0, n):
        return win_d[l, :, c0:c0 + n].rearrange("(k p) c -> p k c", p=128)

    class Carver:
        def __init__(self, base):
            self.o = base
        def take(self, nelem, dt=BF16, parts=128):
            if dt == F32:
                v = AR[0:parts, self.o:self.o + 2 * nelem].bitcast(F32)
                self.o += 2 * nelem
            else:
                v = AR[0:parts, self.o:self.o + nelem]
                self.o += nelem
            assert self.o <= ARN, self.o
            return v

    xparks = [x.ap().rearrange("p (n k t) -> p n k t", n=5, k=8) for x in xpark_ds]
    XTrs = [[None] * 5 for _ in range(3)]

    def ntile_of(tok):
        return min(tok // 512, 4)

    def program():
      for i in range(RING_N):
          ring_res[i] = Res("ring%d" % i)
          ring_sem[i] = P.new_dsem()
      ring_i[0] = 0
      for i in range(8):
          bank_res[i] = Res("bank%d" % i)
      bank_i[0] = 0
      for q_ in range(3):
          XTrs[q_][:] = [Res("XT%d_%d" % (q_, i)) for i in range(5)]
      try:
          program_body()
      except Stop:
          P.barrier()
      P.flush()

    CI = [0]

    def mark(k):
        if KSTOP == k or KSTOP == CI[0] * 20 + k:
            raise Stop()

    def program_body():
        CI[0] = -10
        cres = Res("consts")
        for dst, src in ((identb, identb_d), (identf, identf_d), (onesb, onesb_d), (onesf, onesf_d), (tril, tril_d)):
            P.dma("sp", dst[:, :], src[:, :], writes=[cres])
        P.dma("sp", ep[:].rearrange("p a b c d -> p (a b c d)"), ep_d[:, :], writes=[cres])
        P.dma("sp", es_s[:].rearrange("p a b c -> p (a b c)"), es_d[:, :], writes=[cres])
        P.dma("sp", en_s[:].rearrange("p a b c -> p (a b c)"), en_d[:, :], writes=[cres])
        P.dma("sp", flagb[:, :], flags_d[0:1, :].partition_broadcast(128), writes=[cres])
        with nc.allow_non_contiguous_dma(reason="tiny per-partition parameter columns"):
            for wi, gd in enumerate((gpm_d, gqm_d, gpf_d, gqf_d)):
                for l in range(DEPTH):
                    P.dma("sp", vecs[:, l, wi, :], gd[l, :].rearrange("(k p) -> p k", p=128), writes=[cres])
            for l in range(DEPTH):
                P.dma("sp", bgc[:, l, :], bg_d[l, :].rearrange("(k p) -> p k", p=128), writes=[cres])
                for i in range(3):
                    P.dma("sp", cwc[:, l, i, :], cw_d[l, i, :].rearrange("(k p) -> p k", p=128), writes=[cres])
                P.dma("sp", cwc[:, l, 3, :], cbias_d[l, :].rearrange("(k p) -> p k", p=128), writes=[cres])
        P.barrier()
        mark(1)

        out_sem = None

        cv0 = Carver(0)
        xst = [cv0.take(1024, F32) for _ in range(2)]
        xst_res = [Res("xst0"), Res("xst1")]
        xt = [cv0.take(4096, F32).rearrange("p (k t) -> p k t", k=8) for _ in range(2)]
        xtr = [Res("xt0"), Res("xt1")]
        for (xsrc_d, xpark, XTr) in ((xp2_d, xparks[2], XTrs[2]), (xp1_d, xparks[1], XTrs[1]), (xp_d, xparks[0], XTrs[0])):
          for ti, (t0, m) in enumerate(TTILES):
            s = ti % 2
            ni = ntile_of(t0)
            xs_ = ni % 2
            lc = t0 - NTILES[ni][0]
            src = xsrc_d[t0:t0 + m, :] if ti < 16 else xs_d[:, :]
            P.dma("sp", xst[s][0:m, :], src, writes=[xst_res[s]])
            for half in range(2):
                bk, br = bank()
                def tr(e, s=s, m=m, half=half, bk=bk):
                    ins = None
                    for j in range(4):
                        k = half * 4 + j
                        ins = e.transpose(bk[:, j * 128:j * 128 + m], xst[s][0:m, k * 128:(k + 1) * 128], identf[0:m, 0:m])
                    return ins
                P.op("pe", tr, reads=[xst_res[s], cres], writes=[br])
                P.op("dve", lambda e, half=half, bk=bk, lc=lc, m=m, xs_=xs_: e.tensor_copy(
                    out=xt[xs_][:, half * 4:half * 4 + 4, lc:lc + m],
                    in_=bk[:, :].rearrange("p (j t) -> p j t", j=4)[:, :, 0:m]), reads=[br], writes=[xtr[xs_]])
            if lc + m == NTILES[ni][1]:
                n = NTILES[ni][1]
                P.dma("act", xpark[:, ni, :, 0:n], xt[xs_][:, :, 0:n], reads=[xtr[xs_]], writes=[XTr[ni]])
        P.barrier()

        mark(2)
        TWIN = {}

        def norm_stats(src3, srcr, n, sqb, sqr, rs, rsr):
            sqr_b = TWIN.setdefault(id(sqr), Res("sq_twin"))
            P.op("act", lambda e: e.activation(out=sqb[:, 0:4, 0:n], in_=src3[:, 0:4, :], func=AF.Square), reads=[srcr], writes=[sqr])
            P.op("dve", lambda e: e.tensor_tensor(out=sqb[:, 4:8, 0:n], in0=src3[:, 4:8, :], in1=src3[:, 4:8, :], op=ALU.mult), reads=[srcr], writes=[sqr_b])
            bk, br = bank()
            def mm(e):
                ins = None
                for k in range(8):
                    ins = e.matmul(bk[:, 0:n], onesb[:, :], sqb[:, k, 0:n], start=(k == 0), stop=(k == 7))
                return ins
            P.op("pe", mm, reads=[sqr, sqr_b, cres], writes=[br])
            P.op("act", lambda e: e.activation(out=rs[:, 0:n], in_=bk[:, 0:n], func=AF.Sqrt, bias=EPS, scale=1.0 / D),
                 reads=[br], writes=[rsr])
            P.op("dve", lambda e: e.reciprocal(out=rs[:, 0:n], in_=rs[:, 0:n]), reads=[rsr], writes=[rsr])

        CUR = [None, None]

        def rmsnorm_h(gcol, hT, hTr, sqb, sqr, rs, rsr, xt, xtr, tiles=(0, 1, 2, 3, 4)):
            xpark, XTr = CUR
            for ni in tiles:
                c0, n = NTILES[ni]
                s = ni % 2
                P.dma("sp", xt[s][:, :, 0:n], xpark[:, ni, :, 0:n], reads=[XTr[ni]], writes=[xtr[s]])
                norm_stats(xt[s][:, :, 0:n], xtr[s], n, sqb[s], sqr[s], rs[s], rsr[s])
                def sc(e, c0=c0, n=n, s=s):
                    ins = None
                    for k in range(8):
                        ins = e.scalar_tensor_tensor(out=hT[:, k, c0:c0 + n], in0=xt[s][:, k, 0:n], scalar=gcol[:, k:k + 1],
                                                     in1=rs[s][:, 0:n], op0=ALU.mult, op1=ALU.mult)
                    return ins
                P.op("dve", sc, reads=[xtr[s], rsr[s], cres], writes=[hTr[ni]])

        def proj_fm(wslot, wres, wc0, hT, hTr, ni, kch=8):
            c0, n = NTILES[ni]
            bk, br = bank()
            def mm(e):
                ins = None
                for k in range(kch):
                    ins = e.matmul(bk[:, 0:n], wslot[:, k, wc0:wc0 + 128], hT[:, k, c0:c0 + n], start=(k == 0), stop=(k == kch - 1))
                return ins
            P.op("pe", mm, reads=[wres, hTr[ni]], writes=[br])
            return bk, br

        XTL = set()

        def post_norm_residual(mixf, mixr, gcol, ni, sqb, sqr, rs, rsr, xt, xtr, nxt=None):
            xpark, XTr = CUR
            c0, n = NTILES[ni]
            s = ni % 2
            key = (id(xtr[0]), ni)
            if key in XTL:
                XTL.discard(key)
            else:
                P.dma("sp", xt[s][:, :, 0:n], xpark[:, ni, :, 0:n], reads=[XTr[ni]], writes=[xtr[s]])
            if nxt is not None:
                n2 = NTILES[nxt][1]
                XTL.add((id(xtr[0]), nxt))
                P.dma("sp", xt[nxt % 2][:, :, 0:n2], xpark[:, nxt, :, 0:n2], reads=[XTr[nxt]], writes=[xtr[nxt % 2]])
            sqb, sqr, rs, rsr = sqb[s], sqr[s], rs[s], rsr[s]
            norm_stats(mixf[:, :, 0:n], mixr, n, sqb, sqr, rs, rsr)
            def sc1(e):
                ins = None
                for k in range(8):
                    ins = e.scalar_tensor_tensor(out=mixf[:, k, 0:n], in0=mixf[:, k, 0:n], scalar=gcol[:, k:k + 1],
                                                 in1=rs[:, 0:n], op0=ALU.mult, op1=ALU.mult)
                return ins
            P.chain("dve", [sc1, lambda e: e.tensor_tensor(out=xt[s][:, :, 0:n], in0=xt[s][:, :, 0:n], in1=mixf[:, :, 0:n], op=ALU.add)],
                    reads=[rsr, cres], writes=[mixr, xtr[s]])
            P.dma("sp", xpark[:, ni, :, 0:n], xt[s][:, :, 0:n], reads=[xtr[s]], writes=[XTr[ni]])

        def layer(l, q_, bin_d, fcol, bout_d, mode, outputs):
            CUR[0], CUR[1] = xparks[q_], XTrs[q_]
            CI[0] = CI[0] + 1 if CI[0] >= 0 else 0
            cvA = Carver(0)
            hT = cvA.take(8 * NT).rearrange("p (k t) -> p k t", k=8); hTr = [Res("hT%d" % i) for i in range(5)]
            outb = cvA.take(4 * NT).rearrange("p (h t) -> p h t", h=4); outbr = Res("outb")
            zp = cvA.take(4 * 2050).rearrange("p (c t) -> p c t", c=4); zpr = Res("zp")
            zs = cvA.take(4 * 24).rearrange("p (c b t) -> p c b t", c=4, b=4); zsr = Res("zs")
            zsel = cvA.take(40, F32).rearrange("p (c t) -> p c t", c=4)
            base3 = cvA.o
            KT = cvA.take(2 * 4096).rearrange("p (h t) -> p h t", h=2); KTr = Res("KT")
            KTs = cvA.take(2 * 16).rearrange("p (h t) -> p h t", h=2)
            hx = cvA.take(5472); hxr = Res("hx")
            vh = hx[:, 0:5460].rearrange("p (j h d) -> p j h d", j=21, h=4)
            baseU = cvA.o
            cvP = Carver(baseU)
            kvst = [cvP.take(512, F32) for _ in range(2)]; kvstr = [Res("kvst0"), Res("kvst1")]
            gst = cvP.take(8 * 512).rearrange("p (r c) -> p r c", r=8); gstr = Res("gst")
            tmpb = cvP.take(512); tmpr = Res("tmpb")
            sqb = [cvP.take(8 * 512).rearrange("p (k t) -> p k t", k=8) for _ in range(2)]; sqr = [Res("sqb0"), Res("sqb1")]
            rs = [cvP.take(512, F32) for _ in range(2)]; rsr = [Res("rs0"), Res("rs1")]
            xt = [cvP.take(4096, F32).rearrange("p (k t) -> p k t", k=8) for _ in range(2)]; xtr = [Res("xt0"), Res("xt1")]

            gpm = vecs[:, l, 0, :]; gqm = vecs[:, l, 1, :]; gpf = vecs[:, l, 2, :]; gqf = vecs[:, l, 3, :]

            rmsnorm_h(gpm, hT, hTr, sqb, sqr, rs, rsr, xt, xtr, tiles=(0, 1))
            mark(3)
            if mode == "full":
                wf32 = cvP.take(128, F32); wf32r = Res("wf32")
                gres_ = Res("gmlp_consts")
                P.dma("sp", lng[:, :], alg_d[l:l + 1, :].partition_broadcast(128), writes=[gres_])
                P.dma("sp", lnb[:, :], alb_d[l:l + 1, :].partition_broadcast(128), writes=[gres_])
                P.dma("sp", bsrf[0:1, :, :], abs_d[l:l + 1, :, :], writes=[gres_])
                P.op("dve", lambda e: e.tensor_copy(out=bsr[:], in_=bsrf[:]), reads=[gres_], writes=[gres_])
                wsf = [cvP.take(128, F32) for _ in range(4)]; wsfr = [Res("wsf%d" % g) for g in range(4)]
                wsb = [cvP.take(128) for _ in range(4)]; wsbr = [Res("wsb%d" % g) for g in range(4)]
                for g in range(4):
                    P.dma("sp", wsf[g][:, :], aws_d[l, g, :, :], writes=[wsfr[g]])
                    P.op("dve", lambda e, g=g: e.tensor_tensor(out=wsb[g][:, :], in0=wsf[g][:, :], in1=tril[:, :], op=ALU.mult), reads=[wsfr[g], cres], writes=[wsbr[g]])
            wkv, wkvr, _ = wload(win_cols(l, C_K, 512))
            for ni, (c0, n) in enumerate(NTILES):
                if ni + 2 < 5:
                    rmsnorm_h(gpm, hT, hTr, sqb, sqr, rs, rsr, xt, xtr, tiles=(ni + 2,))
                for ti, (t0, m) in enumerate(TTILES):
                    if not outputs or ntile_of(t0) != ni:
                        continue
                    bk, br = bank()
                    def mm(e, t0=t0, m=m, bk=bk):
                        ins = None
                        for k in range(8):
                            ins = e.matmul(bk[0:m, :], hT[:, k, t0:t0 + m], wkv[:, k, :], start=(k == 0), stop=(k == 7))
                        return ins
                    P.op("pe", mm, reads=[wkvr, hTr[ni]], writes=[br])
                    s = ti % 2
                    P.op("act", lambda e, m=m, bk=bk, s=s: e.copy(out=kvst[s][0:m, :], in_=bk[0:m, :]), reads=[br], writes=[kvstr[s]])
                    P.dma("act", kv_d[l, t0:t0 + m, :], kvst[s][0:m, :], reads=[kvstr[s]], dsem=out_sem)
                for hp in range(2):
                    bk, br = proj_fm(wkv, wkvr, hp * 128, hT, hTr, ni)
                    if ni < 4:
                        P.op("dve", lambda e, bk=bk, hp=hp, c0=c0, n=n: e.tensor_copy(out=KT[:, hp, 2048 + c0:2048 + c0 + n], in_=bk[:, 0:n]),
                             reads=[br], writes=[KTr])
                    else:
                        P.op("dve", lambda e, bk=bk, hp=hp: e.tensor_copy(out=KTs[:, hp, :], in_=bk[:, 0:16]), reads=[br], writes=[KTr])
            mark(4)
            P.op("dve", lambda e: e.memset(hx[:, :], 1.0), writes=[hxr])
            j = 0
            for g, d in enumerate(DILS):
                nb = 16 // d
                for r in range(d):
                    st = 128 * (nb - 1) * d + r
                    bk, br = bank()
                    def mm(e, st=st, d=d, bk=bk):
                        ins = None
                        for k in range(8):
                            ins = e.matmul(bk[:, 0:256], hT[:, k, st:st + 127 * d + 1:d], wkv[:, k, 256:512], start=(k == 0), stop=(k == 7))
                        return ins
                    P.op("pe", mm, reads=[wkvr] + hTr[0:4], writes=[br])
                    P.op("act", lambda e, bk=bk, j=j: e.copy(out=vh[:, j, :, 0:64], in_=bk[:, 0:256].rearrange("p (h d) -> p h d", h=4)),
                         reads=[br], writes=[hxr])
                    j += 1
            wcx, wcxr, _ = wload(win_cols(l, C_CX, 512))
            wcc, wccr, _ = wload(win_cols(l, C_CC, 512))
            for c in range(4):
                for ni, (c0, n) in enumerate(NTILES):
                    if mode == "halo" and ni != 3:
                        continue
                    bka, bra = proj_fm(wcx, wcxr, c * 128, hT, hTr, ni)
                    bkb, brb = proj_fm(wcc, wccr, c * 128, hT, hTr, ni)
                    P.op("act", lambda e, bka=bka, n=n: e.copy(out=tmpb[:, 0:n], in_=bka[:, 0:n]), reads=[bra], writes=[tmpr])
                    if ni < 4:
                        P.op("dve", lambda e, bkb=bkb, c=c, c0=c0, n=n: e.tensor_tensor(out=zp[:, c, 2 + c0:2 + c0 + n], in0=bkb[:, 0:n], in1=tmpb[:, 0:n], op=ALU.mult),
                             reads=[brb, tmpr], writes=[zpr])
                        if ni == 3:
                            P.op("dve", lambda e, bkb=bkb, c=c: e.tensor_tensor(out=zsel[:, c, 0:2], in0=bkb[:, 510:512], in1=tmpb[:, 510:512], op=ALU.mult),
                                 reads=[brb, tmpr], writes=[zsr])
                    else:
                        def zsm(e, bkb=bkb, c=c):
                            e.tensor_tensor(out=zs[:, c, :, 2:6], in0=bkb[:, 0:16].rearrange("p (b t) -> p b t", b=4),
                                            in1=tmpb[:, 0:16].rearrange("p (b t) -> p b t", b=4), op=ALU.mult)
                            return e.tensor_tensor(out=zsel[:, c, 2:10].rearrange("p (b t) -> p b t", b=4),
                                                   in0=bkb[:, 0:16].rearrange("p (b t) -> p b t", b=4)[:, :, 2:4],
                                                   in1=tmpb[:, 0:16].rearrange("p (b t) -> p b t", b=4)[:, :, 2:4], op=ALU.mult)
                        P.op("dve", zsm, reads=[brb, tmpr], writes=[zsr])
            if mode == "full":
                bk, br = bank()
                def ztr(e, bk=bk):
                    ins = None
                    for c in range(4):
                        ins = e.transpose(bk[0:10, c * 128:(c + 1) * 128], zsel[:, c, :], identf[:, :])
                    return ins
                P.op("pe", ztr, reads=[zsr, cres], writes=[br])
                P.op("act", lambda e, bk=bk: e.copy(out=kvst[0][0:10, :], in_=bk[0:10, :]), reads=[br], writes=[kvstr[0]])
                if outputs:
                    P.dma("sp", zo_d[l, :, :], kvst[0][0:10, :], reads=[kvstr[0]], dsem=out_sem)
                P.dma("sp", kvst[1][0:8, :], sc_d[l, :, :], writes=[kvstr[1]])
                bk, br = bank()
                def str_(e, bk=bk):
                    ins = None
                    for c in range(4):
                        ins = e.transpose(bk[:, c * 8:(c + 1) * 8], kvst[1][0:8, c * 128:(c + 1) * 128], identf[0:8, 0:8])
                    return ins
                P.op("pe", str_, reads=[kvstr[1], cres], writes=[br])
                P.op("dve", lambda e, bk=bk: e.tensor_copy(out=zs[:, :, :, 0:2], in_=bk[:, 0:32].rearrange("p (c b j) -> p c b j", c=4, b=4)),
                     reads=[br], writes=[zsr])

            if mode == "full":
                for g in range(4):
                    bk, br = bank()
                    bkb = bk[:, :].bitcast(BF16)
                    P.op("pe", lambda e, bkb=bkb, g=g: e.transpose(bkb[:, 0:128], wsb[g][:, :], identb[:, :]), reads=[wsbr[g], cres], writes=[br])
                    P.op("act", lambda e, bkb=bkb, g=g: e.copy(out=wst[:, g, :], in_=bkb[:, 0:128]), reads=[br], writes=[gres_])
                P.op("dve", lambda e: e.memset(wss[:, :, :], 0.0), writes=[gres_])
                for g in range(4):
                    for b in range(4):
                        P.dma("sp", wss[b * 4:b * 4 + 4, g, b * 4:b * 4 + 4], wst[0:4, g, 0:4], reads=[gres_, cres], writes=[gres_])
                for b in range(4):
                    P.op("dve", lambda e, b=b: e.tensor_copy(out=bss[0:1, :, b * 4:b * 4 + 4], in_=bsr[0:1, :, 0:4]), reads=[gres_, cres], writes=[gres_])
            mark(5)
            if bout_d is not None:
                bres = Res("bounce")
                for hp in range(2):
                    P.dma("sp", bout_d[:, hp * 2048:(hp + 1) * 2048], KT[:, hp, 2048:4096], reads=[KTr], writes=[bres])
                P.op("dve", lambda e: e.tensor_copy(out=hx[:, 5460:5468].rearrange("p (c t) -> p c t", c=4), in_=zp[:, :, 2048:2050]),
                     reads=[zpr], writes=[hxr])
                P.dma("sp", bout_d[:, 4096:XC], hx[:, :], reads=[hxr], writes=[bres])
            P.barrier()
            if mode == "halo":
                return
            for hp in range(2):
                P.dma("sp", KT[:, hp, 0:2048], bin_d[:, hp * 2048:(hp + 1) * 2048], writes=[KTr])
            P.dma("sp", hx[:, :], bin_d[:, 4096:XC], writes=[hxr])
            for i in range(0, 5472, 1368):
                P.op("dve", lambda e, i=i: e.tensor_scalar(hx[:, i:i + 1368], hx[:, i:i + 1368], fcol, None, ALU.mult),
                     reads=[hxr, cres], writes=[hxr])
            P.op("dve", lambda e: e.tensor_copy(out=zp[:, :, 0:2], in_=hx[:, 5460:5468].rearrange("p (c t) -> p c t", c=4)),
                 reads=[hxr], writes=[zpr])
            P.barrier()

            mark(6)
            cvB = Carver(baseU)
            QT = cvB.take(3 * NT).rearrange("p (g t) -> p g t", g=3); QTr = Res("QT")
            numT = cvB.take(2 * NT, F32).rearrange("p (h t) -> p h t", h=2); numr = Res("numT")
            VAs = cvB.take(130).rearrange("p (h d) -> p h d", h=2)
            VT = cvB.take(NT); VTr = Res("VT")
            baseV = cvB.o
            for hp in range(2):
                cvB = Carver(baseV)
                VA = [cvB.take(n * 130).rearrange("p (j h d) -> p j h d", j=n, h=2) for n in (17, 20, 32)]; VAr = Res("VA")
                pexp = [cvB.take(512) for _ in range(2)]; pexpr = [Res("pexp0"), Res("pexp1")]
                ptm = [cvB.take(512) for _ in range(2)]; ptr_ = [Res("pt0"), Res("pt1")]
                wqa, wqar, _ = wload(win_cols(l, C_Q, 512))
                wqb, wqbr, _ = wload(win_cols(l, C_Q + 512, 256))
                for g in range(3):
                    qc = g * 256 + hp * 128
                    ws_, wr_, wc0 = (wqa, wqar, qc) if qc < 512 else (wqb, wqbr, qc - 512)
                    for ni, (c0, n) in enumerate(NTILES):
                        bk, br = proj_fm(ws_, wr_, wc0, hT, hTr, ni)
                        P.op("act", lambda e, bk=bk, g=g, c0=c0, n=n: e.copy(out=QT[:, g, c0:c0 + n], in_=bk[:, 0:n]), reads=[br], writes=[QTr])
                wv, wvr, _ = wload(win_cols(l, C_V + hp * 128, 128))
                if outputs:
                    cvK = Carver(baseV + 11018)
                    kc = cvK.take(36 * 128).rearrange("p (j c) -> p j c", j=36)
                    vc = cvK.take(36 * 128).rearrange("p (j c) -> p j c", j=36)
                    if hp == 0:
                        kcr = Res("kc"); vcr = Res("vc")
                    cache_loads = []
                    for b in range(4):
                        for (cd, dst, rr) in ((ck_d, kc, kcr), (cv_d, vc, vcr)):
                            cs = slice(hp * 128, hp * 128 + 128)
                            cache_loads.append((dst[:, b * 9, :], cd[l, b, 1920:2048, cs], rr))
                            cache_loads.append((dst[:, b * 9 + 1:b * 9 + 5, :], cd[l, b, 1536:2048, cs].rearrange("(p r) c -> p r c", r=4), rr))
                            cache_loads.append((dst[:, b * 9 + 5:b * 9 + 9, :], cd[l, b, :, cs].rearrange("(p s) c -> p s c", s=16)[:, 0:4, :], rr))
                else:
                    cache_loads = []
                for g in range(3):
                    P.op("dve", lambda e, g=g: e.memset(VA[g][:, :, :, 64:65], 1.0), writes=[VAr])
                P.op("dve", lambda e: e.memset(VAs[0:16, :, 64:65], 1.0), writes=[VAr])
                for ni, (c0, n) in enumerate(NTILES):
                    bk, br = proj_fm(wv, wvr, 0, hT, hTr, ni)
                    P.op("act", lambda e, bk=bk, c0=c0, n=n: e.copy(out=VT[:, c0:c0 + n], in_=bk[:, 0:n]), reads=[br], writes=[VTr])
                for g, d in enumerate(DILS):
                    nb = 16 // d
                    blocks = [(r, n) for r in range(d) for n in range(nb)]
                    for b0 in range(0, 16, 4):
                        bk, br = bank()
                        bkb = bk[:, :].bitcast(BF16)
                        def trv(e, b0=b0, bkb=bkb, d=d, blocks=blocks):
                            ins = None
                            for jj in range(4):
                                r, n = blocks[b0 + jj]
                                st = 128 * n * d + r
                                ins = e.transpose(bkb[:, jj * 128:(jj + 1) * 128], VT[:, st:st + 127 * d + 1:d], identb[:, :])
                            return ins
                        P.op("pe", trv, reads=[VTr, cres], writes=[br])
                        r0, n0 = blocks[b0]
                        t_first = r0 * (nb + 1) + n0 + 1
                        step = 1 if nb >= 4 else (nb + 1)
                        P.op("dve", lambda e, bkb=bkb, g=g, t_first=t_first, step=step: e.tensor_copy(
                            out=VA[g][:, t_first:t_first + 3 * step + 1:step, :, 0:64],
                            in_=bkb[:, 0:512].rearrange("p (j h d) -> p j h d", j=4, h=2)), reads=[br], writes=[VAr])
                bk, br = bank()
                bkb = bk[:, :].bitcast(BF16)
                P.op("pe", lambda e, bkb=bkb: e.transpose(bkb[0:16, 0:128], VT[:, 2048:2064], identb[:, :]), reads=[VTr, cres], writes=[br])
                P.op("dve", lambda e, bkb=bkb: e.tensor_copy(out=VAs[0:16, :, 0:64], in_=bkb[0:16, 0:128].rearrange("p (h d) -> p h d", h=2)),
                     reads=[br], writes=[VAr])
                P.op("dve", lambda e, hp=hp: e.tensor_copy(out=VA[0][:, 0, :, :], in_=vh[:, 0, hp * 2:hp * 2 + 2, :]), reads=[hxr], writes=[VAr])
                P.op("dve", lambda e, hp=hp: e.tensor_copy(out=VA[1][:, 0:20:5, :, :], in_=vh[:, 1:5, hp * 2:hp * 2 + 2, :]), reads=[hxr], writes=[VAr])
                P.op("dve", lambda e, hp=hp: e.tensor_copy(out=VA[2][:, 0:32:2, :, :], in_=vh[:, 5:21, hp * 2:hp * 2 + 2, :]), reads=[hxr], writes=[VAr])

                mark(7)
                blks = []
                for g, d in enumerate(DILS):
                    nb = 16 // d
                    for r in range(d):
                        for n in range(nb):
                            st = 128 * n * d + r
                            blks.append(dict(
                                g=g, tl=r * (nb + 1) + n, s=len(blks) % 2,
                                sl_q=slice(st, st + 127 * d + 1, d),
                                sl_k=[slice(2048 + st - 128 * d, 2048 + st - 128 * d + 127 * d + 1, d),
                                      slice(2048 + st, 2048 + st + 127 * d + 1, d)]))
                pexh = [[Res("pexp%d_%d" % (s_, h_)) for h_ in range(2)] for s_ in range(2)]

                def S_stage(B):
                    s = B["s"]
                    for hh in range(2):
                        bk, br = bank()
                        def mmS(e):
                            ins = None
                            for kb in range(2):
                                ins = e.matmul(bk[:, kb * 128:(kb + 1) * 128],
                                               KT[hh * 64:(hh + 1) * 64, hp, B["sl_k"][kb]], QT[hh * 64:(hh + 1) * 64, B["g"], B["sl_q"]],
                                               start=True, stop=True)
                            return ins
                        P.op("pe", mmS, reads=[KTr, QTr], writes=[br])
                        P.op("act", lambda e: e.activation(out=pexp[s][:, hh * 256:(hh + 1) * 256], in_=bk[:, 0:256], func=AF.Exp, scale=0.125),
                             reads=[br], writes=[pexh[s][hh]])

                def M_stage(B):
                    s = B["s"]
                    P.op("dve" if B["s"] == 0 else "pool", lambda e: e.tensor_tensor(
                        out=ptm[s][:, :], in0=pexp[s][:, :],
                        in1=ep[:, B["g"], hp * 2:hp * 2 + 2, :, :].rearrange("p h k q -> p (h k q)"), op=ALU.mult),
                        reads=[pexh[s][0], pexh[s][1], cres], writes=[ptr_[s]])

                def V_stage(B):
                    s = B["s"]
                    bk2, br2 = bank()
                    def mmV(e):
                        ins = None
                        for hh in range(2):
                            for kb in range(2):
                                ins = e.matmul(bk2[0:65, hh * 128:(hh + 1) * 128], VA[B["g"]][:, B["tl"] + kb, hh, :],
                                               ptm[s][:, (hh * 2 + kb) * 128:(hh * 2 + kb + 1) * 128],
                                               start=(kb == 0), stop=(kb == 1))
                        return ins
                    P.op("pe", mmV, reads=[VAr, ptr_[s]], writes=[br2])
                    if B["g"] == 0:
                        P.op("dve", lambda e: e.tensor_copy(
                            out=numT[0:65, :, B["sl_q"]], in_=bk2[0:65, 0:256].rearrange("p (h q) -> p h q", h=2)),
                            reads=[br2], writes=[numr])
                    else:
                        P.op("dve", lambda e: e.tensor_tensor(
                            out=numT[0:65, :, B["sl_q"]], in0=numT[0:65, :, B["sl_q"]],
                            in1=bk2[0:65, 0:256].rearrange("p (h q) -> p h q", h=2), op=ALU.add),
                            reads=[br2], writes=[numr])

                NB_ = len(blks)
                for i in range(NB_ + 2):
                    if i % 2 == 0 and cache_loads:
                        dst_, src_, rr_ = cache_loads.pop(0)
                        P.dma("pool", dst_, src_, writes=[rr_])
                    if i < NB_:
                        S_stage(blks[i])
                    if 0 <= i - 1 < NB_:
                        M_stage(blks[i - 1])
                    if 0 <= i - 2 < NB_:
                        V_stage(blks[i - 2])
                while cache_loads:
                    dst_, src_, rr_ = cache_loads.pop(0)
                    P.dma("pool", dst_, src_, writes=[rr_])
                P.barrier()
                mark(8)
                if not outputs:
                    P.op("dve", lambda e: e.memset(numT[0:65, :, 2048:2064], 1.0), writes=[numr])
                else:
                    cvS = Carver(baseV)
                    ktc = cvS.take(36 * 128).rearrange("p (j c) -> p j c", j=36); ktcr = Res("ktc")
                    vas = cvS.take(36 * 130).rearrange("p (j h d) -> p j h d", j=36, h=2); vasr = Res("vas")
                    psm = cvS.take(288); psmr = Res("psm")
                    psn = cvS.take(96); psnr = Res("psn")
                    P.op("dve", lambda e: e.memset(vas[:, :, :, 64:65], 1.0), writes=[vasr])
                    P.op("dve", lambda e: e.tensor_copy(out=vas[:, :, :, 0:64], in_=vc[:, :, :].rearrange("p j (h d) -> p j h d", h=2)),
                         reads=[vcr], writes=[vasr])
                    for j0 in range(0, 36, 8):
                        nj = min(8, 36 - j0)
                        bk, br = bank()
                        bkb = bk[:, :].bitcast(BF16)
                        def trk(e, bkb=bkb, j0=j0, nj=nj):
                            ins = None
                            for jj in range(nj):
                                ins = e.transpose(bkb[:, jj * 128:(jj + 1) * 128], kc[:, j0 + jj, :], identb[:, :])
                            return ins
                        P.op("pe", trk, reads=[kcr, cres], writes=[br])
                        P.op("act", lambda e, bkb=bkb, j0=j0, nj=nj: e.copy(out=ktc[:, j0:j0 + nj, :], in_=bkb[:, 0:nj * 128].rearrange("p (j c) -> p j c", j=nj)),
                             reads=[br], writes=[ktcr])
                    for bh in range(2):
                      for hh in range(2):
                        bk, br = bank()
                        def mmS2(e, bk=bk, bh=bh, hp=hp, hh=hh):
                            ins = None
                            for bb in range(2):
                                b = bh * 2 + bb
                                for tile in range(9):
                                    g = 0 if tile == 0 else (1 if tile < 5 else 2)
                                    col = (bb * 9 + tile) * 4
                                    ins = e.matmul(bk[:, col:col + 4], ktc[hh * 64:(hh + 1) * 64, b * 9 + tile, :],
                                                   QT[hh * 64:(hh + 1) * 64, g, 2048 + b * 4:2048 + b * 4 + 4], start=True, stop=True)
                            return ins
                        P.op("pe", mmS2, reads=[ktcr, QTr], writes=[br])
                        P.op("act", lambda e, bk=bk, bh=bh, hh=hh: e.activation(
                            out=psm[:, bh * 144:(bh + 1) * 144].rearrange("p (j h t) -> p j h t", j=18, h=2)[:, :, hh, :],
                            in_=bk[:, 0:72].rearrange("p (j t) -> p j t", j=18), func=AF.Exp, scale=0.125),
                             reads=[br], writes=[psmr])
                    for b in range(4):
                        P.op("dve", lambda e, b=b, hp=hp: e.tensor_tensor(
                            out=psm[:, b * 72:(b + 1) * 72].rearrange("p (j h t) -> p j h t", j=9, h=2),
                            in0=psm[:, b * 72:(b + 1) * 72].rearrange("p (j h t) -> p j h t", j=9, h=2),
                            in1=es_s[:, :, hp * 2:hp * 2 + 2, :], op=ALU.mult), reads=[psmr, cres], writes=[psmr])
                    for hh in range(2):
                        bk, br = bank()
                        def mmN(e, bk=bk, hp=hp, hh=hh):
                            ins = None
                            for g in range(3):
                                ins = e.matmul(bk[0:16, g * 16:(g + 1) * 16], KTs[hh * 64:(hh + 1) * 64, hp, :], QT[hh * 64:(hh + 1) * 64, g, 2048:2064],
                                               start=True, stop=True)
                            return ins
                        P.op("pe", mmN, reads=[KTr, QTr], writes=[br])
                        P.op("act", lambda e, bk=bk, hh=hh: e.activation(
                            out=psn[0:16, :].rearrange("p (g h q) -> p g h q", g=3, h=2)[:, :, hh, :],
                            in_=bk[0:16, 0:48].rearrange("p (g q) -> p g q", g=3), func=AF.Exp, scale=0.125), reads=[br], writes=[psnr])
                    P.op("dve", lambda e, hp=hp: e.tensor_tensor(out=psn[0:16, :].rearrange("p (g h q) -> p g h q", g=3, h=2),
                                                                  in0=psn[0:16, :].rearrange("p (g h q) -> p g h q", g=3, h=2),
                                                                  in1=en_s[0:16, :, hp * 2:hp * 2 + 2, :], op=ALU.mult), reads=[psnr, cres], writes=[psnr])
                    bk, br = bank()
                    def mmPV(e, bk=bk):
                        ins = None
                        first = True
                        for hh in range(2):
                            for g in range(3):
                                col = (g * 2 + hh) * 16
                                ins = e.matmul(bk[0:65, hh * 16:(hh + 1) * 16], VAs[0:16, hh, :], psn[0:16, col:col + 16], start=first, stop=False,
                                               skip_group_check=True)
                                first = False
                            for b in range(4):
                                for tile in range(9):
                                    col = ((b * 9 + tile) * 2 + hh) * 4
                                    ins = e.matmul(bk[0:65, hh * 16 + b * 4:hh * 16 + b * 4 + 4], vas[:, b * 9 + tile, hh, :], psm[:, col:col + 4],
                                                   start=False, stop=(tile == 8 and b == 3 and hh == 1), skip_group_check=True)
                        return ins
                    P.op("pe", mmPV, reads=[VAr, vasr, psmr, psnr], writes=[br])
                    P.op("dve", lambda e, bk=bk: e.tensor_copy(out=numT[0:65, :, 2048:2064], in_=bk[0:65, 0:32].rearrange("p (h q) -> p h q", h=2)),
                         reads=[br], writes=[numr])

                mark(9)
                P.op("dve", lambda e: e.reciprocal(out=numT[64:65, :, :], in_=numT[64:65, :, :]), reads=[numr], writes=[numr])
                for hh in range(2):
                    for ni, (c0, n) in enumerate(NTILES):
                        bk, br = bank()
                        P.op("pe", lambda e, bk=bk, hh=hh, c0=c0, n=n: e.matmul(bk[0:64, 0:n], onesf[64:65, 0:64], numT[64:65, hh, c0:c0 + n], start=True, stop=True),
                             reads=[numr, cres], writes=[br])
                        P.op("dve", lambda e, bk=bk, hh=hh, c0=c0, n=n, hp=hp: e.tensor_tensor(
                            out=outb[0:64, hp * 2 + hh, c0:c0 + n], in0=numT[0:64, hh, c0:c0 + n], in1=bk[0:64, 0:n], op=ALU.mult),
                            reads=[br, numr], writes=[outbr])
                P.barrier()

            mark(10)
            cvC = Carver(base3)
            uT = cvC.take(4 * NT).rearrange("p (c t) -> p c t", c=4); uTr = Res("uT")
            outc = cvC.take(4 * NT).rearrange("p (c t) -> p c t", c=4); outcr = Res("outc")
            merged = cvC.take(8 * NT).rearrange("p (c t) -> p c t", c=8); mergedr = [Res("mg%d" % i) for i in range(5)]
            sg = [cvC.take(512) for _ in range(3)]; sgr = [Res("sg%d" % i) for i in range(3)]
            mt = [cvC.take(512, F32) for _ in range(2)]; mtr = [Res("mt0"), Res("mt1")]
            vnb = [cvC.take(512) for _ in range(2)]; vnbr = [Res("vnb0"), Res("vnb1")]
            vtmp = [cvC.take(512, F32) for _ in range(2)]; vtmpr = [Res("vt0"), Res("vt1")]
            stats = cvC.take(16, F32); statr = Res("stats")
            ctmp = cvC.take(512, F32); ctmpr = Res("ctmp")
            wf32 = cvC.take(128, F32); wf32r = Res("wf32")
            tmpb = cvC.take(512); tmpr = Res("tmpb2")

            wua, wuar, _ = wload(win_cols(l, C_UA, 512))
            wva, wvar, _ = wload(win_cols(l, C_VA, 512))
            wcb, wcbr, _ = wload(win_cols(l, C_CB, 512))
            stats2 = [stats[:, 0:16], cvC.take(16, F32)]; statr2 = [Res("stats0"), Res("stats1")]

            def u_group(c, ni):
                c0, n = NTILES[ni]
                bk, br = proj_fm(wua, wuar, c * 128, hT, hTr, ni)
                P.op("act", lambda e: e.activation(out=uT[:, c, c0:c0 + n], in_=bk[:, 0:n], func=AF.Gelu_apprx_tanh),
                     reads=[br], writes=[uTr])

            def v_front(ti):
                t0, m = TTILES[ti]
                s = ti % 2
                st_, str2 = stats2[s], statr2[s]
                bk, br = bank()
                def mm(e):
                    ins = None
                    for k in range(8):
                        ins = e.matmul(bk[0:m, :], hT[:, k, t0:t0 + m], wva[:, k, :], start=(k == 0), stop=(k == 7))
                    return ins
                P.op("pe", mm, reads=[wvar, hTr[ntile_of(t0)]], writes=[br])
                P.op("act", lambda e: e.activation(out=vtmp[s][0:m, :], in_=bk[0:m, :], func=AF.Gelu_apprx_tanh), reads=[br], writes=[vtmpr[s]])
                P.chain("dve", [
                    lambda e: e.bn_stats(out=st_[0:m, 0:6], in_=vtmp[s][0:m, :]),
                    lambda e: e.bn_aggr(out=st_[0:m, 6:8], in_=st_[0:m, 0:6]),
                    lambda e: e.tensor_scalar(st_[0:m, 8:9], st_[0:m, 7:8], EPS, None, ALU.add)],
                    reads=[vtmpr[s]], writes=[str2])
                P.op("act", lambda e: e.activation(out=st_[0:m, 9:10], in_=st_[0:m, 8:9], func=AF.Sqrt), reads=[str2], writes=[str2])
                fns = [
                    lambda e: e.reciprocal(out=st_[0:m, 9:10], in_=st_[0:m, 9:10]),
                    lambda e: e.scalar_tensor_tensor(out=vtmp[s][0:m, :], in0=vtmp[s][0:m, :], scalar=st_[0:m, 6:7], in1=lng[0:m, :],
                                                     op0=ALU.subtract, op1=ALU.mult)]
                if ti == 16:
                    fns.append(lambda e: e.scalar_tensor_tensor(out=vtmp[s][0:m, :], in0=vtmp[s][0:m, :], scalar=st_[0:m, 9:10], in1=lnb[0:m, :],
                                                                op0=ALU.mult, op1=ALU.add))
                    fns.append(lambda e: e.tensor_copy(out=vnb[s][0:m, :], in_=vtmp[s][0:m, :]))
                else:
                    fns.append(lambda e: e.scalar_tensor_tensor(out=vnb[s][0:m, :], in0=vtmp[s][0:m, :], scalar=st_[0:m, 9:10], in1=lnb[0:m, :],
                                                                op0=ALU.mult, op1=ALU.add))
                P.chain("dve", fns, reads=[str2, cres, gres_], writes=[vtmpr[s], vnbr[s], str2])
                if ti == 16 and outputs:
                    P.dma("sp", vn_d[l, :, :], vtmp[s][0:16, :], reads=[vtmpr[s]], dsem=out_sem)

            def v_back(ti):
                t0, m = TTILES[ti]
                s = ti % 2
                bk2, br2 = bank()
                def mmg(e):
                    ins = None
                    for g in range(4):
                        if ti < 16:
                            e.matmul(bk2[:, g * 128:(g + 1) * 128], vnb[s][:, g * 128:(g + 1) * 128], wst[:, g, :], start=True, stop=False)
                            ins = e.matmul(bk2[:, g * 128:(g + 1) * 128], onesb[0:1, 0:128], bsr[0:1, g, :], start=False, stop=True)
                        else:
                            e.matmul(bk2[:, g * 16:(g + 1) * 16], vnb[s][0:16, g * 128:(g + 1) * 128], wss[0:16, g, :], start=True, stop=False)
                            ins = e.matmul(bk2[:, g * 16:(g + 1) * 16], onesb[0:1, 0:128], bss[0:1, g, :], start=False, stop=True)
                    return ins
                P.op("pe", mmg, reads=[vnbr[s], cres, gres_], writes=[br2])
                P.op("dve", lambda e: e.tensor_tensor(
                    out=uT[:, :, t0:t0 + m], in0=uT[:, :, t0:t0 + m], in1=bk2[:, 0:4 * m].rearrange("p (g t) -> p g t", g=4), op=ALU.mult),
                    reads=[br2, uTr], writes=[uTr])

            def cb_group(c, ni):
                c0, n = NTILES[ni]
                bk, br = proj_fm(wcb, wcbr, c * 128, hT, hTr, ni)
                if ni < 4:
                    z0, z1, z2 = zp[:, c, c0:c0 + n], zp[:, c, c0 + 1:c0 + 1 + n], zp[:, c, c0 + 2:c0 + 2 + n]
                    a = ctmp[:, 0:n]; o_ = outc[:, c, c0:c0 + n]; cbp = bk[:, 0:n]
                else:
                    z0, z1, z2 = zs[:, c, :, 0:4], zs[:, c, :, 1:5], zs[:, c, :, 2:6]
                    a = ctmp[:, 0:16].rearrange("p (b t) -> p b t", b=4)
                    o_ = outc[:, c, 2048:2064].rearrange("p (b t) -> p b t", b=4)
                    cbp = bk[:, 0:16].rearrange("p (b t) -> p b t", b=4)
                P.chain("dve", [
                    lambda e: e.tensor_scalar(a, z2, cwc[:, l, 2, c:c + 1], cwc[:, l, 3, c:c + 1], ALU.mult, ALU.add),
                    lambda e: e.scalar_tensor_tensor(out=a, in0=z1, scalar=cwc[:, l, 1, c:c + 1], in1=a, op0=ALU.mult, op1=ALU.add),
                    lambda e: e.scalar_tensor_tensor(out=a, in0=z0, scalar=cwc[:, l, 0, c:c + 1], in1=a, op0=ALU.mult, op1=ALU.add),
                    lambda e: e.tensor_tensor(out=o_, in0=a, in1=cbp, op=ALU.mult)],
                    reads=[br, zpr, zsr, cres], writes=[ctmpr, outcr])

            pending = None
            for ni in range(5):
                for c in range(4):
                    u_group(c, ni)
                tiles = [ti for ti, (t0, m) in enumerate(TTILES) if ntile_of(t0) == ni]
                for idx, ti in enumerate(tiles):
                    v_front(ti)
                    if ni < 4:
                        cb_group(idx, ni)
                    else:
                        for c in range(4):
                            cb_group(c, ni)
                    if pending is not None:
                        v_back(pending)
                    pending = ti
            v_back(pending)
            mark(11)
            mark(12)
            gview = win_d[l, :, C_GA:C_GA + 3072].rearrange("(k p) (b c) -> p k b c", p=128, b=3)
            with nc.allow_non_contiguous_dma(reason="128-column weight slices"):
                for f in range(8):
                    sg_i = ring_slot()
                    wg3 = []
                    for bq, cg in enumerate((C_GA, C_GB, C_GC)):
                        wq_, wg3r, _ = wload(win_cols(l, cg + f * 128, 128), slot=sg_i, off=bq * 1024)
                        wg3.append(wq_)
                    si = sg_i
                    wA, wbr_r, _ = wload(wba_d[l, :, f * 128:(f + 1) * 128].rearrange("(k p) c -> p k c", p=128), slot=si, off=3072)
                    wC, _, _ = wload(wbc_d[l, :, f * 128:(f + 1) * 128].rearrange("(k p) c -> p k c", p=128), slot=si, off=3584)
                    wB, _, _ = wload(wbb_d[l, :, f * 128:(f + 1) * 128].rearrange("(k p) c -> p k c", p=64), slot=si, off=4096)
                    for ni, (c0, n) in enumerate(NTILES):
                        prods = []
                        for bi_ in range(3):
                            bk, br = bank()
                            def mmgate(e, bk=bk, bi_=bi_, c0=c0, n=n, wg3=wg3):
                                ins = None
                                for k in range(8):
                                    ins = e.matmul(bk[:, 0:n], wg3[bi_][:, k, :], hT[:, k, c0:c0 + n], start=(k == 0), stop=(k == 7))
                                return ins
                            P.op("pe", mmgate, reads=[wg3r, hTr[ni]], writes=[br])
                            P.op("act", lambda e, bk=bk, bi_=bi_, n=n, f=f, l=l: e.activation(out=sg[bi_][:, 0:n], in_=bk[:, 0:n], func=AF.Sigmoid,
                                                                                          bias=bgc[:, l, bi_ * 8 + f:bi_ * 8 + f + 1]),
                                 reads=[br, cres], writes=[sgr[bi_]])
                            bk2, br2 = bank()
                            def mmbr(e, bk2=bk2, bi_=bi_, c0=c0, n=n, wA=wA, wB=wB, wC=wC):
                                ins = None
                                for k in range(4):
                                    if bi_ == 0:
                                        ins = e.matmul(bk2[:, 0:n], wA[:, k, :], uT[:, k, c0:c0 + n], start=(k == 0), stop=(k == 3))
                                    elif bi_ == 1:
                                        ins = e.matmul(bk2[:, 0:n], wB[0:64, k, :], outb[0:64, k, c0:c0 + n], start=(k == 0), stop=(k == 3))
                                    else:
                                        ins = e.matmul(bk2[:, 0:n], wC[:, k, :], outc[:, k, c0:c0 + n], start=(k == 0), stop=(k == 3))
                                return ins
                            P.op("pe", mmbr, reads=[wbr_r, uTr, outbr, outcr], writes=[br2])
                            prods.append((bk2, br2))
                        P.chain("dve", [
                            lambda e: e.tensor_tensor(out=mt[0][:, 0:n], in0=sg[0][:, 0:n], in1=prods[0][0][:, 0:n], op=ALU.mult),
                            lambda e: e.tensor_tensor(out=mt[1][:, 0:n], in0=sg[1][:, 0:n], in1=prods[1][0][:, 0:n], op=ALU.mult),
                            lambda e: e.tensor_tensor(out=mt[0][:, 0:n], in0=mt[0][:, 0:n], in1=mt[1][:, 0:n], op=ALU.add),
                            lambda e: e.tensor_tensor(out=mt[1][:, 0:n], in0=sg[2][:, 0:n], in1=prods[2][0][:, 0:n], op=ALU.mult),
                            lambda e: e.tensor_tensor(out=merged[:, f, c0:c0 + n], in0=mt[0][:, 0:n], in1=mt[1][:, 0:n], op=ALU.add)],
                            reads=[sgr[0], sgr[1], sgr[2], prods[0][1], prods[1][1], prods[2][1]], writes=[mtr[0], mtr[1], mergedr[ni]])
            P.barrier()
            mark(13)
            cvD = Carver(0)
            mixf2 = [cvD.take(8 * 512, F32).rearrange("p (k t) -> p k t", k=8) for _ in range(2)]; mixr2 = [[Res("mixf0a"), Res("mixf0b")], [Res("mixf1a"), Res("mixf1b")]]
            sqb2 = [cvD.take(8 * 512).rearrange("p (k t) -> p k t", k=8) for _ in range(2)]; sqr2 = [Res("sqb20"), Res("sqb21")]
            rs2 = [cvD.take(512, F32) for _ in range(2)]; rsr2 = [Res("rs20"), Res("rs21")]
            assert cvD.o <= base3
            cvD2 = Carver(base3)
            xt2 = [cvD2.take(4096, F32).rearrange("p (k t) -> p k t", k=8) for _ in range(2)]; xtr2 = [Res("xt20"), Res("xt21")]
            wos = [wload(wo_d[l, :, half * 512:(half + 1) * 512].rearrange("(k p) c -> p k c", p=128)) for half in range(2)]
            def wo_tile(ni):
                c0, n = NTILES[ni]
                mixf, mixr = mixf2[ni % 2], mixr2[ni % 2]
                for f in range(8):
                    bk, br = bank()
                    def mm(e):
                        ins = None
                        for k in range(8):
                            ins = e.matmul(bk[:, 0:n], wos[f // 4][0][:, k, (f % 4) * 128:(f % 4 + 1) * 128], merged[:, k, c0:c0 + n], start=(k == 0), stop=(k == 7))
                        return ins
                    P.op("pe", mm, reads=[wos[f // 4][1], mergedr[ni]], writes=[br])
                    if f % 2 == 0:
                        P.op("act", lambda e: e.copy(out=mixf[:, f, 0:n], in_=bk[:, 0:n]), reads=[br], writes=[mixr[0]])
                    else:
                        P.op("dve", lambda e: e.tensor_copy(out=mixf[:, f, 0:n], in_=bk[:, 0:n]), reads=[br], writes=[mixr[1]])
            for ni in range(5):
                wo_tile(ni)
                if ni >= 1:
                    post_norm_residual(mixf2[(ni - 1) % 2], mixr2[(ni - 1) % 2], gqm, ni - 1, sqb2, sqr2, rs2, rsr2, xt2, xtr2, nxt=ni)
            post_norm_residual(mixf2[0], mixr2[0], gqm, 4, sqb2, sqr2, rs2, rsr2, xt2, xtr2)
            P.barrier()

            mark(14)
            cvF = Carver(0)
            hF = cvF.take(8 * NT).rearrange("p (k t) -> p k t", k=8); hFr = [Res("hF%d" % i) for i in range(5)]
            sqb = [cvF.take(8 * 512).rearrange("p (k t) -> p k t", k=8) for _ in range(2)]; sqr = [Res("sqbF0"), Res("sqbF1")]
            rs = [cvF.take(512, F32) for _ in range(2)]; rsr = [Res("rsF0"), Res("rsF1")]
            actb = cvF.take(22 * 1040).rearrange("p (j t) -> p j t", j=22); actr = Res("actb")
            mixf = cvF.take(8 * 512, F32).rearrange("p (k t) -> p k t", k=8); mixr = [Res("mixFa"), Res("mixFb")]
            stmp = [cvF.take(512) for _ in range(2)]; stmpr = [Res("st0"), Res("st1")]
            xt = [cvF.take(4096, F32).rearrange("p (k t) -> p k t", k=8) for _ in range(2)]; xtr = [Res("xtF0"), Res("xtF1")]
            rmsnorm_h(gpf, hF, hFr, sqb, sqr, rs, rsr, xt, xtr, tiles=(0, 1))
            for hh_, nis in enumerate(((0, 1), (2, 3, 4))):
                base = NTILES[nis[0]][0]
                for j4 in range(0, 22, 2):
                    nj = 2
                    sgu = ring_slot()
                    wgt, wgtr, _ = wload(wg_d[l, :, j4 * 128:(j4 + nj) * 128].rearrange("(k p) c -> p k c", p=128), slot=sgu, off=0)
                    wup, wupr, _ = wload(wu_d[l, :, j4 * 128:(j4 + nj) * 128].rearrange("(k p) c -> p k c", p=128), slot=sgu, off=2048)
                    for jj in range(nj):
                        j = j4 + jj
                        for ni in nis:
                            c0, n = NTILES[ni]
                            bkg, brg = proj_fm(wgt, wgtr, jj * 128, hF, hFr, ni)
                            bku, bru = proj_fm(wup, wupr, jj * 128, hF, hFr, ni)
                            s = (j + ni) % 2
                            P.op("act", lambda e, bkg=bkg, s=s, n=n: e.activation(out=stmp[s][:, 0:n], in_=bkg[:, 0:n], func=AF.Silu), reads=[brg], writes=[stmpr[s]])
                            P.op("dve", lambda e, bku=bku, s=s, n=n, j=j, c0=c0, base=base: e.tensor_tensor(
                                out=actb[:, j, c0 - base:c0 - base + n], in0=stmp[s][:, 0:n], in1=bku[:, 0:n], op=ALU.mult),
                                reads=[bru, stmpr[s]], writes=[actr])
                    if hh_ == 0 and j4 in (0, 2, 4):
                        rmsnorm_h(gpf, hF, hFr, sqb, sqr, rs, rsr, xt, xtr, tiles=(2 + j4 // 2,))
                for ni in nis:
                    c0, n = NTILES[ni]
                    bks = [bank() for _ in range(8)]
                    for j4 in range(0, 22, 4):
                        nj = min(4, 22 - j4)
                        wdn, wdnr, _ = wload(wd_d[l, j4 * 128:(j4 + nj) * 128, :].rearrange("(j p) c -> p j c", p=128))
                        def mm(e, bks=bks, j4=j4, nj=nj, c0=c0, n=n, base=base, wdn=wdn):
                            ins = None
                            for f in range(8):
                                for jj in range(nj):
                                    j = j4 + jj
                                    ins = e.matmul(bks[f][0][:, 0:n], wdn[:, jj, f * 128:(f + 1) * 128], actb[:, j, c0 - base:c0 - base + n],
                                                   start=(j == 0), stop=(j == 21))
                            return ins
                        P.op("pe", mm, reads=[wdnr, actr], writes=[b[1] for b in bks])
                    for f in range(8):
                        if f % 2 == 0:
                            P.op("act", lambda e, f=f, n=n, bks=bks: e.copy(out=mixf[:, f, 0:n], in_=bks[f][0][:, 0:n]), reads=[bks[f][1]], writes=[mixr[0]])
                        else:
                            P.op("dve", lambda e, f=f, n=n, bks=bks: e.tensor_copy(out=mixf[:, f, 0:n], in_=bks[f][0][:, 0:n]), reads=[bks[f][1]], writes=[mixr[1]])
                    nxt_ = nis[nis.index(ni) + 1] if nis.index(ni) + 1 < len(nis) else None
                    post_norm_residual(mixf, mixr, gqf, ni, sqb, sqr, rs, rsr, xt, xtr, nxt=nxt_)
            P.barrier()

        f1 = flagb[:, 0:1]; f2 = flagb[:, 1:2]
        layer(0, 2, None, None, bounce_d[0], "halo", False)
        layer(0, 1, bounce_d[0], f2, bounce_d[1], "full", False)
        layer(0, 0, bounce_d[1], f1, None, "full", True)
        layer(1, 1, None, None, bounce_d[2], "halo", False)
        layer(1, 0, bounce_d[2], f1, None, "full", True)
        xpark, XTr = xparks[0], XTrs[0]
        cvO = Carver(0)
        ost = [cvO.take(1024, F32) for _ in range(2)]; ostr = [Res("ost0"), Res("ost1")]
        xt = [cvO.take(4096, F32).rearrange("p (k t) -> p k t", k=8) for _ in range(2)]; xtr = [Res("xtO0"), Res("xtO1")]
        for ni, (c0, n) in enumerate(NTILES):
            xs_ = ni % 2
            P.dma("sp", xt[xs_][:, :, 0:n], xpark[:, ni, :, 0:n], reads=[XTr[ni]], writes=[xtr[xs_]])
            for lt in range(0, n, 128):
                m = min(128, n - lt)
                t0 = c0 + lt
                ti = t0 // 128
                s = ti % 2
                for half in range(2):
                    bk, br = bank()
                    def tr(e, bk=bk, half=half, lt=lt, m=m, xs_=xs_):
                        ins = None
                        for jj in range(4):
                            k = half * 4 + jj
                            ins = e.transpose(bk[0:m, jj * 128:(jj + 1) * 128], xt[xs_][:, k, lt:lt + m], identf[:, :])
                        return ins
                    P.op("pe", tr, reads=[xtr[xs_], cres], writes=[br])
                    P.op("act", lambda e, bk=bk, half=half, s=s, m=m: e.copy(out=ost[s][0:m, half * 512:(half + 1) * 512], in_=bk[0:m, :]),
                         reads=[br], writes=[ostr[s]])
                dst = yp_d[t0:t0 + m, :] if ni < 4 else ys_d[:, :]
                P.dma("act", dst, ost[s][0:m, :], reads=[ostr[s]], dsem=out_sem)
        P.barrier()

    def mk(name):
        def body(eng):
            P.reset(name, eng)
            program()
        return body
    with nc.Block() as block:
        block.tensor(mk("pe"))
        block.scalar(mk("act"))
        block.vector(mk("dve"))
        block.gpsimd(mk("pool"))
        block.sync(mk("sp"))
    es.close()
    return nc


_NC_CACHE = {}


def kernel(**inputs):
    f32 = np.float32
    inp = {k: np.ascontiguousarray(np.asarray(v), dtype=f32) for k, v in inputs.items()}
    if "nc" not in _NC_CACHE:
        _NC_CACHE["nc"] = build()
    nc = _NC_CACHE["nc"]
    consts = host_consts()
    shared = {k: inp[k] for k in ("g_pre_mix", "g_post_mix", "g_pre_ffn", "g_post_ffn", "w_in", "a_ln_g", "a_ln_b", "a_ws", "a_bs",
                                  "c_conv_w", "c_conv_b", "w_br_a", "w_br_b", "w_br_c", "b_gate", "w_o", "w_ff_gate", "w_ff_up", "w_ff_down")}
    in_maps = []
    for c in range(NCORES):
        b, q = c // 4, c % 4
        m = dict(shared)
        m.update(consts)
        m["xp"] = np.ascontiguousarray(inp["x_prompt"][b, q * NPR:(q + 1) * NPR, :])
        m["xs"] = np.ascontiguousarray(inp["x_sample"][c * 4:(c + 1) * 4].reshape(NSM, D))
        m["ck"] = np.ascontiguousarray(inp["cache_k_win"][:, c * 4:(c + 1) * 4].reshape(DEPTH, 4, 2048, 256))
        m["cv"] = np.ascontiguousarray(inp["cache_v_win"][:, c * 4:(c + 1) * 4].reshape(DEPTH, 4, 2048, 256))
        m["sc"] = np.ascontiguousarray(inp["state_conv"][:, c * 4:(c + 1) * 4].reshape(DEPTH, 8, 512))
        q1, q2 = max(q - 1, 0), max(q - 2, 0)
        m["xp1"] = np.ascontiguousarray(inp["x_prompt"][b, q1 * NPR:(q1 + 1) * NPR, :])
        m["xp2"] = np.ascontiguousarray(inp["x_prompt"][b, q2 * NPR:(q2 + 1) * NPR, :])
        m["flags"] = np.array([[1.0 if q >= 1 else 0.0, 1.0 if q >= 2 else 0.0]], f32)
        in_maps.append(m)
    res = run_bass_kernel_spmd(nc, in_maps, core_ids=list(range(NCORES)))
    shp = {"yp": (NPR, D), "ys": (NSM, D), "kv": (DEPTH, NT, 512), "zo": (DEPTH, 10, 512), "vns": (DEPTH, NSM, 512)}
    R = [{k: np.asarray(r[k]).reshape(shp[k]) for k in shp} for r in res.results]
    yp = np.stack([np.concatenate([R[b * 4 + q]["yp"] for q in range(4)], axis=0) for b in range(2)]).astype(f32)
    ys = np.concatenate([R[c]["ys"].reshape(4, 4, D) for c in range(NCORES)], axis=0).astype(f32)
    kp = np.stack([np.stack([R[b * 4 + 3]["kv"][l, 0:NPR, 0:256].reshape(NPR, 4, 64) for b in range(2)]) for l in range(DEPTH)]).astype(f32)
    vp = np.stack([np.stack([R[b * 4 + 3]["kv"][l, 0:NPR, 256:512].reshape(NPR, 4, 64) for b in range(2)]) for l in range(DEPTH)]).astype(f32)
    ksm = np.stack([np.concatenate([R[c]["kv"][l, NPR:NT, 0:256].reshape(4, 4, 4, 64) for c in range(NCORES)], axis=0) for l in range(DEPTH)]).astype(f32)
    vsm = np.stack([np.concatenate([R[c]["kv"][l, NPR:NT, 256:512].reshape(4, 4, 4, 64) for c in range(NCORES)], axis=0) for l in range(DEPTH)]).astype(f32)
    cp = np.stack([np.stack([R[b * 4 + 3]["zo"][l, 0:2, :] for b in range(2)]) for l in range(DEPTH)]).astype(f32)
    cs = np.stack([np.concatenate([R[c]["zo"][l, 2:10, :].reshape(4, 2, 512) for c in range(NCORES)], axis=0) for l in range(DEPTH)]).astype(f32)
    av = np.stack([np.concatenate([R[c]["vns"][l].reshape(4, 4, 512) for c in range(NCORES)], axis=0) for l in range(DEPTH)]).astype(f32)
    return (yp, ys, kp, vp, ksm, vsm, cp, cs, av)
```

```python
import os
import numpy as np
import ml_dtypes
from contextlib import ExitStack
import concourse.bass as bass
import concourse.mybir as mybir
from concourse.bass_utils import run_bass_kernel_spmd

F32 = mybir.dt.float32
BF16 = mybir.dt.bfloat16
AF = mybir.ActivationFunctionType
ALU = mybir.AluOpType
AX = mybir.AxisListType

NCORES = 8
D = 1024
DEPTH = 2
NPR = 2048
NSM = 16
NT = NPR + NSM
INW = 6912
DFF = 2816
NTILES = [(0, 512), (512, 512), (1024, 512), (1536, 512), (2048, 16)]
TTILES = [(i * 128, 128) for i in range(16)] + [(2048, 16)]
DILS = (1, 4, 16)
EPS = 1e-6
XC = 9568
C_UA, C_VA, C_Q, C_K, C_V, C_CX, C_CB, C_CC, C_GA, C_GB, C_GC = 0, 512, 1024, 1792, 2048, 2304, 2816, 3328, 3840, 4864, 5888
SAME_ENG_SYNC = True


KSTOP = int(os.environ.get("KSTOP", "-1"))


class Stop(Exception):
    pass


class Res:
    __slots__ = ("name", "w", "r")

    def __init__(self, name):
        self.name = name
        self.w = None
        self.r = []


class DSem:
    def __init__(self, handle, idx):
        self.h = handle
        self.idx = idx
        self.total = 0


class Prog:
    ENG = ("pe", "act", "dve", "pool", "sp")
    NPOOL = 24

    def __init__(self, nc, es):
        self.nc, self.es = nc, es
        self.sem = {e: es.enter_context(nc.semaphore("s_" + e)) for e in self.ENG}
        self.handles = []
        self.named = {}
        self.reset(None, None)

    def reset(self, cur, eobj):
        self.cur, self.eobj = cur, eobj
        self.cnt = {e: 0 for e in self.ENG}
        self.seen = {}
        self.dsems = []
        self.hidx = 0
        self.pool = None
        self.pool_i = 0
        self.pend = None

    def named_sem(self, name):
        if name not in self.named:
            self.named[name] = self.es.enter_context(self.nc.semaphore(name))
        return self.named[name]

    def new_dsem(self):
        if self.hidx >= len(self.handles):
            self.handles.append(self.es.enter_context(self.nc.semaphore("d%d" % len(self.handles))))
        d = DSem(self.handles[self.hidx], self.hidx)
        self.hidx += 1
        self.dsems.append(d)
        return d

    @staticmethod
    def _flat(rs):
        out = []
        for r in rs:
            if isinstance(r, (list, tuple)):
                out.extend(Prog._flat(r))
            else:
                out.append(r)
        return out

    def _waits(self, eng, reads, writes):
        reads, writes = self._flat(reads), self._flat(writes)
        toks = []
        for r in reads:
            if r.w is not None:
                toks.append(r.w)
        for r in writes:
            if r.w is not None:
                toks.append(r.w)
            toks.extend(r.r)
        need = {}
        for (h, val, key, src) in toks:
            if src == eng and (eng == "pe" or not SAME_ENG_SYNC) and key == "c_" + eng:
                continue
            if need.get(key, (None, 0))[1] < val:
                need[key] = (h, val)
        waits = []
        for key, (h, val) in need.items():
            if self.seen.get((eng, key), 0) >= val:
                continue
            self.seen[(eng, key)] = val
            waits.append((h, val))
        return waits

    def _commit(self, tok, reads, writes):
        reads, writes = self._flat(reads), self._flat(writes)
        for r in reads:
            r.r.append(tok)
        for r in writes:
            r.w = tok
            r.r = []

    def _emit(self, eng, waits, fn, sh, inc):
        if eng != self.cur:
            return
        for h, v in waits:
            self.eobj.wait_ge(h, v)
        if fn is not None:
            ins = fn(self.eobj)
            if inc:
                ins.then_inc(sh, inc)

    def op(self, eng, fn, reads=(), writes=()):
        self.flush()
        waits = self._waits(eng, reads, writes)
        self.cnt[eng] += 1
        tok = (self.sem[eng], self.cnt[eng], "c_" + eng, eng)
        self._emit(eng, waits, fn, self.sem[eng], 1)
        self._commit(tok, reads, writes)
        return tok

    def dma(self, eng, out, in_, reads=(), writes=(), dsem=None, ring=False):
        if not ring:
            self.flush()
        extra = []
        if dsem is None:
            if self.pool is None:
                self.pool = {"sp": [self.new_dsem() for _ in range(self.NPOOL)],
                             "pool": [self.new_dsem() for _ in range(16)],
                             "act": [self.new_dsem() for _ in range(8)]}
                self.pool_i = {"sp": 0, "pool": 0, "act": 0}
            pl = self.pool[eng]
            dsem = pl[self.pool_i[eng] % len(pl)]
            self.pool_i[eng] += 1
            key = "d_%d" % dsem.idx
            if dsem.total > self.seen.get((eng, key), 0):
                self.seen[(eng, key)] = dsem.total
                extra.append((dsem.h, dsem.total))
        waits = extra + self._waits(eng, reads, writes)
        dsem.total += 16
        tok = (dsem.h, dsem.total, "d_%d" % dsem.idx, "dma")
        self._emit(eng, waits, lambda e: e.dma_start(out=out, in_=in_, allow_slow_non_contiguous=True), dsem.h, 16)
        self._commit(tok, reads, writes)
        return tok

    def chain(self, eng, fns, reads=(), writes=()):
        cr = Res("chain")
        tok = None
        for fn in fns:
            tok = self.op(eng, fn, reads=list(reads) + [cr], writes=list(writes) + [cr])
        return tok

    def custom(self, eng, fn, sem_h, inc, key, total, reads=(), writes=()):
        self.flush()
        waits = self._waits(eng, reads, writes)
        tok = (sem_h, total, key, "dma")
        self._emit(eng, waits, fn, sem_h, inc)
        self._commit(tok, reads, writes)
        return tok

    def barrier(self):
        self.flush()
        self.pend = (dict(self.cnt), [(d, d.total) for d in self.dsems])

    def flush(self):
        if self.pend is None:
            return
        cnt, dtot = self.pend
        self.pend = None
        for e in self.ENG:
            waits = []
            for x in self.ENG:
                if x == e:
                    continue
                key = "c_" + x
                if cnt[x] > self.seen.get((e, key), 0):
                    self.seen[(e, key)] = cnt[x]
                    waits.append((self.sem[x], cnt[x]))
            for d, tot in dtot:
                key = "d_%d" % d.idx
                if tot > self.seen.get((e, key), 0):
                    self.seen[(e, key)] = tot
                    waits.append((d.h, tot))
            self._emit(e, waits, None, None, 0)


def alibi_slopes():
    return np.array([2.0 ** (-8.0 * (i + 1) / 12) for i in range(12)], np.float64).reshape(3, 4)


def host_consts():
    bf = ml_dtypes.bfloat16
    c = {}
    c["identb"] = np.eye(128, dtype=np.float32).astype(bf)
    c["identf"] = np.eye(128, dtype=np.float32)
    c["onesb"] = np.ones((128, 128), np.float32).astype(bf)
    c["onesf"] = np.ones((128, 64), np.float32)
    c["tril"] = np.tril(np.ones((128, 128), np.float32))
    sl = alibi_slopes()
    k = np.arange(128)[:, None]
    q = np.arange(128)[None, :]
    ep = np.zeros((128, 3, 4, 2, 128), np.float64)
    for g, d in enumerate(DILS):
        for h in range(4):
            dist0 = q + 128 - k
            ep[:, g, h, 0, :] = np.where(k >= q, np.exp(-sl[g, h] * d * dist0), 0.0)
            dist1 = q - k
            ep[:, g, h, 1, :] = np.where(k <= q, np.exp(-sl[g, h] * d * dist1), 0.0)
    c["ep"] = ep.reshape(128, 3 * 4 * 2 * 128).astype(np.float32).astype(bf)
    es_ = np.zeros((128, 9, 4, 4), np.float64)
    p = np.arange(128)
    for h in range(4):
        for t in range(4):
            tap = 128 + t - p
            es_[:, 0, h, t] = np.where(p >= t, np.exp(-sl[0, h] * 1 * tap), 0.0)
            tap = 128 - p
            es_[:, 1 + t, h, t] = np.exp(-sl[1, h] * 4 * tap)
            es_[:, 5 + t, h, t] = np.exp(-sl[2, h] * 16 * tap)
    c["es"] = es_.reshape(128, 144).astype(np.float32).astype(bf)
    en = np.zeros((16, 3, 4, 16), np.float64)
    for b in range(4):
        for t2 in range(4):
            for t in range(4):
                if t2 > t:
                    continue
                for g, d in enumerate(DILS):
                    if (t - t2) % d != 0 or (t - t2) // d > 128:
                        continue
                    for h in range(4):
                        en[b * 4 + t2, g, h, b * 4 + t] = np.exp(-sl[g, h] * (t - t2))
    c["en"] = en.reshape(16, 192).astype(np.float32).astype(bf)
    return c


def build():
    nc = bass.Bass("TRN2", target_bir_lowering=False)
    es = ExitStack()
    P = Prog(nc, es)

    def din(name, shape, dt=F32):
        return nc.dram_tensor(name, list(shape), dt, kind="ExternalInput")

    def dout(name, shape, dt=F32):
        return nc.dram_tensor(name, list(shape), dt, kind="ExternalOutput")

    xp_d = din("xp", [NPR, D]); xs_d = din("xs", [NSM, D])
    ck_d = din("ck", [DEPTH, 4, 2048, 256]); cv_d = din("cv", [DEPTH, 4, 2048, 256])
    sc_d = din("sc", [DEPTH, 8, 512])
    gpm_d = din("g_pre_mix", [DEPTH, D]); gqm_d = din("g_post_mix", [DEPTH, D])
    gpf_d = din("g_pre_ffn", [DEPTH, D]); gqf_d = din("g_post_ffn", [DEPTH, D])
    win_d = din("w_in", [DEPTH, D, INW])
    alg_d = din("a_ln_g", [DEPTH, 512]); alb_d = din("a_ln_b", [DEPTH, 512])
    aws_d = din("a_ws", [DEPTH, 4, 128, 128]); abs_d = din("a_bs", [DEPTH, 4, 128])
    cw_d = din("c_conv_w", [DEPTH, 3, 512]); cbias_d = din("c_conv_b", [DEPTH, 512])
    wba_d = din("w_br_a", [DEPTH, 512, D]); wbb_d = din("w_br_b", [DEPTH, 256, D]); wbc_d = din("w_br_c", [DEPTH, 512, D])
    bg_d = din("b_gate", [DEPTH, 3 * D]); wo_d = din("w_o", [DEPTH, D, D])
    wg_d = din("w_ff_gate", [DEPTH, D, DFF]); wu_d = din("w_ff_up", [DEPTH, D, DFF]); wd_d = din("w_ff_down", [DEPTH, DFF, D])
    flags_d = din("flags", [1, 2])
    xp1_d = din("xp1", [NPR, D]); xp2_d = din("xp2", [NPR, D])
    identb_d = din("identb", [128, 128], BF16); identf_d = din("identf", [128, 128])
    onesb_d = din("onesb", [128, 128], BF16); onesf_d = din("onesf", [128, 64])
    tril_d = din("tril", [128, 128]); ep_d = din("ep", [128, 3072], BF16)
    es_d = din("es", [128, 144], BF16); en_d = din("en", [16, 192], BF16)

    yp_d = dout("yp", [NPR, D]); ys_d = dout("ys", [NSM, D])
    kv_d = dout("kv", [DEPTH, NT, 512]); zo_d = dout("zo", [DEPTH, 10, 512]); vn_d = dout("vns", [DEPTH, NSM, 512])

    bounce_d = [nc.dram_tensor("bounce%d" % i, [128, XC], BF16) for i in range(4)]
    xpark_ds = [nc.dram_tensor("xpark%d" % i, [128, 5 * 8 * 512], F32) for i in range(3)]

    def sb(name, shape, dt=F32):
        return es.enter_context(nc.sbuf_tensor(name, list(shape), dt))

    identb = sb("identb_s", [128, 128], BF16); identf = sb("identf_s", [128, 128])
    onesb = sb("onesb_s", [128, 128], BF16); onesf = sb("onesf_s", [128, 64])
    tril = sb("tril_s", [128, 128]); ep = sb("ep_s", [128, 3, 4, 2, 128], BF16)
    es_s = sb("es_s", [128, 9, 4, 4], BF16); en_s = sb("en_s", [16, 3, 4, 16], BF16)
    flagb = sb("flagb", [128, 2])
    vecs = sb("vecs", [128, DEPTH, 4, 8])
    bgc = sb("bgc", [128, DEPTH, 24])
    cwc = sb("cwc", [128, DEPTH, 4, 4])
    lng = sb("lng", [128, 512]); lnb = sb("lnb", [128, 512])
    bsrf = sb("bsrf", [1, 4, 128]); bsr = sb("bsr", [1, 4, 128], BF16)
    wst = sb("wst", [128, 4, 128], BF16)
    wss = sb("wss", [16, 4, 16], BF16)
    bss = sb("bss", [1, 4, 16], BF16)
    RING_N = 3
    ring = [sb("ring%d" % i, [128, 4608], BF16) for i in range(RING_N)]
    ring_res = [None] * RING_N
    ring_sem = [None] * RING_N
    ring_i = [0]
    ARN = 83760
    AR = sb("arena", [128, ARN], BF16)

    banks = [es.enter_context(nc.psum_tensor("bank%d" % i, [128, 512], F32)) for i in range(8)]
    bank_res = [None] * 8
    bank_i = [0]

    def bank():
        i = bank_i[0] % 8
        bank_i[0] += 1
        return banks[i], bank_res[i]

    def ring_slot():
        i = ring_i[0] % RING_N
        ring_i[0] += 1
        return i

    def wload(src_ap, slot=None, off=0):
        i = ring_slot() if slot is None else slot
        n = 1
        for s in src_ap.shape[1:]:
            n *= s
        dst = ring[i][0:src_ap.shape[0], off:off + n]
        if len(src_ap.shape) == 3:
            dst = dst.rearrange("p (a b) -> p a b", a=src_ap.shape[1])
        elif len(src_ap.shape) == 4:
            dst = dst.rearrange("p (a b c) -> p a b c", a=src_ap.shape[1], b=src_ap.shape[2])
        P.dma("pool", dst, src_ap, writes=[ring_res[i]], dsem=ring_sem[i], ring=True)
        return dst, ring_res[i], i

    def win_cols(l, c0, n):
        return win_d[l, :, c0:c0 + n].rearrange("(k p) c -> p k c", p=128)

    class Carver:
        def __init__(self, base):
            self.o = base
        def take(self, nelem, dt=BF16, parts=128):
            if dt == F32:
                v = AR[0:parts, self.o:self.o + 2 * nelem].bitcast(F32)
                self.o += 2 * nelem
            else:
                v = AR[0:parts, self.o:self.o + nelem]
                self.o += nelem
            assert self.o <= ARN, self.o
            return v

    xparks = [x.ap().rearrange("p (n k t) -> p n k t", n=5, k=8) for x in xpark_ds]
    XTrs = [[None] * 5 for _ in range(3)]

    def ntile_of(tok):
        return min(tok // 512, 4)

    def program():
      for i in range(RING_N):
          ring_res[i] = Res("ring%d" % i)
          ring_sem[i] = P.new_dsem()
      ring_i[0] = 0
      for i in range(8):
          bank_res[i] = Res("bank%d" % i)
      bank_i[0] = 0
      for q_ in range(3):
          XTrs[q_][:] = [Res("XT%d_%d" % (q_, i)) for i in range(5)]
      try:
          program_body()
      except Stop:
          P.barrier()
      P.flush()

    CI = [0]

    def mark(k):
        if KSTOP == k or KSTOP == CI[0] * 20 + k:
            raise Stop()

    def program_body():
        CI[0] = -10
        cres = Res("consts")
        for dst, src in ((identb, identb_d), (identf, identf_d), (onesb, onesb_d), (onesf, onesf_d), (tril, tril_d)):
            P.dma("sp", dst[:, :], src[:, :], writes=[cres])
        P.dma("sp", ep[:].rearrange("p a b c d -> p (a b c d)"), ep_d[:, :], writes=[cres])
        P.dma("sp", es_s[:].rearrange("p a b c -> p (a b c)"), es_d[:, :], writes=[cres])
        P.dma("sp", en_s[:].rearrange("p a b c -> p (a b c)"), en_d[:, :], writes=[cres])
        P.dma("sp", flagb[:, :], flags_d[0:1, :].partition_broadcast(128), writes=[cres])
        with nc.allow_non_contiguous_dma(reason="tiny per-partition parameter columns"):
            for wi, gd in enumerate((gpm_d, gqm_d, gpf_d, gqf_d)):
                for l in range(DEPTH):
                    P.dma("sp", vecs[:, l, wi, :], gd[l, :].rearrange("(k p) -> p k", p=128), writes=[cres])
            for l in range(DEPTH):
                P.dma("sp", bgc[:, l, :], bg_d[l, :].rearrange("(k p) -> p k", p=128), writes=[cres])
                for i in range(3):
                    P.dma("sp", cwc[:, l, i, :], cw_d[l, i, :].rearrange("(k p) -> p k", p=128), writes=[cres])
                P.dma("sp", cwc[:, l, 3, :], cbias_d[l, :].rearrange("(k p) -> p k", p=128), writes=[cres])
        P.barrier()
        mark(1)

        out_sem = None

        cv0 = Carver(0)
        xst = [cv0.take(1024, F32) for _ in range(2)]
        xst_res = [Res("xst0"), Res("xst1")]
        xt = [cv0.take(4096, F32).rearrange("p (k t) -> p k t", k=8) for _ in range(2)]
        xtr = [Res("xt0"), Res("xt1")]
        for (xsrc_d, xpark, XTr) in ((xp2_d, xparks[2], XTrs[2]), (xp1_d, xparks[1], XTrs[1]), (xp_d, xparks[0], XTrs[0])):
          for ti, (t0, m) in enumerate(TTILES):
            s = ti % 2
            ni = ntile_of(t0)
            xs_ = ni % 2
            lc = t0 - NTILES[ni][0]
            src = xsrc_d[t0:t0 + m, :] if ti < 16 else xs_d[:, :]
            P.dma("sp", xst[s][0:m, :], src, writes=[xst_res[s]])
            for half in range(2):
                bk, br = bank()
                def tr(e, s=s, m=m, half=half, bk=bk):
                    ins = None
                    for j in range(4):
                        k = half * 4 + j
                        ins = e.transpose(bk[:, j * 128:j * 128 + m], xst[s][0:m, k * 128:(k + 1) * 128], identf[0:m, 0:m])
                    return ins
                P.op("pe", tr, reads=[xst_res[s], cres], writes=[br])
                P.op("dve", lambda e, half=half, bk=bk, lc=lc, m=m, xs_=xs_: e.tensor_copy(
                    out=xt[xs_][:, half * 4:half * 4 + 4, lc:lc + m],
                    in_=bk[:, :].rearrange("p (j t) -> p j t", j=4)[:, :, 0:m]), reads=[br], writes=[xtr[xs_]])
            if lc + m == NTILES[ni][1]:
                n = NTILES[ni][1]
                P.dma("act", xpark[:, ni, :, 0:n], xt[xs_][:, :, 0:n], reads=[xtr[xs_]], writes=[XTr[ni]])
        P.barrier()

        mark(2)
        TWIN = {}

        def norm_stats(src3, srcr, n, sqb, sqr, rs, rsr):
            sqr_b = TWIN.setdefault(id(sqr), Res("sq_twin"))
            P.op("act", lambda e: e.activation(out=sqb[:, 0:4, 0:n], in_=src3[:, 0:4, :], func=AF.Square), reads=[srcr], writes=[sqr])
            P.op("dve", lambda e: e.tensor_tensor(out=sqb[:, 4:8, 0:n], in0=src3[:, 4:8, :], in1=src3[:, 4:8, :], op=ALU.mult), reads=[srcr], writes=[sqr_b])
            bk, br = bank()
            def mm(e):
                ins = None
                for k in range(8):
                    ins = e.matmul(bk[:, 0:n], onesb[:, :], sqb[:, k, 0:n], start=(k == 0), stop=(k == 7))
                return ins
            P.op("pe", mm, reads=[sqr, sqr_b, cres], writes=[br])
            P.op("act", lambda e: e.activation(out=rs[:, 0:n], in_=bk[:, 0:n], func=AF.Sqrt, bias=EPS, scale=1.0 / D),
                 reads=[br], writes=[rsr])
            P.op("dve", lambda e: e.reciprocal(out=rs[:, 0:n], in_=rs[:, 0:n]), reads=[rsr], writes=[rsr])

        CUR = [None, None]

        def rmsnorm_h(gcol, hT, hTr, sqb, sqr, rs, rsr, xt, xtr, tiles=(0, 1, 2, 3, 4)):
            xpark, XTr = CUR
            for ni in tiles:
                c0, n = NTILES[ni]
                s = ni % 2
                P.dma("sp", xt[s][:, :, 0:n], xpark[:, ni, :, 0:n], reads=[XTr[ni]], writes=[xtr[s]])
                norm_stats(xt[s][:, :, 0:n], xtr[s], n, sqb[s], sqr[s], rs[s], rsr[s])
                def sc(e, c0=c0, n=n, s=s):
                    ins = None
                    for k in range(8):
                        ins = e.scalar_tensor_tensor(out=hT[:, k, c0:c0 + n], in0=xt[s][:, k, 0:n], scalar=gcol[:, k:k + 1],
                                                     in1=rs[s][:, 0:n], op0=ALU.mult, op1=ALU.mult)
                    return ins
                P.op("dve", sc, reads=[xtr[s], rsr[s], cres], writes=[hTr[ni]])

        def proj_fm(wslot, wres, wc0, hT, hTr, ni, kch=8):
            c0, n = NTILES[ni]
            bk, br = bank()
            def mm(e):
                ins = None
                for k in range(kch):
                    ins = e.matmul(bk[:, 0:n], wslot[:, k, wc0:wc0 + 128], hT[:, k, c0:c0 + n], start=(k == 0), stop=(k == kch - 1))
                return ins
            P.op("pe", mm, reads=[wres, hTr[ni]], writes=[br])
            return bk, br

        XTL = set()

        def post_norm_residual(mixf, mixr, gcol, ni, sqb, sqr, rs, rsr, xt, xtr, nxt=None):
            xpark, XTr = CUR
            c0, n = NTILES[ni]
            s = ni % 2
            key = (id(xtr[0]), ni)
            if key in XTL:
                XTL.discard(key)
            else:
                P.dma("sp", xt[s][:, :, 0:n], xpark[:, ni, :, 0:n], reads=[XTr[ni]], writes=[xtr[s]])
            if nxt is not None:
                n2 = NTILES[nxt][1]
                XTL.add((id(xtr[0]), nxt))
                P.dma("sp", xt[nxt % 2][:, :, 0:n2], xpark[:, nxt, :, 0:n2], reads=[XTr[nxt]], writes=[xtr[nxt % 2]])
            sqb, sqr, rs, rsr = sqb[s], sqr[s], rs[s], rsr[s]
            norm_stats(mixf[:, :, 0:n], mixr, n, sqb, sqr, rs, rsr)
            def sc1(e):
                ins = None
                for k in range(8):
                    ins = e.scalar_tensor_tensor(out=mixf[:, k, 0:n], in0=mixf[:, k, 0:n], scalar=gcol[:, k:k + 1],
                                                 in1=rs[:, 0:n], op0=ALU.mult, op1=ALU.mult)
                return ins
            P.chain("dve", [sc1, lambda e: e.tensor_tensor(out=xt[s][:, :, 0:n], in0=xt[s][:, :, 0:n], in1=mixf[:, :, 0:n], op=ALU.add)],
                    reads=[rsr, cres], writes=[mixr, xtr[s]])
            P.dma("sp", xpark[:, ni, :, 0:n], xt[s][:, :, 0:n], reads=[xtr[s]], writes=[XTr[ni]])

        def layer(l, q_, bin_d, fcol, bout_d, mode, outputs):
            CUR[0], CUR[1] = xparks[q_], XTrs[q_]
            CI[0] = CI[0] + 1 if CI[0] >= 0 else 0
            cvA = Carver(0)
            hT = cvA.take(8 * NT).rearrange("p (k t) -> p k t", k=8); hTr = [Res("hT%d" % i) for i in range(5)]
            outb = cvA.take(4 * NT).rearrange("p (h t) -> p h t", h=4); outbr = Res("outb")
            zp = cvA.take(4 * 2050).rearrange("p (c t) -> p c t", c=4); zpr = Res("zp")
            zs = cvA.take(4 * 24).rearrange("p (c b t) -> p c b t", c=4, b=4); zsr = Res("zs")
            zsel = cvA.take(40, F32).rearrange("p (c t) -> p c t", c=4)
            base3 = cvA.o
            KT = cvA.take(2 * 4096).rearrange("p (h t) -> p h t", h=2); KTr = Res("KT")
            KTs = cvA.take(2 * 16).rearrange("p (h t) -> p h t", h=2)
            hx = cvA.take(5472); hxr = Res("hx")
            vh = hx[:, 0:5460].rearrange("p (j h d) -> p j h d", j=21, h=4)
            baseU = cvA.o
            cvP = Carver(baseU)
            kvst = [cvP.take(512, F32) for _ in range(2)]; kvstr = [Res("kvst0"), Res("kvst1")]
            gst = cvP.take(8 * 512).rearrange("p (r c) -> p r c", r=8); gstr = Res("gst")
            tmpb = cvP.take(512); tmpr = Res("tmpb")
            sqb = [cvP.take(8 * 512).rearrange("p (k t) -> p k t", k=8) for _ in range(2)]; sqr = [Res("sqb0"), Res("sqb1")]
            rs = [cvP.take(512, F32) for _ in range(2)]; rsr = [Res("rs0"), Res("rs1")]
            xt = [cvP.take(4096, F32).rearrange("p (k t) -> p k t", k=8) for _ in range(2)]; xtr = [Res("xt0"), Res("xt1")]

            gpm = vecs[:, l, 0, :]; gqm = vecs[:, l, 1, :]; gpf = vecs[:, l, 2, :]; gqf = vecs[:, l, 3, :]

            rmsnorm_h(gpm, hT, hTr, sqb, sqr, rs, rsr, xt, xtr, tiles=(0, 1))
            mark(3)
            if mode == "full":
                wf32 = cvP.take(128, F32); wf32r = Res("wf32")
                gres_ = Res("gmlp_consts")
                P.dma("sp", lng[:, :], alg_d[l:l + 1, :].partition_broadcast(128), writes=[gres_])
                P.dma("sp", lnb[:, :], alb_d[l:l + 1, :].partition_broadcast(128), writes=[gres_])
                P.dma("sp", bsrf[0:1, :, :], abs_d[l:l + 1, :, :], writes=[gres_])
                P.op("dve", lambda e: e.tensor_copy(out=bsr[:], in_=bsrf[:]), reads=[gres_], writes=[gres_])
                wsf = [cvP.take(128, F32) for _ in range(4)]; wsfr = [Res("wsf%d" % g) for g in range(4)]
                wsb = [cvP.take(128) for _ in range(4)]; wsbr = [Res("wsb%d" % g) for g in range(4)]
                for g in range(4):
                    P.dma("sp", wsf[g][:, :], aws_d[l, g, :, :], writes=[wsfr[g]])
                    P.op("dve", lambda e, g=g: e.tensor_tensor(out=wsb[g][:, :], in0=wsf[g][:, :], in1=tril[:, :], op=ALU.mult), reads=[wsfr[g], cres], writes=[wsbr[g]])
            wkv, wkvr, _ = wload(win_cols(l, C_K, 512))
            for ni, (c0, n) in enumerate(NTILES):
                if ni + 2 < 5 and not (mode == "halo" and ni + 2 == 4):
                    rmsnorm_h(gpm, hT, hTr, sqb, sqr, rs, rsr, xt, xtr, tiles=(ni + 2,))
                if mode == "halo" and ni == 4:
                    continue
                for ti, (t0, m) in enumerate(TTILES):
                    if not outputs or ntile_of(t0) != ni:
                        continue
                    bk, br = bank()
                    def mm(e, t0=t0, m=m, bk=bk):
                        ins = None
                        for k in range(8):
                            ins = e.matmul(bk[0:m, :], hT[:, k, t0:t0 + m], wkv[:, k, :], start=(k == 0), stop=(k == 7))
                        return ins
                    P.op("pe", mm, reads=[wkvr, hTr[ni]], writes=[br])
                    s = ti % 2
                    P.op("act", lambda e, m=m, bk=bk, s=s: e.copy(out=kvst[s][0:m, :], in_=bk[0:m, :]), reads=[br], writes=[kvstr[s]])
                    P.dma("act", kv_d[l, t0:t0 + m, :], kvst[s][0:m, :], reads=[kvstr[s]], dsem=out_sem)
                for hp in range(2):
                    bk, br = proj_fm(wkv, wkvr, hp * 128, hT, hTr, ni)
                    if ni < 4:
                        P.op("dve", lambda e, bk=bk, hp=hp, c0=c0, n=n: e.tensor_copy(out=KT[:, hp, 2048 + c0:2048 + c0 + n], in_=bk[:, 0:n]),
                             reads=[br], writes=[KTr])
                    else:
                        P.op("dve", lambda e, bk=bk, hp=hp: e.tensor_copy(out=KTs[:, hp, :], in_=bk[:, 0:16]), reads=[br], writes=[KTr])
            mark(4)
            P.op("dve", lambda e: e.memset(hx[:, :], 1.0), writes=[hxr])
            j = 0
            for g, d in enumerate(DILS):
                nb = 16 // d
                for r in range(d):
                    st = 128 * (nb - 1) * d + r
                    bk, br = bank()
                    def mm(e, st=st, d=d, bk=bk):
                        ins = None
                        for k in range(8):
                            ins = e.matmul(bk[:, 0:256], hT[:, k, st:st + 127 * d + 1:d], wkv[:, k, 256:512], start=(k == 0), stop=(k == 7))
                        return ins
                    P.op("pe", mm, reads=[wkvr] + hTr[0:4], writes=[br])
                    P.op("act", lambda e, bk=bk, j=j: e.copy(out=vh[:, j, :, 0:64], in_=bk[:, 0:256].rearrange("p (h d) -> p h d", h=4)),
                         reads=[br], writes=[hxr])
                    j += 1
            wcx, wcxr, _ = wload(win_cols(l, C_CX, 512))
            wcc, wccr, _ = wload(win_cols(l, C_CC, 512))
            for c in range(4):
                for ni, (c0, n) in enumerate(NTILES):
                    if mode == "halo" and ni != 3:
                        continue
                    bka, bra = proj_fm(wcx, wcxr, c * 128, hT, hTr, ni)
                    bkb, brb = proj_fm(wcc, wccr, c * 128, hT, hTr, ni)
                    P.op("act", lambda e, bka=bka, n=n: e.copy(out=tmpb[:, 0:n], in_=bka[:, 0:n]), reads=[bra], writes=[tmpr])
                    if ni < 4:
                        P.op("dve", lambda e, bkb=bkb, c=c, c0=c0, n=n: e.tensor_tensor(out=zp[:, c, 2 + c0:2 + c0 + n], in0=bkb[:, 0:n], in1=tmpb[:, 0:n], op=ALU.mult),
                             reads=[brb, tmpr], writes=[zpr])
                        if ni == 3:
                            P.op("dve", lambda e, bkb=bkb, c=c: e.tensor_tensor(out=zsel[:, c, 0:2], in0=bkb[:, 510:512], in1=tmpb[:, 510:512], op=ALU.mult),
                                 reads=[brb, tmpr], writes=[zsr])
                    else:
                        def zsm(e, bkb=bkb, c=c):
                            e.tensor_tensor(out=zs[:, c, :, 2:6], in0=bkb[:, 0:16].rearrange("p (b t) -> p b t", b=4),
                                            in1=tmpb[:, 0:16].rearrange("p (b t) -> p b t", b=4), op=ALU.mult)
                            return e.tensor_tensor(out=zsel[:, c, 2:10].rearrange("p (b t) -> p b t", b=4),
                                                   in0=bkb[:, 0:16].rearrange("p (b t) -> p b t", b=4)[:, :, 2:4],
                                                   in1=tmpb[:, 0:16].rearrange("p (b t) -> p b t", b=4)[:, :, 2:4], op=ALU.mult)
                        P.op("dve", zsm, reads=[brb, tmpr], writes=[zsr])
            if mode == "full":
                bk, br = bank()
                def ztr(e, bk=bk):
                    ins = None
                    for c in range(4):
                        ins = e.transpose(bk[0:10, c * 128:(c + 1) * 128], zsel[:, c, :], identf[:, :])
                    return ins
                P.op("pe", ztr, reads=[zsr, cres], writes=[br])
                P.op("act", lambda e, bk=bk: e.copy(out=kvst[0][0:10, :], in_=bk[0:10, :]), reads=[br], writes=[kvstr[0]])
                if outputs:
                    P.dma("sp", zo_d[l, :, :], kvst[0][0:10, :], reads=[kvstr[0]], dsem=out_sem)
                P.dma("sp", kvst[1][0:8, :], sc_d[l, :, :], writes=[kvstr[1]])
                bk, br = bank()
                def str_(e, bk=bk):
                    ins = None
                    for c in range(4):
                        ins = e.transpose(bk[:, c * 8:(c + 1) * 8], kvst[1][0:8, c * 128:(c + 1) * 128], identf[0:8, 0:8])
                    return ins
                P.op("pe", str_, reads=[kvstr[1], cres], writes=[br])
                P.op("dve", lambda e, bk=bk: e.tensor_copy(out=zs[:, :, :, 0:2], in_=bk[:, 0:32].rearrange("p (c b j) -> p c b j", c=4, b=4)),
                     reads=[br], writes=[zsr])

            if mode == "full":
                for g in range(4):
                    bk, br = bank()
                    bkb = bk[:, :].bitcast(BF16)
                    P.op("pe", lambda e, bkb=bkb, g=g: e.transpose(bkb[:, 0:128], wsb[g][:, :], identb[:, :]), reads=[wsbr[g], cres], writes=[br])
                    P.op("act", lambda e, bkb=bkb, g=g: e.copy(out=wst[:, g, :], in_=bkb[:, 0:128]), reads=[br], writes=[gres_])
                P.op("dve", lambda e: e.memset(wss[:, :, :], 0.0), writes=[gres_])
                for g in range(4):
                    for b in range(4):
                        P.dma("sp", wss[b * 4:b * 4 + 4, g, b * 4:b * 4 + 4], wst[0:4, g, 0:4], reads=[gres_, cres], writes=[gres_])
                for b in range(4):
                    P.op("dve", lambda e, b=b: e.tensor_copy(out=bss[0:1, :, b * 4:b * 4 + 4], in_=bsr[0:1, :, 0:4]), reads=[gres_, cres], writes=[gres_])
            mark(5)
            if bout_d is not None:
                bres = Res("bounce")
                for hp in range(2):
                    P.dma("sp", bout_d[:, hp * 2048:(hp + 1) * 2048], KT[:, hp, 2048:4096], reads=[KTr], writes=[bres])
                P.op("dve", lambda e: e.tensor_copy(out=hx[:, 5460:5468].rearrange("p (c t) -> p c t", c=4), in_=zp[:, :, 2048:2050]),
                     reads=[zpr], writes=[hxr])
                P.dma("sp", bout_d[:, 4096:XC], hx[:, :], reads=[hxr], writes=[bres])
            P.barrier()
            if mode == "halo":
                return
            for hp in range(2):
                P.dma("sp", KT[:, hp, 0:2048], bin_d[:, hp * 2048:(hp + 1) * 2048], writes=[KTr])
            P.dma("sp", hx[:, :], bin_d[:, 4096:XC], writes=[hxr])
            for i in range(0, 5472, 1368):
                P.op("dve", lambda e, i=i: e.tensor_scalar(hx[:, i:i + 1368], hx[:, i:i + 1368], fcol, None, ALU.mult),
                     reads=[hxr, cres], writes=[hxr])
            P.op("dve", lambda e: e.tensor_copy(out=zp[:, :, 0:2], in_=hx[:, 5460:5468].rearrange("p (c t) -> p c t", c=4)),
                 reads=[hxr], writes=[zpr])
            P.barrier()

            mark(6)
            cvB = Carver(baseU)
            QT = cvB.take(3 * NT).rearrange("p (g t) -> p g t", g=3); QTr = Res("QT")
            numT = cvB.take(2 * NT, F32).rearrange("p (h t) -> p h t", h=2); numr = Res("numT")
            VAs = cvB.take(130).rearrange("p (h d) -> p h d", h=2)
            VT = cvB.take(NT); VTr = Res("VT")
            baseV = cvB.o
            for hp in range(2):
                cvB = Carver(baseV)
                VA = [cvB.take(n * 130).rearrange("p (j h d) -> p j h d", j=n, h=2) for n in (17, 20, 32)]; VAr = Res("VA")
                pexp = [cvB.take(512) for _ in range(2)]; pexpr = [Res("pexp0"), Res("pexp1")]
                ptm = [cvB.take(512) for _ in range(2)]; ptr_ = [Res("pt0"), Res("pt1")]
                wqa, wqar, _ = wload(win_cols(l, C_Q, 512))
                wqb, wqbr, _ = wload(win_cols(l, C_Q + 512, 256))
                for g in range(3):
                    qc = g * 256 + hp * 128
                    ws_, wr_, wc0 = (wqa, wqar, qc) if qc < 512 else (wqb, wqbr, qc - 512)
                    for ni, (c0, n) in enumerate(NTILES):
                        bk, br = proj_fm(ws_, wr_, wc0, hT, hTr, ni)
                        P.op("act", lambda e, bk=bk, g=g, c0=c0, n=n: e.copy(out=QT[:, g, c0:c0 + n], in_=bk[:, 0:n]), reads=[br], writes=[QTr])
                wv, wvr, _ = wload(win_cols(l, C_V + hp * 128, 128))
                if outputs:
                    cvK = Carver(baseV + 11018)
                    kc = cvK.take(36 * 128).rearrange("p (j c) -> p j c", j=36)
                    vc = cvK.take(36 * 128).rearrange("p (j c) -> p j c", j=36)
                    if hp == 0:
                        kcr = Res("kc"); vcr = Res("vc")
                    cache_loads = []
                    for b in range(4):
                        for (cd, dst, rr) in ((ck_d, kc, kcr), (cv_d, vc, vcr)):
                            cs = slice(hp * 128, hp * 128 + 128)
                            cache_loads.append((dst[:, b * 9, :], cd[l, b, 1920:2048, cs], rr))
                            cache_loads.append((dst[:, b * 9 + 1:b * 9 + 5, :], cd[l, b, 1536:2048, cs].rearrange("(p r) c -> p r c", r=4), rr))
                            cache_loads.append((dst[:, b * 9 + 5:b * 9 + 9, :], cd[l, b, :, cs].rearrange("(p s) c -> p s c", s=16)[:, 0:4, :], rr))
                else:
                    cache_loads = []
                for g in range(3):
                    P.op("dve", lambda e, g=g: e.memset(VA[g][:, :, :, 64:65], 1.0), writes=[VAr])
                P.op("dve", lambda e: e.memset(VAs[0:16, :, 64:65], 1.0), writes=[VAr])
                for ni, (c0, n) in enumerate(NTILES):
                    bk, br = proj_fm(wv, wvr, 0, hT, hTr, ni)
                    P.op("act", lambda e, bk=bk, c0=c0, n=n: e.copy(out=VT[:, c0:c0 + n], in_=bk[:, 0:n]), reads=[br], writes=[VTr])
                for g, d in enumerate(DILS):
                    nb = 16 // d
                    blocks = [(r, n) for r in range(d) for n in range(nb)]
                    for b0 in range(0, 16, 4):
                        bk, br = bank()
                        bkb = bk[:, :].bitcast(BF16)
                        def trv(e, b0=b0, bkb=bkb, d=d, blocks=blocks):
                            ins = None
                            for jj in range(4):
                                r, n = blocks[b0 + jj]
                                st = 128 * n * d + r
                                ins = e.transpose(bkb[:, jj * 128:(jj + 1) * 128], VT[:, st:st + 127 * d + 1:d], identb[:, :])
                            return ins
                        P.op("pe", trv, reads=[VTr, cres], writes=[br])
                        r0, n0 = blocks[b0]
                        t_first = r0 * (nb + 1) + n0 + 1
                        step = 1 if nb >= 4 else (nb + 1)
                        P.op("dve", lambda e, bkb=bkb, g=g, t_first=t_first, step=step: e.tensor_copy(
                            out=VA[g][:, t_first:t_first + 3 * step + 1:step, :, 0:64],
                            in_=bkb[:, 0:512].rearrange("p (j h d) -> p j h d", j=4, h=2)), reads=[br], writes=[VAr])
                bk, br = bank()
                bkb = bk[:, :].bitcast(BF16)
                P.op("pe", lambda e, bkb=bkb: e.transpose(bkb[0:16, 0:128], VT[:, 2048:2064], identb[:, :]), reads=[VTr, cres], writes=[br])
                P.op("dve", lambda e, bkb=bkb: e.tensor_copy(out=VAs[0:16, :, 0:64], in_=bkb[0:16, 0:128].rearrange("p (h d) -> p h d", h=2)),
                     reads=[br], writes=[VAr])
                P.op("dve", lambda e, hp=hp: e.tensor_copy(out=VA[0][:, 0, :, :], in_=vh[:, 0, hp * 2:hp * 2 + 2, :]), reads=[hxr], writes=[VAr])
                P.op("dve", lambda e, hp=hp: e.tensor_copy(out=VA[1][:, 0:20:5, :, :], in_=vh[:, 1:5, hp * 2:hp * 2 + 2, :]), reads=[hxr], writes=[VAr])
                P.op("dve", lambda e, hp=hp: e.tensor_copy(out=VA[2][:, 0:32:2, :, :], in_=vh[:, 5:21, hp * 2:hp * 2 + 2, :]), reads=[hxr], writes=[VAr])

                mark(7)
                blks = []
                for g, d in enumerate(DILS):
                    nb = 16 // d
                    for r in range(d):
                        for n in range(nb):
                            st = 128 * n * d + r
                            blks.append(dict(
                                g=g, tl=r * (nb + 1) + n, s=len(blks) % 2,
                                sl_q=slice(st, st + 127 * d + 1, d),
                                sl_k=[slice(2048 + st - 128 * d, 2048 + st - 128 * d + 127 * d + 1, d),
                                      slice(2048 + st, 2048 + st + 127 * d + 1, d)]))
                pexh = [[Res("pexp%d_%d" % (s_, h_)) for h_ in range(2)] for s_ in range(2)]

                def S_stage(B):
                    s = B["s"]
                    for hh in range(2):
                        bk, br = bank()
                        def mmS(e):
                            ins = None
                            for kb in range(2):
                                ins = e.matmul(bk[:, kb * 128:(kb + 1) * 128],
                                               KT[hh * 64:(hh + 1) * 64, hp, B["sl_k"][kb]], QT[hh * 64:(hh + 1) * 64, B["g"], B["sl_q"]],
                                               start=True, stop=True)
                            return ins
                        P.op("pe", mmS, reads=[KTr, QTr], writes=[br])
                        P.op("act", lambda e: e.activation(out=pexp[s][:, hh * 256:(hh + 1) * 256], in_=bk[:, 0:256], func=AF.Exp, scale=0.125),
                             reads=[br], writes=[pexh[s][hh]])

                def M_stage(B):
                    s = B["s"]
                    P.op("dve" if B["s"] == 0 else "pool", lambda e: e.tensor_tensor(
                        out=ptm[s][:, :], in0=pexp[s][:, :],
                        in1=ep[:, B["g"], hp * 2:hp * 2 + 2, :, :].rearrange("p h k q -> p (h k q)"), op=ALU.mult),
                        reads=[pexh[s][0], pexh[s][1], cres], writes=[ptr_[s]])

                def V_stage(B):
                    s = B["s"]
                    bk2, br2 = bank()
                    def mmV(e):
                        ins = None
                        for hh in range(2):
                            for kb in range(2):
                                ins = e.matmul(bk2[0:65, hh * 128:(hh + 1) * 128], VA[B["g"]][:, B["tl"] + kb, hh, :],
                                               ptm[s][:, (hh * 2 + kb) * 128:(hh * 2 + kb + 1) * 128],
                                               start=(kb == 0), stop=(kb == 1))
                        return ins
                    P.op("pe", mmV, reads=[VAr, ptr_[s]], writes=[br2])
                    if B["g"] == 0:
                        P.op("dve", lambda e: e.tensor_copy(
                            out=numT[0:65, :, B["sl_q"]], in_=bk2[0:65, 0:256].rearrange("p (h q) -> p h q", h=2)),
                            reads=[br2], writes=[numr])
                    else:
                        P.op("dve", lambda e: e.tensor_tensor(
                            out=numT[0:65, :, B["sl_q"]], in0=numT[0:65, :, B["sl_q"]],
                            in1=bk2[0:65, 0:256].rearrange("p (h q) -> p h q", h=2), op=ALU.add),
                            reads=[br2], writes=[numr])

                NB_ = len(blks)
                for i in range(NB_ + 2):
                    if i % 2 == 0 and cache_loads:
                        dst_, src_, rr_ = cache_loads.pop(0)
                        P.dma("pool", dst_, src_, writes=[rr_])
                    if i < NB_:
                        S_stage(blks[i])
                    if 0 <= i - 1 < NB_:
                        M_stage(blks[i - 1])
                    if 0 <= i - 2 < NB_:
                        V_stage(blks[i - 2])
                while cache_loads:
                    dst_, src_, rr_ = cache_loads.pop(0)
                    P.dma("pool", dst_, src_, writes=[rr_])
                P.barrier()
                mark(8)
                if not outputs:
                    P.op("dve", lambda e: e.memset(numT[0:65, :, 2048:2064], 1.0), writes=[numr])
                else:
                    cvS = Carver(baseV)
                    ktc = cvS.take(36 * 128).rearrange("p (j c) -> p j c", j=36); ktcr = Res("ktc")
                    vas = cvS.take(36 * 130).rearrange("p (j h d) -> p j h d", j=36, h=2); vasr = Res("vas")
                    psm = cvS.take(288); psmr = Res("psm")
                    psn = cvS.take(96); psnr = Res("psn")
                    P.op("dve", lambda e: e.memset(vas[:, :, :, 64:65], 1.0), writes=[vasr])
                    P.op("dve", lambda e: e.tensor_copy(out=vas[:, :, :, 0:64], in_=vc[:, :, :].rearrange("p j (h d) -> p j h d", h=2)),
                         reads=[vcr], writes=[vasr])
                    for j0 in range(0, 36, 8):
                        nj = min(8, 36 - j0)
                        bk, br = bank()
                        bkb = bk[:, :].bitcast(BF16)
                        def trk(e, bkb=bkb, j0=j0, nj=nj):
                            ins = None
                            for jj in range(nj):
                                ins = e.transpose(bkb[:, jj * 128:(jj + 1) * 128], kc[:, j0 + jj, :], identb[:, :])
                            return ins
                        P.op("pe", trk, reads=[kcr, cres], writes=[br])
                        P.op("act", lambda e, bkb=bkb, j0=j0, nj=nj: e.copy(out=ktc[:, j0:j0 + nj, :], in_=bkb[:, 0:nj * 128].rearrange("p (j c) -> p j c", j=nj)),
                             reads=[br], writes=[ktcr])
                    for bh in range(2):
                      for hh in range(2):
                        bk, br = bank()
                        def mmS2(e, bk=bk, bh=bh, hp=hp, hh=hh):
                            ins = None
                            for bb in range(2):
                                b = bh * 2 + bb
                                for tile in range(9):
                                    g = 0 if tile == 0 else (1 if tile < 5 else 2)
                                    col = (bb * 9 + tile) * 4
                                    ins = e.matmul(bk[:, col:col + 4], ktc[hh * 64:(hh + 1) * 64, b * 9 + tile, :],
                                                   QT[hh * 64:(hh + 1) * 64, g, 2048 + b * 4:2048 + b * 4 + 4], start=True, stop=True)
                            return ins
                        P.op("pe", mmS2, reads=[ktcr, QTr], writes=[br])
                        P.op("act", lambda e, bk=bk, bh=bh, hh=hh: e.activation(
                            out=psm[:, bh * 144:(bh + 1) * 144].rearrange("p (j h t) -> p j h t", j=18, h=2)[:, :, hh, :],
                            in_=bk[:, 0:72].rearrange("p (j t) -> p j t", j=18), func=AF.Exp, scale=0.125),
                             reads=[br], writes=[psmr])
                    for b in range(4):
                        P.op("dve", lambda e, b=b, hp=hp: e.tensor_tensor(
                            out=psm[:, b * 72:(b + 1) * 72].rearrange("p (j h t) -> p j h t", j=9, h=2),
                            in0=psm[:, b * 72:(b + 1) * 72].rearrange("p (j h t) -> p j h t", j=9, h=2),
                            in1=es_s[:, :, hp * 2:hp * 2 + 2, :], op=ALU.mult), reads=[psmr, cres], writes=[psmr])
                    for hh in range(2):
                        bk, br = bank()
                        def mmN(e, bk=bk, hp=hp, hh=hh):
                            ins = None
                            for g in range(3):
                                ins = e.matmul(bk[0:16, g * 16:(g + 1) * 16], KTs[hh * 64:(hh + 1) * 64, hp, :], QT[hh * 64:(hh + 1) * 64, g, 2048:2064],
                                               start=True, stop=True)
                            return ins
                        P.op("pe", mmN, reads=[KTr, QTr], writes=[br])
                        P.op("act", lambda e, bk=bk, hh=hh: e.activation(
                            out=psn[0:16, :].rearrange("p (g h q) -> p g h q", g=3, h=2)[:, :, hh, :],
                            in_=bk[0:16, 0:48].rearrange("p (g q) -> p g q", g=3), func=AF.Exp, scale=0.125), reads=[br], writes=[psnr])
                    P.op("dve", lambda e, hp=hp: e.tensor_tensor(out=psn[0:16, :].rearrange("p (g h q) -> p g h q", g=3, h=2),
                                                                  in0=psn[0:16, :].rearrange("p (g h q) -> p g h q", g=3, h=2),
                                                                  in1=en_s[0:16, :, hp * 2:hp * 2 + 2, :], op=ALU.mult), reads=[psnr, cres], writes=[psnr])
                    bk, br = bank()
                    def mmPV(e, bk=bk):
                        ins = None
                        first = True
                        for hh in range(2):
                            for g in range(3):
                                col = (g * 2 + hh) * 16
                                ins = e.matmul(bk[0:65, hh * 16:(hh + 1) * 16], VAs[0:16, hh, :], psn[0:16, col:col + 16], start=first, stop=False,
                                               skip_group_check=True)
                                first = False
                            for b in range(4):
                                for tile in range(9):
                                    col = ((b * 9 + tile) * 2 + hh) * 4
                                    ins = e.matmul(bk[0:65, hh * 16 + b * 4:hh * 16 + b * 4 + 4], vas[:, b * 9 + tile, hh, :], psm[:, col:col + 4],
                                                   start=False, stop=(tile == 8 and b == 3 and hh == 1), skip_group_check=True)
                        return ins
                    P.op("pe", mmPV, reads=[VAr, vasr, psmr, psnr], writes=[br])
                    P.op("dve", lambda e, bk=bk: e.tensor_copy(out=numT[0:65, :, 2048:2064], in_=bk[0:65, 0:32].rearrange("p (h q) -> p h q", h=2)),
                         reads=[br], writes=[numr])

                mark(9)
                P.op("dve", lambda e: e.reciprocal(out=numT[64:65, :, :], in_=numT[64:65, :, :]), reads=[numr], writes=[numr])
                for hh in range(2):
                    for ni, (c0, n) in enumerate(NTILES):
                        bk, br = bank()
                        P.op("pe", lambda e, bk=bk, hh=hh, c0=c0, n=n: e.matmul(bk[0:64, 0:n], onesf[64:65, 0:64], numT[64:65, hh, c0:c0 + n], start=True, stop=True),
                             reads=[numr, cres], writes=[br])
                        P.op("dve", lambda e, bk=bk, hh=hh, c0=c0, n=n, hp=hp: e.tensor_tensor(
                            out=outb[0:64, hp * 2 + hh, c0:c0 + n], in0=numT[0:64, hh, c0:c0 + n], in1=bk[0:64, 0:n], op=ALU.mult),
                            reads=[br, numr], writes=[outbr])
                P.barrier()

            mark(10)
            cvC = Carver(base3)
            uT = cvC.take(4 * NT).rearrange("p (c t) -> p c t", c=4); uTr = Res("uT")
            outc = cvC.take(4 * NT).rearrange("p (c t) -> p c t", c=4); outcr = Res("outc")
            merged = cvC.take(8 * NT).rearrange("p (c t) -> p c t", c=8); mergedr = [Res("mg%d" % i) for i in range(5)]
            sg = [cvC.take(512) for _ in range(3)]; sgr = [Res("sg%d" % i) for i in range(3)]
            mt = [cvC.take(512, F32) for _ in range(2)]; mtr = [Res("mt0"), Res("mt1")]
            vnb = [cvC.take(512) for _ in range(2)]; vnbr = [Res("vnb0"), Res("vnb1")]
            vtmp = [cvC.take(512, F32) for _ in range(2)]; vtmpr = [Res("vt0"), Res("vt1")]
            stats = cvC.take(16, F32); statr = Res("stats")
            ctmp = cvC.take(512, F32); ctmpr = Res("ctmp")
            wf32 = cvC.take(128, F32); wf32r = Res("wf32")
            tmpb = cvC.take(512); tmpr = Res("tmpb2")

            wua, wuar, _ = wload(win_cols(l, C_UA, 512))
            wva, wvar, _ = wload(win_cols(l, C_VA, 512))
            wcb, wcbr, _ = wload(win_cols(l, C_CB, 512))
            stats2 = [stats[:, 0:16], cvC.take(16, F32)]; statr2 = [Res("stats0"), Res("stats1")]

            def u_group(c, ni):
                c0, n = NTILES[ni]
                bk, br = proj_fm(wua, wuar, c * 128, hT, hTr, ni)
                P.op("act", lambda e: e.activation(out=uT[:, c, c0:c0 + n], in_=bk[:, 0:n], func=AF.Gelu_apprx_tanh),
                     reads=[br], writes=[uTr])

            def v_front(ti):
                t0, m = TTILES[ti]
                s = ti % 2
                st_, str2 = stats2[s], statr2[s]
                bk, br = bank()
                def mm(e):
                    ins = None
                    for k in range(8):
                        ins = e.matmul(bk[0:m, :], hT[:, k, t0:t0 + m], wva[:, k, :], start=(k == 0), stop=(k == 7))
                    return ins
                P.op("pe", mm, reads=[wvar, hTr[ntile_of(t0)]], writes=[br])
                P.op("act", lambda e: e.activation(out=vtmp[s][0:m, :], in_=bk[0:m, :], func=AF.Gelu_apprx_tanh), reads=[br], writes=[vtmpr[s]])
                P.chain("dve", [
                    lambda e: e.bn_stats(out=st_[0:m, 0:6], in_=vtmp[s][0:m, :]),
                    lambda e: e.bn_aggr(out=st_[0:m, 6:8], in_=st_[0:m, 0:6]),
                    lambda e: e.tensor_scalar(st_[0:m, 8:9], st_[0:m, 7:8], EPS, None, ALU.add)],
                    reads=[vtmpr[s]], writes=[str2])
                P.op("act", lambda e: e.activation(out=st_[0:m, 9:10], in_=st_[0:m, 8:9], func=AF.Sqrt), reads=[str2], writes=[str2])
                fns = [
                    lambda e: e.reciprocal(out=st_[0:m, 9:10], in_=st_[0:m, 9:10]),
                    lambda e: e.scalar_tensor_tensor(out=vtmp[s][0:m, :], in0=vtmp[s][0:m, :], scalar=st_[0:m, 6:7], in1=lng[0:m, :],
                                                     op0=ALU.subtract, op1=ALU.mult)]
                if ti == 16:
                    fns.append(lambda e: e.scalar_tensor_tensor(out=vtmp[s][0:m, :], in0=vtmp[s][0:m, :], scalar=st_[0:m, 9:10], in1=lnb[0:m, :],
                                                                op0=ALU.mult, op1=ALU.add))
                    fns.append(lambda e: e.tensor_copy(out=vnb[s][0:m, :], in_=vtmp[s][0:m, :]))
                else:
                    fns.append(lambda e: e.scalar_tensor_tensor(out=vnb[s][0:m, :], in0=vtmp[s][0:m, :], scalar=st_[0:m, 9:10], in1=lnb[0:m, :],
                                                                op0=ALU.mult, op1=ALU.add))
                P.chain("dve", fns, reads=[str2, cres, gres_], writes=[vtmpr[s], vnbr[s], str2])
                if ti == 16 and outputs:
                    P.dma("sp", vn_d[l, :, :], vtmp[s][0:16, :], reads=[vtmpr[s]], dsem=out_sem)

            def v_back(ti):
                t0, m = TTILES[ti]
                s = ti % 2
                bk2, br2 = bank()
                def mmg(e):
                    ins = None
                    for g in range(4):
                        if ti < 16:
                            e.matmul(bk2[:, g * 128:(g + 1) * 128], vnb[s][:, g * 128:(g + 1) * 128], wst[:, g, :], start=True, stop=False)
                            ins = e.matmul(bk2[:, g * 128:(g + 1) * 128], onesb[0:1, 0:128], bsr[0:1, g, :], start=False, stop=True)
                        else:
                            e.matmul(bk2[:, g * 16:(g + 1) * 16], vnb[s][0:16, g * 128:(g + 1) * 128], wss[0:16, g, :], start=True, stop=False)
                            ins = e.matmul(bk2[:, g * 16:(g + 1) * 16], onesb[0:1, 0:128], bss[0:1, g, :], start=False, stop=True)
                    return ins
                P.op("pe", mmg, reads=[vnbr[s], cres, gres_], writes=[br2])
                P.op("dve", lambda e: e.tensor_tensor(
                    out=uT[:, :, t0:t0 + m], in0=uT[:, :, t0:t0 + m], in1=bk2[:, 0:4 * m].rearrange("p (g t) -> p g t", g=4), op=ALU.mult),
                    reads=[br2, uTr], writes=[uTr])

            def cb_group(c, ni):
                c0, n = NTILES[ni]
                bk, br = proj_fm(wcb, wcbr, c * 128, hT, hTr, ni)
                if ni < 4:
                    z0, z1, z2 = zp[:, c, c0:c0 + n], zp[:, c, c0 + 1:c0 + 1 + n], zp[:, c, c0 + 2:c0 + 2 + n]
                    a = ctmp[:, 0:n]; o_ = outc[:, c, c0:c0 + n]; cbp = bk[:, 0:n]
                else:
                    z0, z1, z2 = zs[:, c, :, 0:4], zs[:, c, :, 1:5], zs[:, c, :, 2:6]
                    a = ctmp[:, 0:16].rearrange("p (b t) -> p b t", b=4)
                    o_ = outc[:, c, 2048:2064].rearrange("p (b t) -> p b t", b=4)
                    cbp = bk[:, 0:16].rearrange("p (b t) -> p b t", b=4)
                P.chain("dve", [
                    lambda e: e.tensor_scalar(a, z2, cwc[:, l, 2, c:c + 1], cwc[:, l, 3, c:c + 1], ALU.mult, ALU.add),
                    lambda e: e.scalar_tensor_tensor(out=a, in0=z1, scalar=cwc[:, l, 1, c:c + 1], in1=a, op0=ALU.mult, op1=ALU.add),
                    lambda e: e.scalar_tensor_tensor(out=a, in0=z0, scalar=cwc[:, l, 0, c:c + 1], in1=a, op0=ALU.mult, op1=ALU.add),
                    lambda e: e.tensor_tensor(out=o_, in0=a, in1=cbp, op=ALU.mult)],
                    reads=[br, zpr, zsr, cres], writes=[ctmpr, outcr])

            pending = None
            for ni in range(5):
                for c in range(4):
                    u_group(c, ni)
                tiles = [ti for ti, (t0, m) in enumerate(TTILES) if ntile_of(t0) == ni]
                for idx, ti in enumerate(tiles):
                    v_front(ti)
                    if ni < 4:
                        cb_group(idx, ni)
                    else:
                        for c in range(4):
                            cb_group(c, ni)
                    if pending is not None:
                        v_back(pending)
                    pending = ti
            v_back(pending)
            mark(11)
            mark(12)
            gview = win_d[l, :, C_GA:C_GA + 3072].rearrange("(k p) (b c) -> p k b c", p=128, b=3)
            with nc.allow_non_contiguous_dma(reason="128-column weight slices"):
                for f in range(8):
                    sg_i = ring_slot()
                    wg3 = []
                    for bq, cg in enumerate((C_GA, C_GB, C_GC)):
                        wq_, wg3r, _ = wload(win_cols(l, cg + f * 128, 128), slot=sg_i, off=bq * 1024)
                        wg3.append(wq_)
                    si = sg_i
                    wA, wbr_r, _ = wload(wba_d[l, :, f * 128:(f + 1) * 128].rearrange("(k p) c -> p k c", p=128), slot=si, off=3072)
                    wC, _, _ = wload(wbc_d[l, :, f * 128:(f + 1) * 128].rearrange("(k p) c -> p k c", p=128), slot=si, off=3584)
                    wB, _, _ = wload(wbb_d[l, :, f * 128:(f + 1) * 128].rearrange("(k p) c -> p k c", p=64), slot=si, off=4096)
                    for ni, (c0, n) in enumerate(NTILES):
                        prods = []
                        for bi_ in range(3):
                            bk, br = bank()
                            def mmgate(e, bk=bk, bi_=bi_, c0=c0, n=n, wg3=wg3):
                                ins = None
                                for k in range(8):
                                    ins = e.matmul(bk[:, 0:n], wg3[bi_][:, k, :], hT[:, k, c0:c0 + n], start=(k == 0), stop=(k == 7))
                                return ins
                            P.op("pe", mmgate, reads=[wg3r, hTr[ni]], writes=[br])
                            P.op("act", lambda e, bk=bk, bi_=bi_, n=n, f=f, l=l: e.activation(out=sg[bi_][:, 0:n], in_=bk[:, 0:n], func=AF.Sigmoid,
                                                                                          bias=bgc[:, l, bi_ * 8 + f:bi_ * 8 + f + 1]),
                                 reads=[br, cres], writes=[sgr[bi_]])
                            bk2, br2 = bank()
                            def mmbr(e, bk2=bk2, bi_=bi_, c0=c0, n=n, wA=wA, wB=wB, wC=wC):
                                ins = None
                                for k in range(4):
                                    if bi_ == 0:
                                        ins = e.matmul(bk2[:, 0:n], wA[:, k, :], uT[:, k, c0:c0 + n], start=(k == 0), stop=(k == 3))
                                    elif bi_ == 1:
                                        ins = e.matmul(bk2[:, 0:n], wB[0:64, k, :], outb[0:64, k, c0:c0 + n], start=(k == 0), stop=(k == 3))
                                    else:
                                        ins = e.matmul(bk2[:, 0:n], wC[:, k, :], outc[:, k, c0:c0 + n], start=(k == 0), stop=(k == 3))
                                return ins
                            P.op("pe", mmbr, reads=[wbr_r, uTr, outbr, outcr], writes=[br2])
                            prods.append((bk2, br2))
                        P.chain("dve", [
                            lambda e: e.tensor_tensor(out=mt[0][:, 0:n], in0=sg[0][:, 0:n], in1=prods[0][0][:, 0:n], op=ALU.mult),
                            lambda e: e.tensor_tensor(out=mt[1][:, 0:n], in0=sg[1][:, 0:n], in1=prods[1][0][:, 0:n], op=ALU.mult),
                            lambda e: e.tensor_tensor(out=mt[0][:, 0:n], in0=mt[0][:, 0:n], in1=mt[1][:, 0:n], op=ALU.add),
                            lambda e: e.tensor_tensor(out=mt[1][:, 0:n], in0=sg[2][:, 0:n], in1=prods[2][0][:, 0:n], op=ALU.mult),
                            lambda e: e.tensor_tensor(out=merged[:, f, c0:c0 + n], in0=mt[0][:, 0:n], in1=mt[1][:, 0:n], op=ALU.add)],
                            reads=[sgr[0], sgr[1], sgr[2], prods[0][1], prods[1][1], prods[2][1]], writes=[mtr[0], mtr[1], mergedr[ni]])
            P.barrier()
            mark(13)
            cvD = Carver(0)
            mixf2 = [cvD.take(8 * 512, F32).rearrange("p (k t) -> p k t", k=8) for _ in range(2)]; mixr2 = [[Res("mixf0a"), Res("mixf0b")], [Res("mixf1a"), Res("mixf1b")]]
            sqb2 = [cvD.take(8 * 512).rearrange("p (k t) -> p k t", k=8) for _ in range(2)]; sqr2 = [Res("sqb20"), Res("sqb21")]
            rs2 = [cvD.take(512, F32) for _ in range(2)]; rsr2 = [Res("rs20"), Res("rs21")]
            assert cvD.o <= base3
            cvD2 = Carver(base3)
            xt2 = [cvD2.take(4096, F32).rearrange("p (k t) -> p k t", k=8) for _ in range(2)]; xtr2 = [Res("xt20"), Res("xt21")]
            wos = [wload(wo_d[l, :, half * 512:(half + 1) * 512].rearrange("(k p) c -> p k c", p=128)) for half in range(2)]
            def wo_tile(ni):
                c0, n = NTILES[ni]
                mixf, mixr = mixf2[ni % 2], mixr2[ni % 2]
                for f in range(8):
                    bk, br = bank()
                    def mm(e):
                        ins = None
                        for k in range(8):
                            ins = e.matmul(bk[:, 0:n], wos[f // 4][0][:, k, (f % 4) * 128:(f % 4 + 1) * 128], merged[:, k, c0:c0 + n], start=(k == 0), stop=(k == 7))
                        return ins
                    P.op("pe", mm, reads=[wos[f // 4][1], mergedr[ni]], writes=[br])
                    if f % 2 == 0:
                        P.op("act", lambda e: e.copy(out=mixf[:, f, 0:n], in_=bk[:, 0:n]), reads=[br], writes=[mixr[0]])
                    else:
                        P.op("dve", lambda e: e.tensor_copy(out=mixf[:, f, 0:n], in_=bk[:, 0:n]), reads=[br], writes=[mixr[1]])
            for ni in range(5):
                wo_tile(ni)
                if ni >= 1:
                    post_norm_residual(mixf2[(ni - 1) % 2], mixr2[(ni - 1) % 2], gqm, ni - 1, sqb2, sqr2, rs2, rsr2, xt2, xtr2, nxt=ni)
            post_norm_residual(mixf2[0], mixr2[0], gqm, 4, sqb2, sqr2, rs2, rsr2, xt2, xtr2)
            P.barrier()

            mark(14)
            cvF = Carver(0)
            hF = cvF.take(8 * NT).rearrange("p (k t) -> p k t", k=8); hFr = [Res("hF%d" % i) for i in range(5)]
            sqb = [cvF.take(8 * 512).rearrange("p (k t) -> p k t", k=8) for _ in range(2)]; sqr = [Res("sqbF0"), Res("sqbF1")]
            rs = [cvF.take(512, F32) for _ in range(2)]; rsr = [Res("rsF0"), Res("rsF1")]
            actb = cvF.take(22 * 1040).rearrange("p (j t) -> p j t", j=22); actr = Res("actb")
            mixf = cvF.take(8 * 512, F32).rearrange("p (k t) -> p k t", k=8); mixr = [Res("mixFa"), Res("mixFb")]
            stmp = [cvF.take(512) for _ in range(2)]; stmpr = [Res("st0"), Res("st1")]
            xt = [cvF.take(4096, F32).rearrange("p (k t) -> p k t", k=8) for _ in range(2)]; xtr = [Res("xtF0"), Res("xtF1")]
            rmsnorm_h(gpf, hF, hFr, sqb, sqr, rs, rsr, xt, xtr, tiles=(0, 1))
            for hh_, nis in enumerate(((0, 1), (2, 3, 4))):
                base = NTILES[nis[0]][0]
                for j4 in range(0, 22, 2):
                    nj = 2
                    sgu = ring_slot()
                    wgt, wgtr, _ = wload(wg_d[l, :, j4 * 128:(j4 + nj) * 128].rearrange("(k p) c -> p k c", p=128), slot=sgu, off=0)
                    wup, wupr, _ = wload(wu_d[l, :, j4 * 128:(j4 + nj) * 128].rearrange("(k p) c -> p k c", p=128), slot=sgu, off=2048)
                    for jj in range(nj):
                        j = j4 + jj
                        for ni in nis:
                            c0, n = NTILES[ni]
                            bkg, brg = proj_fm(wgt, wgtr, jj * 128, hF, hFr, ni)
                            bku, bru = proj_fm(wup, wupr, jj * 128, hF, hFr, ni)
                            s = (j + ni) % 2
                            P.op("act", lambda e, bkg=bkg, s=s, n=n: e.activation(out=stmp[s][:, 0:n], in_=bkg[:, 0:n], func=AF.Silu), reads=[brg], writes=[stmpr[s]])
                            P.op("dve", lambda e, bku=bku, s=s, n=n, j=j, c0=c0, base=base: e.tensor_tensor(
                                out=actb[:, j, c0 - base:c0 - base + n], in0=stmp[s][:, 0:n], in1=bku[:, 0:n], op=ALU.mult),
                                reads=[bru, stmpr[s]], writes=[actr])
                    if hh_ == 0 and j4 in (0, 2, 4):
                        rmsnorm_h(gpf, hF, hFr, sqb, sqr, rs, rsr, xt, xtr, tiles=(2 + j4 // 2,))
                for ni in nis:
                    c0, n = NTILES[ni]
                    bks = [bank() for _ in range(8)]
                    for j4 in range(0, 22, 4):
                        nj = min(4, 22 - j4)
                        wdn, wdnr, _ = wload(wd_d[l, j4 * 128:(j4 + nj) * 128, :].rearrange("(j p) c -> p j c", p=128))
                        def mm(e, bks=bks, j4=j4, nj=nj, c0=c0, n=n, base=base, wdn=wdn):
                            ins = None
                            for f in range(8):
                                for jj in range(nj):
                                    j = j4 + jj
                                    ins = e.matmul(bks[f][0][:, 0:n], wdn[:, jj, f * 128:(f + 1) * 128], actb[:, j, c0 - base:c0 - base + n],
                                                   start=(j == 0), stop=(j == 21))
                            return ins
                        P.op("pe", mm, reads=[wdnr, actr], writes=[b[1] for b in bks])
                    for f in range(8):
                        if f % 2 == 0:
                            P.op("act", lambda e, f=f, n=n, bks=bks: e.copy(out=mixf[:, f, 0:n], in_=bks[f][0][:, 0:n]), reads=[bks[f][1]], writes=[mixr[0]])
                        else:
                            P.op("dve", lambda e, f=f, n=n, bks=bks: e.tensor_copy(out=mixf[:, f, 0:n], in_=bks[f][0][:, 0:n]), reads=[bks[f][1]], writes=[mixr[1]])
                    nxt_ = nis[nis.index(ni) + 1] if nis.index(ni) + 1 < len(nis) else None
                    post_norm_residual(mixf, mixr, gqf, ni, sqb, sqr, rs, rsr, xt, xtr, nxt=nxt_)
            P.barrier()

        f1 = flagb[:, 0:1]; f2 = flagb[:, 1:2]
        layer(0, 2, None, None, bounce_d[0], "halo", False)
        layer(0, 1, bounce_d[0], f2, bounce_d[1], "full", False)
        layer(0, 0, bounce_d[1], f1, None, "full", True)
        layer(1, 1, None, None, bounce_d[2], "halo", False)
        layer(1, 0, bounce_d[2], f1, None, "full", True)
        xpark, XTr = xparks[0], XTrs[0]
        cvO = Carver(0)
        ost = [cvO.take(1024, F32) for _ in range(2)]; ostr = [Res("ost0"), Res("ost1")]
        xt = [cvO.take(4096, F32).rearrange("p (k t) -> p k t", k=8) for _ in range(2)]; xtr = [Res("xtO0"), Res("xtO1")]
        for ni, (c0, n) in enumerate(NTILES):
            xs_ = ni % 2
            P.dma("sp", xt[xs_][:, :, 0:n], xpark[:, ni, :, 0:n], reads=[XTr[ni]], writes=[xtr[xs_]])
            for lt in range(0, n, 128):
                m = min(128, n - lt)
                t0 = c0 + lt
                ti = t0 // 128
                s = ti % 2
                for half in range(2):
                    bk, br = bank()
                    def tr(e, bk=bk, half=half, lt=lt, m=m, xs_=xs_):
                        ins = None
                        for jj in range(4):
                            k = half * 4 + jj
                            ins = e.transpose(bk[0:m, jj * 128:(jj + 1) * 128], xt[xs_][:, k, lt:lt + m], identf[:, :])
                        return ins
                    P.op("pe", tr, reads=[xtr[xs_], cres], writes=[br])
                    P.op("act", lambda e, bk=bk, half=half, s=s, m=m: e.copy(out=ost[s][0:m, half * 512:(half + 1) * 512], in_=bk[0:m, :]),
                         reads=[br], writes=[ostr[s]])
                dst = yp_d[t0:t0 + m, :] if ni < 4 else ys_d[:, :]
                P.dma("act", dst, ost[s][0:m, :], reads=[ostr[s]], dsem=out_sem)
        P.barrier()

    def mk(name):
        def body(eng):
            P.reset(name, eng)
            program()
        return body
    with nc.Block() as block:
        block.tensor(mk("pe"))
        block.scalar(mk("act"))
        block.vector(mk("dve"))
        block.gpsimd(mk("pool"))
        block.sync(mk("sp"))
    es.close()
    return nc


_NC_CACHE = {}


def kernel(**inputs):
    f32 = np.float32
    inp = {k: np.ascontiguousarray(np.asarray(v), dtype=f32) for k, v in inputs.items()}
    if "nc" not in _NC_CACHE:
        _NC_CACHE["nc"] = build()
    nc = _NC_CACHE["nc"]
    consts = host_consts()
    shared = {k: inp[k] for k in ("g_pre_mix", "g_post_mix", "g_pre_ffn", "g_post_ffn", "w_in", "a_ln_g", "a_ln_b", "a_ws", "a_bs",
                                  "c_conv_w", "c_conv_b", "w_br_a", "w_br_b", "w_br_c", "b_gate", "w_o", "w_ff_gate", "w_ff_up", "w_ff_down")}
    in_maps = []
    for c in range(NCORES):
        b, q = c // 4, c % 4
        m = dict(shared)
        m.update(consts)
        m["xp"] = np.ascontiguousarray(inp["x_prompt"][b, q * NPR:(q + 1) * NPR, :])
        m["xs"] = np.ascontiguousarray(inp["x_sample"][c * 4:(c + 1) * 4].reshape(NSM, D))
        m["ck"] = np.ascontiguousarray(inp["cache_k_win"][:, c * 4:(c + 1) * 4].reshape(DEPTH, 4, 2048, 256))
        m["cv"] = np.ascontiguousarray(inp["cache_v_win"][:, c * 4:(c + 1) * 4].reshape(DEPTH, 4, 2048, 256))
        m["sc"] = np.ascontiguousarray(inp["state_conv"][:, c * 4:(c + 1) * 4].reshape(DEPTH, 8, 512))
        q1, q2 = max(q - 1, 0), max(q - 2, 0)
        m["xp1"] = np.ascontiguousarray(inp["x_prompt"][b, q1 * NPR:(q1 + 1) * NPR, :])
        m["xp2"] = np.ascontiguousarray(inp["x_prompt"][b, q2 * NPR:(q2 + 1) * NPR, :])
        m["flags"] = np.array([[1.0 if q >= 1 else 0.0, 1.0 if q >= 2 else 0.0]], f32)
        in_maps.append(m)
    res = run_bass_kernel_spmd(nc, in_maps, core_ids=list(range(NCORES)))
    shp = {"yp": (NPR, D), "ys": (NSM, D), "kv": (DEPTH, NT, 512), "zo": (DEPTH, 10, 512), "vns": (DEPTH, NSM, 512)}
    R = [{k: np.asarray(r[k]).reshape(shp[k]) for k in shp} for r in res.results]
    yp = np.stack([np.concatenate([R[b * 4 + q]["yp"] for q in range(4)], axis=0) for b in range(2)]).astype(f32)
    ys = np.concatenate([R[c]["ys"].reshape(4, 4, D) for c in range(NCORES)], axis=0).astype(f32)
    kp = np.stack([np.stack([R[b * 4 + 3]["kv"][l, 0:NPR, 0:256].reshape(NPR, 4, 64) for b in range(2)]) for l in range(DEPTH)]).astype(f32)
    vp = np.stack([np.stack([R[b * 4 + 3]["kv"][l, 0:NPR, 256:512].reshape(NPR, 4, 64) for b in range(2)]) for l in range(DEPTH)]).astype(f32)
    ksm = np.stack([np.concatenate([R[c]["kv"][l, NPR:NT, 0:256].reshape(4, 4, 4, 64) for c in range(NCORES)], axis=0) for l in range(DEPTH)]).astype(f32)
    vsm = np.stack([np.concatenate([R[c]["kv"][l, NPR:NT, 256:512].reshape(4, 4, 4, 64) for c in range(NCORES)], axis=0) for l in range(DEPTH)]).astype(f32)
    cp = np.stack([np.stack([R[b * 4 + 3]["zo"][l, 0:2, :] for b in range(2)]) for l in range(DEPTH)]).astype(f32)
    cs = np.stack([np.concatenate([R[c]["zo"][l, 2:10, :].reshape(4, 2, 512) for c in range(NCORES)], axis=0) for l in range(DEPTH)]).astype(f32)
    av = np.stack([np.concatenate([R[c]["vns"][l].reshape(4, 4, 512) for c in range(NCORES)], axis=0) for l in range(DEPTH)]).astype(f32)
    return (yp, ys, kp, vp, ksm, vsm, cp, cs, av)
```
